# Optimizing a Trainium2 kernel written in Bass

```python
import math
import jax
import jax.numpy as jnp
from jax import lax
import numpy as np

D_MODEL = 1024
BATCH = 2
SEQ = 8192
DEPTH = 4

D_MIX = D_MODEL
W_GROUP = D_MIX // 4
D_FF = 2816
NORM_EPS = 1e-6
GROUP_NORM_EPS = 1e-5
CONV_W = 4

GLA_HEADS = 4
GLA_DK = W_GROUP // 2
GLA_DV = W_GROUP
GLA_HK = GLA_DK // GLA_HEADS
GLA_HV = GLA_DV // GLA_HEADS
GLA_RANK = 16
GLA_GATE_NORM = 16.0
GLA_CHUNK = 64
GLA_COLS = 2 * GLA_DK + 2 * GLA_DV + GLA_RANK

LRU_WIDTH = W_GROUP
LRU_BLOCKS = 4
LRU_BS = LRU_WIDTH // LRU_BLOCKS
LRU_C = 8.0
LRU_COLS = 2 * LRU_WIDTH

RW_WIDTH = W_GROUP
RW_HEADS = 4
RW_HS = RW_WIDTH // RW_HEADS
RW_W_RANK = 16
RW_A_RANK = 16
RW_V_RANK = 8
RW_G_RANK = 32
RW_DECAY_SCALE = math.exp(-0.5)
RW_GN_EPS = 64e-5
RW_COLS = 3 * RW_WIDTH + RW_W_RANK + RW_A_RANK + RW_G_RANK

SSD_DINNER = W_GROUP
SSD_HEADDIM = 64
SSD_HEADS = SSD_DINNER // SSD_HEADDIM
SSD_GROUPS = 2
SSD_DSTATE = 128
SSD_CHUNK = 128
SSD_CONV_DIM = SSD_DINNER + 2 * SSD_GROUPS * SSD_DSTATE
SSD_COLS = SSD_DINNER + SSD_CONV_DIM + SSD_HEADS

N_IN = GLA_COLS + LRU_COLS + RW_COLS + SSD_COLS

kernel_name = "hybrid_parallel_groups_gla_rglru_rwkv7_ssd_macaron"


def split_cols(p, widths):
    offsets = [int(o) for o in np.cumsum(widths)[:-1]]
    return jnp.split(p, offsets, axis=-1)


def rmsnorm(x, w, eps=NORM_EPS):
    xf = x.astype(jnp.float32)
    y = xf * lax.rsqrt(jnp.mean(xf * xf, axis=-1, keepdims=True) + eps)
    return (y * w.astype(jnp.float32)).astype(x.dtype)


def swiglu(x, w_gate, w_up, w_down):
    return (jax.nn.silu(x @ w_gate) * (x @ w_up)) @ w_down


def causal_conv(x, w, b):
    k_w, ch = w.shape
    y = lax.conv_general_dilated(
        x, w[:, None, :].astype(x.dtype), window_strides=(1,), padding=[(k_w - 1, 0)],
        dimension_numbers=("NWC", "WIO", "NWC"), feature_group_count=ch)
    return y + b.astype(x.dtype)


def token_shift(x):
    return jnp.pad(x[:, :-1], ((0, 0), (1, 0), (0, 0)))


def gla_chunked(q, k, v, log_a):
    bsz, seq, heads, dk = q.shape
    dv = v.shape[-1]
    n_chunks = seq // GLA_CHUNK

    def chunks(t):
        return t.astype(jnp.float32).reshape(bsz, n_chunks, GLA_CHUNK, heads, t.shape[-1])

    q, k, v, log_a = chunks(q), chunks(k), chunks(v), chunks(log_a)
    b = jnp.cumsum(log_a, axis=2)
    b_last = b[:, :, -1:]
    q_dec = q * jnp.exp(b)
    k_dec = k * jnp.exp(-b)
    causal = jnp.tril(jnp.ones((GLA_CHUNK, GLA_CHUNK), dtype=bool))
    scores = jnp.where(causal, jnp.einsum("bnihd,bnjhd->bnhij", q_dec, k_dec), 0.0)
    o_intra = jnp.einsum("bnhij,bnjhv->bnihv", scores, v)
    kv_chunk = jnp.einsum("bnjhd,bnjhv->nbhdv", k * jnp.exp(b_last - b), v)
    decay_chunk = jnp.exp(jnp.moveaxis(b_last[:, :, 0], 1, 0))

    def step(state, inp):
        dec, kv = inp
        return dec[..., None] * state + kv, state

    _, s_prev = lax.scan(step, jnp.zeros((bsz, heads, dk, dv), jnp.float32), (decay_chunk, kv_chunk))
    o_inter = jnp.einsum("bnihd,nbhdv->bnihv", q_dec, s_prev)
    return (o_intra + o_inter).reshape(bsz, seq, heads, dv)


def gla_group(p, alpha_up, alpha_bias, norm_w):
    bsz, seq, _ = p.shape
    q, k, v, g, stem = split_cols(p, (GLA_DK, GLA_DK, GLA_DV, GLA_DV, GLA_RANK))
    log_a = jax.nn.log_sigmoid((stem @ alpha_up + alpha_bias).astype(jnp.float32)) / GLA_GATE_NORM
    heads_k = lambda t: t.reshape(bsz, seq, GLA_HEADS, GLA_HK)
    o = gla_chunked(heads_k(q) * (GLA_HK ** -0.5), heads_k(k),
                    v.reshape(bsz, seq, GLA_HEADS, GLA_HV), heads_k(log_a))
    o = rmsnorm(o, norm_w, GROUP_NORM_EPS).reshape(bsz, seq, GLA_DV).astype(p.dtype)
    return o * jax.nn.silu(g)


def _linear_recurrence_combine(left, right):
    a_l, b_l = left
    a_r, b_r = right
    return a_l * a_r, a_r * b_l + b_r


def rglru_group(p, conv_w, conv_b, w_a, b_a, w_x, b_x, lam):
    bsz, seq, _ = p.shape
    xb, gate = split_cols(p, (LRU_WIDTH, LRU_WIDTH))
    xb = causal_conv(xb, conv_w, conv_b)
    xblk = xb.reshape(bsz, seq, LRU_BLOCKS, LRU_BS)
    r = jax.nn.sigmoid(jnp.einsum("bsnk,nkj->bsnj", xblk, w_a).reshape(bsz, seq, LRU_WIDTH) + b_a)
    i = jax.nn.sigmoid(jnp.einsum("bsnk,nkj->bsnj", xblk, w_x).reshape(bsz, seq, LRU_WIDTH) + b_x)
    log_a = -LRU_C * r.astype(jnp.float32) * jax.nn.softplus(-lam.astype(jnp.float32))
    a = jnp.exp(log_a)
    u = jnp.sqrt(-jnp.expm1(2.0 * log_a)) * (i * xb).astype(jnp.float32)
    _, h = lax.associative_scan(_linear_recurrence_combine, (a, u), axis=1)
    return h.astype(p.dtype) * jax.nn.gelu(gate)


def rwkv7_scan(r, w, k, v, a, b):
    bsz, _, heads, n = r.shape
    xs = tuple(jnp.moveaxis(t, 1, 0) for t in (r, w, k, v, a, b))

    def step(state, inp):
        r_t, w_t, k_t, v_t, a_t, b_t = inp
        sa = jnp.einsum("bhvk,bhk->bhv", state, a_t)
        state = (state * w_t[:, :, None, :] + sa[..., None] * b_t[:, :, None, :]
                 + v_t[..., None] * k_t[:, :, None, :])
        return state, jnp.einsum("bhvk,bhk->bhv", state, r_t)

    _, y = lax.scan(step, jnp.zeros((bsz, heads, n, n), jnp.float32), xs)
    return jnp.moveaxis(y, 0, 1)


def rwkv7_group(p, mu, w0, w2, a0, a2, g2, k_k, k_a, r_k, gn_w, gn_b, v_first, v_mix):
    bsz, seq, _ = p.shape
    p = p + (token_shift(p) - p) * mu
    r, k, v, s_w, s_a, s_g = split_cols(p, (RW_WIDTH, RW_WIDTH, RW_WIDTH, RW_W_RANK, RW_A_RANK, RW_G_RANK))
    log_w = -RW_DECAY_SCALE * jax.nn.sigmoid((w0 + jnp.tanh(s_w) @ w2).astype(jnp.float32))
    a = jax.nn.sigmoid(a0 + s_a @ a2)
    g = jax.nn.sigmoid(s_g) @ g2
    if v_mix is not None:
        v0, v1, v2 = v_mix
        v = v + (v_first - v) * jax.nn.sigmoid(v0 + (v @ v1) @ v2)
    heads = lambda t: t.astype(jnp.float32).reshape(bsz, seq, RW_HEADS, RW_HS)
    kk = heads(k * k_k)
    kk = kk / jnp.maximum(jnp.sqrt(jnp.sum(kk * kk, axis=-1, keepdims=True)), 1e-12)
    k = k * (1.0 + (a - 1.0) * k_a)
    rh, kh, vh, ah = heads(r), heads(k), heads(v), heads(a)
    y = rwkv7_scan(rh, heads(jnp.exp(log_w)), kh, vh, -kk, kk * ah)
    mean = jnp.mean(y, axis=-1, keepdims=True)
    var = jnp.mean(jnp.square(y - mean), axis=-1, keepdims=True)
    y = ((y - mean) * lax.rsqrt(var + RW_GN_EPS)).reshape(bsz, seq, RW_WIDTH) * gn_w + gn_b
    bonus = jnp.sum(rh * kh * r_k.astype(jnp.float32), axis=-1, keepdims=True) * vh
    y = (y + bonus.reshape(bsz, seq, RW_WIDTH)).astype(p.dtype) * g
    return y, v


def ssd_chunked(x, d_a, b_in, c_in):
    bsz, seq, heads, hp = x.shape
    groups, n = b_in.shape[2], b_in.shape[3]
    rep = heads // groups
    nc = seq // SSD_CHUNK
    x = x.reshape(bsz, nc, SSD_CHUNK, groups, rep, hp)
    d_a = d_a.reshape(bsz, nc, SSD_CHUNK, groups, rep)
    b_in = b_in.astype(jnp.float32).reshape(bsz, nc, SSD_CHUNK, groups, n)
    c_in = c_in.astype(jnp.float32).reshape(bsz, nc, SSD_CHUNK, groups, n)
    cs = jnp.cumsum(d_a, axis=2)
    seg = cs[:, :, :, None] - cs[:, :, None, :]
    causal = jnp.tril(jnp.ones((SSD_CHUNK, SSD_CHUNK), dtype=bool))[:, :, None, None]
    decay = jnp.exp(jnp.where(causal, seg, -jnp.inf))
    cb = jnp.einsum("bclgn,bcsgn->bclsg", c_in, b_in)
    y_diag = jnp.einsum("bclsgr,bcsgrp->bclgrp", cb[..., None] * decay, x)
    cs_last = cs[:, :, -1]
    x_to_end = x * jnp.exp(cs_last[:, :, None] - cs)[..., None]
    states = jnp.einsum("bcsgn,bcsgrp->cbgrpn", b_in, x_to_end)

    def step(h, inp):
        dec, st = inp
        return dec[..., None, None] * h + st, h

    _, h_prev = lax.scan(step, jnp.zeros((bsz, groups, rep, hp, n), jnp.float32),
                         (jnp.moveaxis(jnp.exp(cs_last), 1, 0), states))
    y_off = jnp.einsum("bclgn,cbgrpn->bclgrp", c_in, h_prev) * jnp.exp(cs)[..., None]
    return (y_diag + y_off).reshape(bsz, seq, heads, hp)


def mamba2_group(p, conv_w, conv_b, dt_bias, a_log, d_skip, norm_w):
    bsz, seq, _ = p.shape
    z, xbc, dt = split_cols(p, (SSD_DINNER, SSD_CONV_DIM, SSD_HEADS))
    xbc = jax.nn.silu(causal_conv(xbc, conv_w, conv_b))
    xs, b_in, c_in = split_cols(xbc, (SSD_DINNER, SSD_GROUPS * SSD_DSTATE, SSD_GROUPS * SSD_DSTATE))
    dt = jax.nn.softplus(dt.astype(jnp.float32) + dt_bias.astype(jnp.float32))
    a = -jnp.exp(a_log.astype(jnp.float32))
    xh = xs.astype(jnp.float32).reshape(bsz, seq, SSD_HEADS, SSD_HEADDIM)
    grp = lambda t: t.reshape(bsz, seq, SSD_GROUPS, SSD_DSTATE)
    y = ssd_chunked(xh * dt[..., None], dt * a, grp(b_in), grp(c_in))
    y = y + d_skip.astype(jnp.float32)[:, None] * xh
    y = y.reshape(bsz, seq, SSD_DINNER) * jax.nn.silu(z.astype(jnp.float32))
    gsz = SSD_DINNER // SSD_GROUPS
    y = rmsnorm(y.reshape(bsz, seq, SSD_GROUPS, gsz), norm_w.reshape(SSD_GROUPS, gsz), GROUP_NORM_EPS)
    return y.reshape(bsz, seq, SSD_DINNER).astype(p.dtype)


def setup_inputs(seed: int = 0) -> dict:
    key = jax.random.key(seed)
    keys = jax.random.split(key, 64)
    counter = [0]

    def nk():
        k = keys[counter[0]]
        counter[0] += 1
        return k

    def nrm(shape, scale):
        return jax.random.normal(nk(), shape, jnp.float32) * scale

    def gain(shape):
        return 1.0 + nrm(shape, 0.02)

    def unif(shape, lo, hi):
        return jax.random.uniform(nk(), shape, jnp.float32, lo, hi)

    L = DEPTH
    lru_s = unif((L, LRU_WIDTH), 0.9, 0.999) ** (1.0 / LRU_C)
    dt0 = jnp.exp(unif((L, SSD_HEADS), math.log(1e-3), math.log(1e-1)))
    return {
        "x": nrm((BATCH, SEQ, D_MODEL), 1.0),
        "ffn1_norm": gain((L, D_MODEL)),
        "ffn1_w_gate": nrm((L, D_MODEL, D_FF), D_MODEL ** -0.5),
        "ffn1_w_up": nrm((L, D_MODEL, D_FF), D_MODEL ** -0.5),
        "ffn1_w_down": nrm((L, D_FF, D_MODEL), D_FF ** -0.5),
        "mix_norm": gain((L, D_MODEL)),
        "w_in": nrm((L, D_MODEL, N_IN), D_MODEL ** -0.5),
        "w_out": nrm((L, D_MIX, D_MODEL), D_MIX ** -0.5),
        "gla_alpha_up": nrm((L, GLA_RANK, GLA_DK), GLA_RANK ** -0.5),
        "gla_alpha_bias": nrm((L, GLA_DK), 0.1),
        "gla_norm": gain((L, GLA_HV)),
        "lru_conv_w": nrm((L, CONV_W, LRU_WIDTH), CONV_W ** -0.5),
        "lru_conv_b": nrm((L, LRU_WIDTH), 0.02),
        "lru_w_a": nrm((L, LRU_BLOCKS, LRU_BS, LRU_BS), LRU_BS ** -0.5),
        "lru_b_a": nrm((L, LRU_WIDTH), 0.02),
        "lru_w_x": nrm((L, LRU_BLOCKS, LRU_BS, LRU_BS), LRU_BS ** -0.5),
        "lru_b_x": nrm((L, LRU_WIDTH), 0.02),
        "lru_lambda": jnp.log(lru_s) - jnp.log1p(-lru_s),
        "rw_mu": unif((L, RW_COLS), 0.0, 1.0),
        "rw_w0": nrm((L, RW_WIDTH), 1.0),
        "rw_w2": nrm((L, RW_W_RANK, RW_WIDTH), RW_W_RANK ** -0.5),
        "rw_a0": nrm((L, RW_WIDTH), 0.1),
        "rw_a2": nrm((L, RW_A_RANK, RW_WIDTH), RW_A_RANK ** -0.5),
        "rw_g2": nrm((L, RW_G_RANK, RW_WIDTH), RW_G_RANK ** -0.5),
        "rw_v0": nrm((L - 1, RW_WIDTH), 0.1),
        "rw_v1": nrm((L - 1, RW_WIDTH, RW_V_RANK), RW_WIDTH ** -0.5),
        "rw_v2": nrm((L - 1, RW_V_RANK, RW_WIDTH), RW_V_RANK ** -0.5),
        "rw_k_k": 0.85 + nrm((L, RW_WIDTH), 0.02),
        "rw_k_a": gain((L, RW_WIDTH)),
        "rw_r_k": nrm((L, RW_HEADS, RW_HS), 0.1),
        "rw_gn_w": gain((L, RW_WIDTH)),
        "rw_gn_b": nrm((L, RW_WIDTH), 0.02),
        "ssd_conv_w": nrm((L, CONV_W, SSD_CONV_DIM), CONV_W ** -0.5),
        "ssd_conv_b": nrm((L, SSD_CONV_DIM), 0.02),
        "ssd_dt_bias": dt0 + jnp.log(-jnp.expm1(-dt0)),
        "ssd_a_log": jnp.log(unif((L, SSD_HEADS), 1.0, 16.0)),
        "ssd_d": gain((L, SSD_HEADS)),
        "ssd_norm": gain((L, SSD_DINNER)),
        "ffn2_norm": gain((L, D_MODEL)),
        "ffn2_w_gate": nrm((L, D_MODEL, D_FF), D_MODEL ** -0.5),
        "ffn2_w_up": nrm((L, D_MODEL, D_FF), D_MODEL ** -0.5),
        "ffn2_w_down": nrm((L, D_FF, D_MODEL), D_FF ** -0.5),
        "final_norm": gain((D_MODEL,)),
    }


def reference(x, ffn1_norm, ffn1_w_gate, ffn1_w_up, ffn1_w_down, mix_norm, w_in, w_out,
              gla_alpha_up, gla_alpha_bias, gla_norm,
              lru_conv_w, lru_conv_b, lru_w_a, lru_b_a, lru_w_x, lru_b_x, lru_lambda,
              rw_mu, rw_w0, rw_w2, rw_a0, rw_a2, rw_g2, rw_v0, rw_v1, rw_v2,
              rw_k_k, rw_k_a, rw_r_k, rw_gn_w, rw_gn_b,
              ssd_conv_w, ssd_conv_b, ssd_dt_bias, ssd_a_log, ssd_d, ssd_norm,
              ffn2_norm, ffn2_w_gate, ffn2_w_up, ffn2_w_down, final_norm):
    v_first = None
    for l in range(DEPTH):
        x = x + 0.5 * swiglu(rmsnorm(x, ffn1_norm[l]), ffn1_w_gate[l], ffn1_w_up[l], ffn1_w_down[l])
        proj = rmsnorm(x, mix_norm[l]) @ w_in[l]
        p_gla, p_lru, p_rw, p_ssd = split_cols(proj, (GLA_COLS, LRU_COLS, RW_COLS, SSD_COLS))
        y_gla = gla_group(p_gla, gla_alpha_up[l], gla_alpha_bias[l], gla_norm[l])
        y_lru = rglru_group(p_lru, lru_conv_w[l], lru_conv_b[l], lru_w_a[l], lru_b_a[l],
                            lru_w_x[l], lru_b_x[l], lru_lambda[l])
        v_mix = None if l == 0 else (rw_v0[l - 1], rw_v1[l - 1], rw_v2[l - 1])
        y_rw, v_rw = rwkv7_group(p_rw, rw_mu[l], rw_w0[l], rw_w2[l], rw_a0[l], rw_a2[l], rw_g2[l],
                                 rw_k_k[l], rw_k_a[l], rw_r_k[l], rw_gn_w[l], rw_gn_b[l], v_first, v_mix)
        if l == 0:
            v_first = v_rw
        y_ssd = mamba2_group(p_ssd, ssd_conv_w[l], ssd_conv_b[l], ssd_dt_bias[l], ssd_a_log[l],
                             ssd_d[l], ssd_norm[l])
        y = jnp.concatenate([y_gla, y_lru, y_rw, y_ssd], axis=-1)
        x = x + y @ w_out[l]
        x = x + 0.5 * swiglu(rmsnorm(x, ffn2_norm[l]), ffn2_w_gate[l], ffn2_w_up[l], ffn2_w_down[l])
    return rmsnorm(x, final_norm)
```

```python
import os
import numpy as np
from contextlib import ExitStack
import concourse.bass as bass
import concourse.mybir as mybir
from concourse.bass_utils import run_bass_kernel_spmd

F32 = mybir.dt.float32
BF16 = mybir.dt.bfloat16
AF = mybir.ActivationFunctionType
ALU = mybir.AluOpType
AX = mybir.AxisListType

D = 1024
KC = 8
DFF = 2816
NJ = 22
NIN = 3156
HW = 4
L_FULL = 4
NORM_EPS = 1e-6


class Prog:
    def __init__(self, nc, es):
        self.nc = nc
        self.es = es
        self.streams = {
            "pe": (nc.tensor, 1, 20000),
            "dve": (nc.vector, 1, 20000),
            "act": (nc.scalar, 1, 20000),
            "pool": (nc.gpsimd, 1, 20000),
            "sp": (nc.sync, 16, 1500),
            "poolq": (nc.gpsimd, 16, 1500),
            "cc": (nc.gpsimd, 1, 20000),
        }
        self.issuer = {"pe": "pe", "dve": "dve", "act": "act", "pool": "pool", "sp": "sp",
                       "poolq": "pool", "cc": "pool"}
        self.sems = {s: [] for s in self.streams}
        self.cnt = {s: 0 for s in self.streams}
        self.waited = {}
        self.lastw = {}
        self.readers = {}
        self.same_engine_sync = True
        self.nwaits = 0

    def _sem(self, stream, epoch):
        lst = self.sems[stream]
        while len(lst) <= epoch:
            lst.append(self.es.enter_context(self.nc.semaphore(f"s_{stream}_{len(lst)}")))
        return lst[epoch]

    def _wait(self, issuer, stream, seq):
        key = (issuer, stream)
        if self.waited.get(key, 0) >= seq:
            return
        self.waited[key] = seq
        eng, inc, cap = self.streams[stream]
        epoch = (seq - 1) // cap
        val = ((seq - 1) % cap + 1) * inc
        self.streams[issuer][0].wait_ge(self._sem(stream, epoch), val)
        self.nwaits += 1

    def op(self, stream, fn, reads=(), writes=(), lane=None, mode="full"):
        if stream == "pe":
            if getattr(self, "pe_mode", "full") != mode and self.cnt["pe"] > 0:
                self._wait("pe", "pe", self.cnt["pe"])
            self.pe_mode = mode
        if lane is not None:
            base = stream
            stream = f"{base}.{lane}"
            if stream not in self.streams:
                self.streams[stream] = self.streams[base]
                self.issuer[stream] = self.issuer[base]
                self.sems[stream] = []
                self.cnt[stream] = 0
        issuer = self.issuer[stream]
        deps = set()
        for k in reads:
            if k in self.lastw:
                deps.add(self.lastw[k])
        for k in writes:
            if k in self.lastw:
                deps.add(self.lastw[k])
            for r in self.readers.get(k, ()):
                deps.add(r)
        for (s, q) in sorted(deps):
            if s == stream and (stream == "pe" or not self.same_engine_sync):
                continue
            if s == stream and stream in ("sp", "poolq"):
                pass
            self._wait(issuer, s, q)
        eng, inc, cap = self.streams[stream]
        ins = fn(eng)
        self.cnt[stream] += 1
        seq = self.cnt[stream]
        ins.then_inc(self._sem(stream, (seq - 1) // cap), inc)
        me = (stream, seq)
        for k in reads:
            self.readers.setdefault(k, []).append(me)
        for k in writes:
            self.lastw[k] = me
            self.readers[k] = []
        return me

    def barrier(self):
        for issuer in ("pe", "dve", "act", "pool", "sp"):
            for s in list(self.streams):
                if self.cnt[s] > 0:
                    self._wait(issuer, s, self.cnt[s])

    def final_wait(self, issuer="sp"):
        for s in list(self.streams):
            if self.cnt[s] > 0:
                self._wait(issuer, s, self.cnt[s])


def chan_pp(v, ntile):
    return np.ascontiguousarray(np.asarray(v, np.float32).reshape(ntile, 128).T)


def pack_small(inp, L, rank):
    cols = []
    offs = {}
    pos = [0]
    lcols = [[] for _ in range(L)]
    lpos = [0] * L

    def add(name, a, layer=None):
        a = np.asarray(a, np.float32)
        assert a.ndim == 2 and a.shape[0] <= 128, (name, a.shape)
        buf = np.zeros((128, a.shape[1]), np.float32)
        buf[: a.shape[0]] = a
        if layer is None:
            offs[name] = ("C", pos[0], a.shape[1])
            pos[0] += a.shape[1]
            cols.append(buf)
        else:
            offs[name] = ("L", lpos[layer], a.shape[1])
            lpos[layer] += a.shape[1]
            lcols[layer].append(buf)

    ident = np.eye(128, dtype=np.float32)
    add("ident", ident)
    add("ones", np.ones((128, 128), np.float32))
    selprev = np.zeros((128, 4), np.float32)
    if rank > 0:
        selprev[:, rank - 1] = 1.0
    add("selprev", selprev)
    for l in range(L):
        add(f"n1_{l}", chan_pp(inp["ffn1_norm"][l], 8))
        add(f"nm_{l}", chan_pp(inp["mix_norm"][l], 8))
        add(f"n2_{l}", chan_pp(inp["ffn2_norm"][l], 8))
    add("nf", chan_pp(inp["final_norm"], 8))
    jj = np.arange(128)
    causal = (jj[:, None] <= jj[None, :]).astype(np.float32)
    add("mask4", np.tile(causal, (1, 4)))
    hm = np.zeros((128, 4), np.float32)
    for h in range(4):
        hm[h * 32:(h + 1) * 32, h] = 1.0
    add("hm", hm)
    add("bm", np.repeat(hm, 64, axis=1))
    i64 = np.arange(128) % 64
    j64 = np.arange(64)
    m_lt = (i64[:, None] < j64[None, :]).astype(np.float32)
    m_gt = (j64[None, :] < i64[:, None]).astype(np.float32)
    m_le = (i64[:, None] <= j64[None, :]).astype(np.float32)
    add("maskA", np.concatenate([m_lt, m_gt, m_lt, m_le, m_le], axis=1))
    add("ident2", (i64[:, None] == j64[None, :]).astype(np.float32))
    p128 = np.arange(128)
    add("bones", (p128[:, None] // 64 == p128[None, :] // 64).astype(np.float32))
    for l in range(L):
        mu = np.asarray(inp["rw_mu"][l], np.float32)
        add(f"rw_mur_{l}", chan_pp(mu[0:256], 2), layer=l)
        add(f"rw_muk_{l}", chan_pp(mu[256:512], 2), layer=l)
        add(f"rw_muv_{l}", chan_pp(mu[512:768], 2), layer=l)
        add(f"rw_musw_{l}", mu[768:784][:, None], layer=l)
        add(f"rw_musa_{l}", mu[784:800][:, None], layer=l)
        add(f"rw_musg_{l}", mu[800:832][:, None], layer=l)
        for nm in ("w0", "a0", "k_k", "k_a", "gn_w", "gn_b"):
            add(f"rw_{nm}_{l}", chan_pp(inp[f"rw_{nm}"][l], 2), layer=l)
        add(f"rw_r_k_{l}", chan_pp(np.asarray(inp["rw_r_k"][l], np.float32).reshape(256), 2), layer=l)
        add(f"rw_w2_{l}", np.asarray(inp["rw_w2"][l], np.float32), layer=l)
        add(f"rw_a2_{l}", np.asarray(inp["rw_a2"][l], np.float32), layer=l)
        add(f"rw_g2_{l}", np.asarray(inp["rw_g2"][l], np.float32), layer=l)
        if l > 0:
            add(f"rw_v0_{l}", chan_pp(inp["rw_v0"][l - 1], 2), layer=l)
            v1 = np.asarray(inp["rw_v1"][l - 1], np.float32)
            add(f"rw_v1_{l}", np.concatenate([v1[0:128], v1[128:256]], axis=1), layer=l)
            add(f"rw_v2_{l}", np.asarray(inp["rw_v2"][l - 1], np.float32), layer=l)
        else:
            add(f"rw_v0_{l}", np.zeros((128, 2), np.float32), layer=l)
            add(f"rw_v1_{l}", np.zeros((128, 16), np.float32), layer=l)
            add(f"rw_v2_{l}", np.zeros((8, 256), np.float32), layer=l)
    add("utri", causal)
    add("negmask", (causal - 1.0) * 30000.0)
    for l in range(L):
        cw = np.asarray(inp["ssd_conv_w"][l], np.float32)
        add(f"ssd_cw_{l}", np.concatenate([cw[:, ct * 128:(ct + 1) * 128].T for ct in range(6)], axis=1), layer=l)
        add(f"ssd_cb_{l}", chan_pp(inp["ssd_conv_b"][l], 6), layer=l)
        add(f"ssd_dtb_{l}", np.tile(np.asarray(inp["ssd_dt_bias"][l], np.float32)[None, :], (128, 1)), layer=l)
        add(f"ssd_alog_{l}", np.tile(np.asarray(inp["ssd_a_log"][l], np.float32)[None, :], (128, 1)), layer=l)
        add(f"ssd_d_{l}", np.tile(np.asarray(inp["ssd_d"][l], np.float32)[None, :], (128, 1)), layer=l)
        add(f"ssd_nw_{l}", chan_pp(inp["ssd_norm"][l], 2), layer=l)
    for l in range(L):
        add(f"gla_aup_{l}", np.asarray(inp["gla_alpha_up"][l], np.float32), layer=l)
        add(f"gla_ab_{l}", chan_pp(inp["gla_alpha_bias"][l], 1), layer=l)
        add(f"gla_nw_{l}", np.tile(np.asarray(inp["gla_norm"][l], np.float32)[None, :], (128, 1)), layer=l)
    for l in range(L):
        cw = np.asarray(inp["lru_conv_w"][l], np.float32)
        add(f"lru_cw_{l}", np.concatenate([cw[:, ct * 128:(ct + 1) * 128].T for ct in range(2)], axis=1), layer=l)
        add(f"lru_cb_{l}", chan_pp(inp["lru_conv_b"][l], 2), layer=l)
        add(f"lru_ba_{l}", chan_pp(inp["lru_b_a"][l], 2), layer=l)
        add(f"lru_bx_{l}", chan_pp(inp["lru_b_x"][l], 2), layer=l)
        add(f"lru_lam_{l}", chan_pp(inp["lru_lambda"][l], 2), layer=l)
        for nm, key in (("wa", "lru_w_a"), ("wx", "lru_w_x")):
            w = np.asarray(inp[key][l], np.float32)
            for ct in range(2):
                bd = np.zeros((128, 128), np.float32)
                for nn in range(2):
                    bd[nn * 64:(nn + 1) * 64, nn * 64:(nn + 1) * 64] = w[2 * ct + nn]
                add(f"lru_{nm}_{l}_{ct}", bd, layer=l)
    assert len(set(lpos)) == 1, lpos
    ppl = np.stack([np.concatenate(c, axis=1) for c in lcols], axis=0)
    return np.concatenate(cols, axis=1), ppl, offs


def build(T, L, offs, npp, nppl, mixers=()):
    nc = bass.Bass("TRN2", target_bir_lowering=False)
    NT = T // 512
    FG = min(T, 1024)
    xT_d = nc.dram_tensor("xT", [128, KC, T], F32, kind="ExternalInput").ap()
    out_d = nc.dram_tensor("outT", [128, KC, T], F32, kind="ExternalOutput").ap()
    pp_d = nc.dram_tensor("pp", [128, npp], F32, kind="ExternalInput").ap()
    ppl_d = nc.dram_tensor("ppl", [L, 128, nppl], F32, kind="ExternalInput").ap()
    wgu_d = nc.dram_tensor("wgu", [L * 2 * NJ, 128, 2 * KC * 128], F32, kind="ExternalInput").ap()
    wd_d = nc.dram_tensor("wd", [L * 2 * KC, 128, NJ * 128], F32, kind="ExternalInput").ap()
    win_d = nc.dram_tensor("win", [L, 128, KC, NIN], F32, kind="ExternalInput").ap()
    wout_d = nc.dram_tensor("wout", [L, 128, KC, D], F32, kind="ExternalInput").ap()
    uid = [0]

    with ExitStack() as es:
        P = Prog(nc, es)

        def sb(name, shape, dt):
            return es.enter_context(nc.sbuf_tensor("s_" + name, shape, dt))

        def ps(name, shape, dt=F32):
            return es.enter_context(nc.psum_tensor(name, shape, dt))

        xd = nc.dram_tensor("xd_scratch", [128, KC, T], F32, kind="Internal").ap()
        vfirst_d = nc.dram_tensor("vfirst_scratch", [128, 2, T], F32, kind="Internal").ap()
        xn2 = sb("xn2", [128, KC, HW + T], BF16)
        x = None
        pp = sb("pp", [128, npp], F32)
        ppl = sb("ppl", [128, nppl], F32)
        onesb = sb("onesb", [128, 128], BF16)

        def prm(name):
            kind, o, w = offs[name]
            return (pp if kind == "C" else ppl)[:, o:o + w]

        P.op("sp", lambda e: e.dma_start(out=pp[:], in_=pp_d[:, :]), writes=["pp"])
        P.barrier()

        def load_x(src, first):
            for kc in range(KC):
                P.op("sp", lambda e, kc=kc: e.dma_start(out=x[:, kc, :], in_=src[:, kc, :]),
                     reads=([] if first else [f"xd{kc}_{t}" for t in range(NT)]), writes=[f"x{kc}_{t}" for t in range(NT)])
            P.barrier()

        def store_x():
            for kc in range(KC):
                P.op("sp", lambda e, kc=kc: e.dma_start(out=xd[:, kc, :], in_=x[:, kc, :]),
                     reads=[f"x{kc}_{t}" for t in range(NT)], writes=[f"xd{kc}_{t}" for t in range(NT)])
            P.barrier()
        P.op("dve", lambda e: e.tensor_copy(out=onesb[:], in_=prm("ones")), reads=["pp"], writes=["onesb"])

        def rmsnorm(es2, pool, wname, tok_tiles, dst_fn, dst_key_fn):
            sq, rs, ssp = pool
            for i, tt in enumerate(tok_tiles):
                c0 = tt * 512
                for kc in range(KC):
                    b = (i * KC + kc) % 2
                    P.op("act", lambda e, kc=kc, b=b: e.activation(out=sq[:, b, :], in_=x[:, kc, c0:c0 + 512],
                                                                    func=AF.Square),
                         reads=[f"x{kc}_{tt}"], writes=[f"sq{b}"])
                    P.op("pe", lambda e, kc=kc, b=b: e.matmul(ssp[:, :], lhsT=onesb[:, :], rhs=sq[:, b, :],
                                                               start=(kc == 0), stop=(kc == KC - 1)),
                         reads=[f"sq{b}", "onesb"], writes=["ssp"])
                P.op("act", lambda e: e.activation(out=rs[:, 0, :], in_=ssp[:, :], func=AF.Sqrt,
                                                   scale=1.0 / D, bias=NORM_EPS),
                     reads=["ssp"], writes=["rs0"])
                P.op("dve", lambda e: e.reciprocal(out=rs[:, 1, :], in_=rs[:, 0, :]), reads=["rs0"], writes=["rs1"])
                for kc in range(KC):
                    P.op("dve", lambda e, kc=kc: e.scalar_tensor_tensor(
                        out=dst_fn(kc, tt), in0=x[:, kc, c0:c0 + 512], scalar=prm(wname)[:, kc:kc + 1],
                        in1=rs[:, 1, :], op0=ALU.mult, op1=ALU.mult),
                        reads=[f"x{kc}_{tt}", "rs1", "pp"], writes=[dst_key_fn(kc, tt)])

        def ffn(l, f, wname):
            with ExitStack() as es2:
                def sb2(name, shape, dt):
                    return es2.enter_context(nc.sbuf_tensor(f"{name}_{l}_{f}", shape, dt))

                def ps2(name, shape, dt=F32):
                    return es2.enter_context(nc.psum_tensor(f"{name}_{l}_{f}", shape, dt))
                xn = sb2("f_xn", [128, KC, FG], BF16)
                h = sb2("f_h", [128, NJ, FG], BF16)
                wgu = sb2("f_wgu", [128, 2, 2 * KC * 128], BF16)
                wd = sb2("f_wd", [128, 2, NJ * 128], BF16)
                sq = sb2("f_sq", [128, 2, 512], BF16)
                rs = sb2("f_rs", [128, 2, 512], F32)
                sg = sb2("f_sg", [128, 2, 512], F32)
                ssp = ps2("f_ssp", [128, 512])
                pg = ps2("f_pg", [128, 2, 512])
                pu = ps2("f_pu", [128, 2, 512])
                pd = ps2("f_pd", [128, 2, 512])
                nsub = FG // 512
                for g in range(T // FG):
                    tiles = [g * nsub + s for s in range(nsub)]
                    rmsnorm(es2, (sq, rs, ssp), wname, tiles,
                            lambda kc, tt: xn[:, kc, (tt - g * nsub) * 512:(tt - g * nsub + 1) * 512],
                            lambda kc, tt: f"xn{kc}_{tt - g * nsub}")
                    cnt = 0
                    for j in range(NJ):
                        wb = j % 2
                        row = (l * 2 + f) * NJ + j
                        P.op("poolq", lambda e, wb=wb, row=row: e.dma_start(out=wgu[:, wb, :], in_=wgu_d[row, :, :]),
                             writes=[f"wgu{wb}"], lane=f"wgu{wb}")
                        for s in range(nsub):
                            pb = cnt % 2
                            cnt += 1
                            for gi, pt in ((0, pg), (1, pu)):
                                for kc in range(KC):
                                    o = (gi * KC + kc) * 128
                                    P.op("pe", lambda e, pt=pt, o=o, kc=kc, s=s, pb=pb, wb=wb: e.matmul(
                                        pt[:, pb, :], lhsT=wgu[:, wb, o:o + 128], rhs=xn[:, kc, s * 512:(s + 1) * 512],
                                        start=(kc == 0), stop=(kc == KC - 1)),
                                        reads=[f"wgu{wb}", f"xn{kc}_{s}"], writes=[f"p{gi}_{pb}"])
                            P.op("act", lambda e, pb=pb: e.activation(out=sg[:, pb, :], in_=pg[:, pb, :], func=AF.Silu),
                                 reads=[f"p0_{pb}"], writes=[f"sg{pb}"])
                            P.op("dve", lambda e, pb=pb, j=j, s=s: e.tensor_tensor(
                                out=h[:, j, s * 512:(s + 1) * 512], in0=pu[:, pb, :], in1=sg[:, pb, :], op=ALU.mult),
                                reads=[f"p1_{pb}", f"sg{pb}"], writes=[f"h{j}_{s}"])
                    cnt = 0
                    for m in range(KC):
                        wb = m % 2
                        row = (l * 2 + f) * KC + m
                        P.op("poolq", lambda e, wb=wb, row=row: e.dma_start(out=wd[:, wb, :], in_=wd_d[row, :, :]),
                             writes=[f"wd{wb}"], lane=f"wd{wb}")
                        for s in range(nsub):
                            pb = cnt % 2
                            cnt += 1
                            tt = g * nsub + s
                            for j in range(NJ):
                                P.op("pe", lambda e, j=j, s=s, pb=pb, wb=wb: e.matmul(
                                    pd[:, pb, :], lhsT=wd[:, wb, j * 128:(j + 1) * 128], rhs=h[:, j, s * 512:(s + 1) * 512],
                                    start=(j == 0), stop=(j == NJ - 1)),
                                    reads=[f"wd{wb}", f"h{j}_{s}"], writes=[f"pd{pb}"])
                            P.op("dve", lambda e, m=m, pb=pb, tt=tt: e.scalar_tensor_tensor(
                                out=x[:, m, tt * 512:(tt + 1) * 512], in0=pd[:, pb, :], scalar=0.5,
                                in1=x[:, m, tt * 512:(tt + 1) * 512], op0=ALU.mult, op1=ALU.add),
                                reads=[f"pd{pb}", f"x{m}_{tt}"], writes=[f"x{m}_{tt}"])
                P.barrier()


        def exchange(sbm, tag, src_ap, W):
            uid[0] += 1
            u = uid[0]
            bounce = nc.dram_tensor(f"bnc_{u}", [128, W], F32, kind="Internal").ap()
            gath = nc.dram_tensor(f"gth_{u}", [512, W], F32, kind="Internal").ap()
            cache = sbm.__dict__.setdefault("xcache", {})
            if W not in cache:
                cache[W] = (sbm(f"hg_{u}", [128, 4, W], F32), sbm(f"hr_{u}", [128, W], F32), u)
            hg, res, u0 = cache[W]
            P.op("poolq", lambda e: e.dma_start(out=bounce[:, :], in_=src_ap, allow_slow_non_contiguous=True), reads=[tag], writes=[f"bnc{u}"])
            P.op("cc", lambda e: e.collective_compute("AllGather", ALU.bypass, replica_groups=[[0, 1, 2, 3], [4, 5, 6, 7]],
                                                      ins=[bounce[:, :]], outs=[gath[:, :]]),
                 reads=[f"bnc{u}"], writes=[f"gth{u}"])
            P.op("poolq", lambda e: e.dma_start(out=hg[:], in_=gath.rearrange("(r p) c -> p r c", p=128), allow_slow_non_contiguous=True),
                 reads=[f"gth{u}"], writes=[f"hg{u0}"])
            sel = prm("selprev")
            P.op("dve", lambda e: e.tensor_scalar(out=res[:], in0=hg[:, 0, :], scalar1=sel[:, 0:1], scalar2=None, op0=ALU.mult),
                 reads=[f"hg{u0}", "pp"], writes=[f"hr{u0}"])
            for j in range(1, 4):
                P.op("dve", lambda e, j=j: e.scalar_tensor_tensor(out=res[:], in0=hg[:, j, :], scalar=sel[:, j:j + 1], in1=res[:],
                                                                   op0=ALU.mult, op1=ALU.add),
                     reads=[f"hg{u0}", f"hr{u0}", "pp"], writes=[f"hr{u0}"])
            return res, f"hr{u0}"

        def mixer_phase(l):
            P.op("sp", lambda e: e.dma_start(out=ppl[:], in_=ppl_d[l, :, :]), writes=["pp"], lane="ppl")
            with ExitStack() as esm:
                def sbm(name, shape, dt):
                    return esm.enter_context(nc.sbuf_tensor(f"m{l}_{name}", shape, dt))

                def psm(name, shape, dt=F32):
                    return esm.enter_context(nc.psum_tensor(f"m{l}_{name}", shape, dt))
                pj = psm("pj", [128, 2, 512])
                po = pj
                hst = sbm("hst", [128, KC, HW], F32)
                P.op("dve", lambda e: e.tensor_copy(out=hst[:], in_=xn2[:, :, T:T + HW]),
                     reads=[f"xn2_{kc}_{NT - 1}" for kc in range(KC)], writes=["hst"])
                hres, hkey = exchange(sbm, "hst", hst[:].rearrange("p a b -> p (a b)"), KC * HW)
                P.op("dve", lambda e: e.tensor_copy(out=xn2[:, :, 0:HW], in_=hres[:].rearrange("p (a b) -> p a b", b=HW)),
                     reads=[hkey], writes=["xn2_halo"])
                xn2_keys = [f"xn2_{kc}_{tt}" for kc in range(KC) for tt in range(NT)]

                pcount = [0]

                def proj_fm(wbuf, wkey, col, M, evac, halo_evac=None):
                    for tt in range(NT):
                        pb = pcount[0] % 2
                        pcount[0] += 1
                        for kc in range(KC):
                            P.op("pe", lambda e, kc=kc, pb=pb, tt=tt: e.matmul(
                                pj[0:M, pb, :], lhsT=wbuf[:, kc, col:col + M], rhs=xn2[:, kc, HW + tt * 512:HW + (tt + 1) * 512],
                                start=(kc == 0), stop=(kc == KC - 1)),
                                reads=[wkey, f"xn2_{kc}_{tt}"], writes=[f"pj{pb}"])
                        evac(pj[0:M, pb, :], tt, [f"pj{pb}"])
                    if halo_evac is not None:
                        pb = pcount[0] % 2
                        pcount[0] += 1
                        for kc in range(KC):
                            P.op("pe", lambda e, kc=kc, pb=pb: e.matmul(
                                pj[0:M, pb, 0:HW], lhsT=wbuf[:, kc, col:col + M], rhs=xn2[:, kc, 0:HW],
                                start=(kc == 0), stop=(kc == KC - 1)),
                                reads=[wkey, "xn2_halo"], writes=[f"pj{pb}"])
                        halo_evac(pj[0:M, pb, 0:HW], [f"pj{pb}"])

                def load_win(sbx, name, c0, n):
                    wb = sbx(name, [128, KC, n], BF16)
                    P.op("poolq", lambda e: e.dma_start(out=wb[:], in_=win_d[l, :, :, c0:c0 + n]), writes=[name], lane="win")
                    return wb

                def out_proj(sbx, mi, y, ykeys, cc0=None, ncc=2):
                    if cc0 is None:
                        cc0 = 2 * mi
                    wo = sbx(f"wo{mi}", [128, ncc, D], BF16)
                    ost = sbx(f"ost{mi}", [128, 2, 512], F32)
                    P.op("poolq", lambda e: e.dma_start(out=wo[:], in_=wout_d[l, :, cc0:cc0 + ncc, :]), writes=[f"wo{mi}"], lane="wo")
                    cnt = 0
                    for dm in range(KC):
                        for tt in range(NT):
                            pb = cnt % 2
                            cnt += 1
                            for c2 in range(ncc):
                                P.op("pe", lambda e, c2=c2, dm=dm, tt=tt, pb=pb: e.matmul(
                                    po[:, pb, :], lhsT=wo[:, c2, dm * 128:(dm + 1) * 128], rhs=y[:, c2, tt * 512:(tt + 1) * 512],
                                    start=(c2 == 0), stop=(c2 == ncc - 1)),
                                    reads=[f"wo{mi}"] + ykeys, writes=[f"pj{pb}"])
                            P.op("act", lambda e, pb=pb: e.activation(out=ost[:, pb, :], in_=po[:, pb, :], func=AF.Copy),
                                 reads=[f"pj{pb}"], writes=[f"ost{mi}_{pb}"])
                            P.op("poolq", lambda e, dm=dm, tt=tt, pb=pb: e.dma_start(out=xd[:, dm, tt * 512:(tt + 1) * 512], in_=ost[:, pb, :], accum_op=ALU.add),
                                 reads=[f"ost{mi}_{pb}", f"xd{dm}_{tt}"], writes=[f"xd{dm}_{tt}"], lane=f"xacc{pb}")

                def lru():
                    with ExitStack() as e1:
                        def sb1(name, shape, dt):
                            return e1.enter_context(nc.sbuf_tensor(f"lru{l}_{name}", shape, dt))
                        wl = load_win(sb1, f"lru{l}_w", 784, 512)
                        y = sb1("y", [128, 2, T], BF16)
                        pr = e1.enter_context(nc.psum_tensor(f"lru{l}_pr", [128, 2, 512], F32))
                        for ct in range(2):
                            with ExitStack() as e2:
                                def sb3(name, shape, dt):
                                    return e2.enter_context(nc.sbuf_tensor(f"lru{l}_{ct}_{name}", shape, dt))
                                xb = sb3("xb", [128, HW + T], F32)
                                gt = sb3("gt", [128, T], BF16)
                                xc = sb3("xc", [128, T], F32)
                                rr = sb3("rr", [128, T], F32)
                                ii = sb3("ii", [128, T], F32)
                                tmp = sb3("tmp", [128, T], F32)
                                hh = sb3("hh", [128, T], F32)
                                sm = sb3("sm", [128, 8], F32)
                                hin = sb3("hin", [128, 1], F32)
                                proj_fm(wl, f"lru{l}_w", ct * 128, 128,
                                        lambda p_ap, tt, rk: P.op("act", lambda e: e.activation(out=xb[:, HW + tt * 512:HW + (tt + 1) * 512], in_=p_ap, func=AF.Copy),
                                                                  reads=rk, writes=["xb"]),
                                        lambda p_ap, rk: P.op("act", lambda e: e.activation(out=xb[:, 0:HW], in_=p_ap, func=AF.Copy),
                                                              reads=rk, writes=["xb"]))
                                proj_fm(wl, f"lru{l}_w", 256 + ct * 128, 128,
                                        lambda p_ap, tt, rk: P.op("act", lambda e: e.activation(out=gt[:, tt * 512:(tt + 1) * 512], in_=p_ap, func=AF.Gelu),
                                                                  reads=rk, writes=["gt"]))
                                cw = prm(f"lru_cw_{l}")[:, ct * 4:(ct + 1) * 4]
                                cb = prm(f"lru_cb_{l}")[:, ct:ct + 1]
                                P.op("dve", lambda e: e.tensor_scalar(out=xc[:], in0=xb[:, 1:1 + T], scalar1=cw[:, 0:1], scalar2=cb,
                                                                      op0=ALU.mult, op1=ALU.add), reads=["xb", "pp"], writes=["xc"])
                                for k in range(1, 4):
                                    P.op("dve", lambda e, k=k: e.scalar_tensor_tensor(out=xc[:], in0=xb[:, k + 1:k + 1 + T], scalar=cw[:, k:k + 1],
                                                                                      in1=xc[:], op0=ALU.mult, op1=ALU.add),
                                         reads=["xb", "xc", "pp"], writes=["xc"])
                                for (dst, dkey, wnm, bnm) in ((rr, "rr", "wa", "ba"), (ii, "ii", "wx", "bx")):
                                    wmat = prm(f"lru_{wnm}_{l}_{ct}")
                                    bcol = prm(f"lru_{bnm}_{l}")[:, ct:ct + 1]
                                    for tt in range(NT):
                                        pb = tt % 2
                                        P.op("pe", lambda e, tt=tt, pb=pb, wmat=wmat: e.matmul(pr[:, pb, :], lhsT=wmat, rhs=xc[:, tt * 512:(tt + 1) * 512],
                                                                                               start=True, stop=True),
                                             reads=["xc", "pp"], writes=[f"pr{pb}"])
                                        P.op("act", lambda e, tt=tt, pb=pb, dst=dst, bcol=bcol: e.activation(
                                            out=dst[:, tt * 512:(tt + 1) * 512], in_=pr[:, pb, :], func=AF.Sigmoid, bias=bcol),
                                            reads=[f"pr{pb}", "pp"], writes=[dkey])
                                lam = prm(f"lru_lam_{l}")[:, ct:ct + 1]
                                P.op("act", lambda e: e.activation(out=sm[:, 0:1], in_=lam, func=AF.Exp, scale=-1.0), reads=["pp"], writes=["sm"])
                                P.op("act", lambda e: e.activation(out=sm[:, 1:2], in_=sm[:, 0:1], func=AF.Ln, bias=1.0), reads=["sm"], writes=["sm"])
                                P.op("dve", lambda e: e.tensor_scalar(out=sm[:, 2:3], in0=sm[:, 1:2], scalar1=-8.0, scalar2=None, op0=ALU.mult),
                                     reads=["sm"], writes=["sm"])
                                P.op("act", lambda e: e.activation(out=rr[:], in_=rr[:], func=AF.Exp, scale=sm[:, 2:3]), reads=["rr", "sm"], writes=["rr"])
                                P.op("dve", lambda e: e.tensor_tensor(out=tmp[:], in0=rr[:], in1=rr[:], op=ALU.mult), reads=["rr"], writes=["tmp"])
                                P.op("act", lambda e: e.activation(out=tmp[:], in_=tmp[:], func=AF.Sqrt, scale=-1.0, bias=1.0), reads=["tmp"], writes=["tmp"])
                                P.op("dve", lambda e: e.tensor_tensor(out=ii[:], in0=ii[:], in1=tmp[:], op=ALU.mult), reads=["ii", "tmp"], writes=["ii"])
                                P.op("dve", lambda e: e.tensor_tensor(out=ii[:], in0=ii[:], in1=xc[:], op=ALU.mult), reads=["ii", "xc"], writes=["ii"])
                                P.op("dve", lambda e: e.memset(hin[:], 0.0), writes=["hin"])
                                for rnd in range(4):
                                    P.op("dve", lambda e: e.tensor_tensor_scan(out=hh[:], data0=rr[:], data1=ii[:], initial=hin[:, 0:1],
                                                                               op0=ALU.mult, op1=ALU.add),
                                         reads=["rr", "ii", "hin"], writes=["hh"])
                                    if rnd < 3:
                                        res, rkey = exchange(sb3, "hh", hh[:, T - 1:T], 1)
                                        P.op("dve", lambda e, res=res: e.tensor_copy(out=hin[:], in_=res[:]), reads=[rkey], writes=["hin"])
                                P.op("dve", lambda e: e.tensor_tensor(out=y[:, ct, :], in0=hh[:], in1=gt[:], op=ALU.mult),
                                     reads=["hh", "gt"], writes=[f"ylru{ct}"])
                                P.barrier()
                        out_proj(sb1, 1, y, ["ylru0", "ylru1"])
                        P.barrier()

                if "lru" in mixers:
                    lru()
                P.barrier()

                def gla():
                    NCH = T // 128
                    with ExitStack() as e1:
                        def sb1(name, shape, dt):
                            return e1.enter_context(nc.sbuf_tensor(f"gla{l}_{name}", shape, dt))

                        def ps1(name, shape, dt=F32):
                            return e1.enter_context(nc.psum_tensor(f"gla{l}_{name}", shape, dt))
                        wkey = f"gla{l}_w"
                        qr = sb1("qr", [128, T], BF16)
                        kr = sb1("kr", [128, T], BF16)
                        st = sb1("st", [16, T], F32)
                        vt = sb1("vt", [128, NCH, 256], BF16)
                        gt = sb1("gt", [128, NCH, 256], BF16)
                        bc = sb1("bc", [128, T], F32)
                        br = sb1("br", [128, T], F32)
                        one_t = sb1("one", [128, T], F32)
                        qd = sb1("qd", [128, T], BF16)
                        kd = sb1("kd", [128, T], BF16)
                        sm = sb1("sm", [128, 4 * NCH + 4], F32)
                        identb = sb1("identb", [128, 128], BF16)
                        ptm = ps1("ptm", [128, 512])
                        psc = ps1("psc", [128, 512])
                        pkv = ps1("pkv", [128, 512])
                        ptr = ps1("ptr", [128, 1024], BF16)
                        ew = e1.enter_context(ExitStack())
                        wl = load_win(lambda name, shape, dt: ew.enter_context(nc.sbuf_tensor(name, shape, dt)), wkey, 0, 784)
                        P.op("dve", lambda e: e.tensor_copy(out=identb[:], in_=prm("ident")), reads=["pp"], writes=["identb"])
                        P.op("dve", lambda e: e.memset(one_t[:], 1.0), writes=["one"])
                        sc = 32.0 ** -0.5
                        proj_fm(wl, wkey, 0, 128, lambda p_ap, tt, rk: P.op(
                            "act", lambda e: e.activation(out=qr[:, tt * 512:(tt + 1) * 512], in_=p_ap, func=AF.Copy, scale=sc), reads=rk, writes=["qr"]))
                        proj_fm(wl, wkey, 128, 128, lambda p_ap, tt, rk: P.op(
                            "act", lambda e: e.activation(out=kr[:, tt * 512:(tt + 1) * 512], in_=p_ap, func=AF.Copy), reads=rk, writes=["kr"]))
                        proj_fm(wl, wkey, 768, 16, lambda p_ap, tt, rk: P.op(
                            "act", lambda e: e.activation(out=st[:, tt * 512:(tt + 1) * 512], in_=p_ap, func=AF.Copy), reads=rk, writes=["st"]))
                        for c in range(NCH):
                            for kc in range(KC):
                                P.op("pe", lambda e, kc=kc, c=c: e.matmul(ptm[:, :], lhsT=xn2[:, kc, HW + c * 128:HW + (c + 1) * 128], rhs=wl[:, kc, 256:768],
                                                                           start=(kc == 0), stop=(kc == KC - 1)),
                                     reads=[wkey] + xn2_keys, writes=["ptm"])
                            P.op("act", lambda e, c=c: e.activation(out=vt[:, c, :], in_=ptm[:, 0:256], func=AF.Copy), reads=["ptm"], writes=[f"vt{c}"])
                            P.op("act", lambda e, c=c: e.activation(out=gt[:, c, :], in_=ptm[:, 256:512], func=AF.Silu), reads=["ptm"], writes=[f"gt{c}"])
                        P.barrier()
                        ew.close()
                        PTs = sb1("PTs", [128, NCH, 512], BF16)
                        ktm = sb1("ktm", [128, NCH, 128], BF16)
                        kp = sb1("kp", [128, 128], BF16)
                        qx = sb1("qx", [128, 512], BF16)
                        S = sb1("S", [128, 256], F32)
                        Sb = sb1("Sb", [128, 256], BF16)
                        tkv = sb1("tkv", [128, 256], F32)
                        oa = sb1("oa", [128, NCH, 256], F32)
                        ob_ = sb1("ob", [128, NCH, 256], BF16)
                        y = sb1("y", [128, 2, T], BF16)
                        aup = prm(f"gla_aup_{l}")
                        P.op("dve", lambda e: e.tensor_scalar(out=sm[:, 0:1], in0=prm(f"gla_ab_{l}")[:, 0:1], scalar1=-1.0, scalar2=None, op0=ALU.mult),
                             reads=["pp"], writes=["sm"])
                        for tt in range(NT):
                            P.op("pe", lambda e, tt=tt: e.matmul(psc[:, :], lhsT=aup[0:16, :], rhs=st[:, tt * 512:(tt + 1) * 512], start=True, stop=True),
                                 reads=["pp", "st"], writes=["psc"])
                            P.op("act", lambda e, tt=tt: e.activation(out=br[:, tt * 512:(tt + 1) * 512], in_=psc[:, :], func=AF.Exp, scale=-1.0, bias=sm[:, 0:1]),
                                 reads=["psc", "sm"], writes=["br"])
                        P.op("act", lambda e: e.activation(out=bc[:], in_=br[:], func=AF.Ln, bias=1.0), reads=["br"], writes=["bc"])
                        P.op("dve", lambda e: e.tensor_scalar(out=bc[:], in0=bc[:], scalar1=-1.0 / 16.0, scalar2=None, op0=ALU.mult), reads=["bc"], writes=["bc"])
                        P.op("dve", lambda e: e.tensor_tensor_scan(out=br[:], data0=one_t[:], data1=bc[:], initial=0.0, op0=ALU.mult, op1=ALU.add),
                             reads=["bc", "one"], writes=["br"])
                        br3 = br[:].rearrange("p (c i) -> p c i", i=128)
                        bc3 = bc[:].rearrange("p (c i) -> p c i", i=128)
                        bst = sm[:, 4:4 + NCH]
                        edec = sm[:, 4 + NCH:4 + 2 * NCH]
                        P.op("dve", lambda e: e.memset(sm[:, 4:5], 0.0), writes=["sm"])
                        if NCH > 1:
                            P.op("dve", lambda e: e.tensor_copy(out=sm[:, 5:4 + NCH], in_=br3[:, 0:NCH - 1, 127]), reads=["br"], writes=["sm"])
                        P.op("dve", lambda e: e.tensor_tensor(out=bc3, in0=br3, in1=bst.unsqueeze(2).to_broadcast([128, NCH, 128]), op=ALU.subtract),
                             reads=["br", "sm"], writes=["bc"])
                        P.op("act", lambda e: e.activation(out=edec, in_=bc3[:, :, 127], func=AF.Exp), reads=["bc"], writes=["sm"])
                        P.op("act", lambda e: e.activation(out=br[:], in_=bc[:], func=AF.Exp), reads=["bc"], writes=["br"])
                        P.op("dve", lambda e: e.tensor_tensor(out=qd[:], in0=qr[:], in1=br[:], op=ALU.mult), reads=["qr", "br"], writes=["qd"])
                        P.op("act", lambda e: e.activation(out=bc[:], in_=bc[:], func=AF.Exp, scale=-1.0), reads=["bc"], writes=["bc"])
                        P.op("dve", lambda e: e.tensor_tensor(out=kd[:], in0=kr[:], in1=bc[:], op=ALU.mult), reads=["kr", "bc"], writes=["kd"])
                        hm = prm("hm")
                        vkeys = [f"vt{c}" for c in range(NCH)]
                        P.op("dve", lambda e: e.memset(S[:], 0.0), writes=["S"])
                        for rnd in range(4):
                            P.op("act", lambda e: e.activation(out=Sb[:], in_=S[:], func=AF.Copy), reads=["S"], writes=["Sb"])
                            for c in range(NCH):
                                cs = slice(c * 128, (c + 1) * 128)
                                if rnd == 0:
                                    P.op("dve", lambda e, c=c, cs=cs: e.tensor_scalar(out=kp[:], in0=kd[:, cs], scalar1=edec[:, c:c + 1], scalar2=None, op0=ALU.mult),
                                         reads=["kd", "sm"], writes=["kp"])
                                    P.op("pe", lambda e: e.transpose(out=ptr[:, 0:128], in_=kp[:], identity=identb[:]), reads=["kp", "identb"], writes=["ptr"])
                                    P.op("act", lambda e, c=c: e.activation(out=ktm[:, c, :], in_=ptr[:, 0:128], func=AF.Copy), reads=["ptr"], writes=[f"ktm{c}"])
                                    for h in range(4):
                                        P.op("dve", lambda e, h=h, cs=cs: e.tensor_scalar(out=qx[:, h * 128:(h + 1) * 128], in0=qd[:, cs], scalar1=hm[:, h:h + 1],
                                                                                          scalar2=None, op0=ALU.mult),
                                             reads=["qd", "pp"], writes=["qx"])
                                    P.op("pe", lambda e, cs=cs: e.matmul(psc[:, :], lhsT=kd[:, cs], rhs=qx[:], start=True, stop=True),
                                         reads=["kd", "qx"], writes=["psc"])
                                    P.op("dve", lambda e, c=c: e.tensor_tensor(out=PTs[:, c, :], in0=psc[:, :], in1=prm("mask4"), op=ALU.mult),
                                         reads=["psc", "pp"], writes=[f"PT{c}"])
                                P.op("pe", lambda e, cs=cs: e.matmul(ptm[:, 0:256], lhsT=qd[:, cs], rhs=Sb[:], start=True, stop=False),
                                     reads=["qd", "Sb"], writes=["ptm"])
                                for h in range(4):
                                    P.op("pe", lambda e, h=h, c=c: e.matmul(ptm[:, h * 64:(h + 1) * 64], lhsT=PTs[:, c, h * 128:(h + 1) * 128],
                                                                            rhs=vt[:, c, h * 64:(h + 1) * 64], start=False, stop=(h == 3)),
                                         reads=[f"PT{c}", f"vt{c}"], writes=["ptm"])
                                P.op("act", lambda e, c=c: e.activation(out=oa[:, c, :], in_=ptm[:, 0:256], func=AF.Copy), reads=["ptm"], writes=[f"oa{c}"])
                                P.op("pe", lambda e, c=c: e.matmul(pkv[:, 0:256], lhsT=ktm[:, c, :], rhs=vt[:, c, :], start=True, stop=True),
                                     reads=[f"ktm{c}", f"vt{c}"], writes=["pkv"])
                                P.op("dve", lambda e: e.tensor_tensor(out=tkv[:], in0=pkv[:, 0:256], in1=prm("bm"), op=ALU.mult), reads=["pkv", "pp"], writes=["tkv"])
                                P.op("dve", lambda e, c=c: e.scalar_tensor_tensor(out=S[:], in0=S[:], scalar=edec[:, c:c + 1], in1=tkv[:], op0=ALU.mult, op1=ALU.add),
                                     reads=["S", "tkv", "sm"], writes=["S"])
                                P.op("act", lambda e: e.activation(out=Sb[:], in_=S[:], func=AF.Copy), reads=["S"], writes=["Sb"])
                            if rnd < 3:
                                res, rkey = exchange(sb1, "S", S[:], 256)
                                P.op("dve", lambda e, res=res: e.tensor_copy(out=S[:], in_=res[:]), reads=[rkey], writes=["S"])
                        okeys = [f"oa{c}" for c in range(NCH)]
                        oa4 = oa[:].rearrange("p c (h v) -> p (c h) v", v=64)
                        ob4 = ob_[:].rearrange("p c (h v) -> p (c h) v", v=64)
                        sq_ = sb1("sqv", [128, NCH, 256], F32)
                        sq4 = sq_[:].rearrange("p c (h v) -> p (c h) v", v=64)
                        rsd = sb1("rsd", [128, NCH * 4], F32)
                        P.op("dve", lambda e: e.tensor_tensor(out=sq_[:], in0=oa[:], in1=oa[:], op=ALU.mult), reads=okeys, writes=["sqv"])
                        P.op("dve", lambda e: e.tensor_reduce(out=rsd[:], in_=sq4, axis=AX.X, op=ALU.add), reads=["sqv"], writes=["rsd"])
                        P.op("act", lambda e: e.activation(out=rsd[:], in_=rsd[:], func=AF.Sqrt, scale=1.0 / 64.0, bias=1e-5), reads=["rsd"], writes=["rsd"])
                        P.op("dve", lambda e: e.reciprocal(out=rsd[:], in_=rsd[:]), reads=["rsd"], writes=["rsd"])
                        P.op("dve", lambda e: e.tensor_tensor(out=sq4, in0=oa4, in1=rsd[:].unsqueeze(2).to_broadcast([128, NCH * 4, 64]), op=ALU.mult),
                             reads=okeys + ["rsd"], writes=["sqv"])
                        P.op("dve", lambda e: e.tensor_tensor(out=sq4, in0=sq4, in1=prm(f"gla_nw_{l}").unsqueeze(1).to_broadcast([128, NCH * 4, 64]), op=ALU.mult),
                             reads=["sqv", "pp"], writes=["sqv"])
                        P.op("dve", lambda e: e.tensor_tensor(out=ob_[:], in0=sq_[:], in1=gt[:], op=ALU.mult),
                             reads=["sqv"] + [f"gt{c}" for c in range(NCH)], writes=["ob"])
                        for c in range(NCH):
                            for ct in range(2):
                                P.op("pe", lambda e, c=c, ct=ct: e.transpose(out=ptr[:, 0:128], in_=ob_[:, c, ct * 128:(ct + 1) * 128], identity=identb[:]),
                                     reads=["ob", "identb"], writes=["ptr"])
                                P.op("act", lambda e, c=c, ct=ct: e.activation(out=y[:, ct, c * 128:(c + 1) * 128], in_=ptr[:, 0:128], func=AF.Copy),
                                     reads=["ptr"], writes=["ygla"])
                        out_proj(sb1, 0, y, ["ygla"])
                        P.barrier()

                if "gla" in mixers:
                    gla()
                P.barrier()

                def ssd():
                    NCH = T // 128
                    with ExitStack() as e1:
                        def sb1(name, shape, dt):
                            return e1.enter_context(nc.sbuf_tensor(f"ssd{l}_{name}", shape, dt))

                        def ps1(name, shape, dt=F32):
                            return e1.enter_context(nc.psum_tensor(f"ssd{l}_{name}", shape, dt))
                        wkey = f"ssd{l}_w"
                        BT = sb1("BT", [128, 2, T], BF16)
                        CT = sb1("CT", [128, 2, T], BF16)
                        zt = sb1("zt", [128, NCH, 256], BF16)
                        dtt = sb1("dtt", [128, NCH, 4], F32)
                        dA = sb1("dA", [128, NCH, 4], F32)
                        cs = sb1("cs", [128, NCH, 4], F32)
                        ncs = sb1("ncs", [128, NCH, 4], F32)
                        csl = sb1("csl", [128, NCH, 4], F32)
                        ecs = sb1("ecs", [128, NCH, 4], F32)
                        dend = sb1("dend", [128, NCH, 4], F32)
                        ecsl = sb1("ecsl", [128, NCH, 4], F32)
                        av = sb1("av", [128, 4], F32)
                        xt = sb1("xt", [128, NCH, 256], F32)
                        xdt = sb1("xdt", [128, NCH, 256], BF16)
                        Bt = sb1("Bt", [128, NCH, 256], BF16)
                        identf = prm("ident")
                        ptm = ps1("ptm", [128, 512])
                        pa = ps1("pa", [128, 512])
                        pb_ = ps1("pb", [128, 512])
                        pc = ps1("pc", [128, 512])
                        ew = e1.enter_context(ExitStack())

                        def sbw(name, shape, dt):
                            return ew.enter_context(nc.sbuf_tensor(f"ssd{l}_{name}" if not name.startswith("ssd") else name, shape, dt))
                        wl = load_win(sbw, wkey, 2128, 1028)
                        xb = sbw("xb", [128, HW + T], F32)
                        xc = sbw("xc", [128, T], F32)
                        xsT = sbw("xsT", [128, 2, T], F32)
                        BTf = sbw("BTf", [128, 2, T], F32)
                        for ct in range(6):
                            proj_fm(wl, wkey, 256 + ct * 128, 128,
                                    lambda p_ap, tt, rk: P.op("act", lambda e: e.activation(out=xb[:, HW + tt * 512:HW + (tt + 1) * 512], in_=p_ap, func=AF.Copy),
                                                              reads=rk, writes=["xb"]),
                                    lambda p_ap, rk: P.op("act", lambda e: e.activation(out=xb[:, 0:HW], in_=p_ap, func=AF.Copy), reads=rk, writes=["xb"]))
                            cw = prm(f"ssd_cw_{l}")[:, ct * 4:(ct + 1) * 4]
                            cb = prm(f"ssd_cb_{l}")[:, ct:ct + 1]
                            P.op("dve", lambda e: e.tensor_scalar(out=xc[:], in0=xb[:, 1:1 + T], scalar1=cw[:, 0:1], scalar2=cb, op0=ALU.mult, op1=ALU.add),
                                 reads=["xb", "pp"], writes=["xc"])
                            for k in range(1, 4):
                                P.op("dve", lambda e, k=k: e.scalar_tensor_tensor(out=xc[:], in0=xb[:, k + 1:k + 1 + T], scalar=cw[:, k:k + 1], in1=xc[:],
                                                                                  op0=ALU.mult, op1=ALU.add), reads=["xb", "xc", "pp"], writes=["xc"])
                            if ct < 2:
                                P.op("act", lambda e, ct=ct: e.activation(out=xsT[:, ct, :], in_=xc[:], func=AF.Silu), reads=["xc"], writes=["xsT"])
                            elif ct < 4:
                                P.op("act", lambda e, ct=ct: e.activation(out=BTf[:, ct - 2, :], in_=xc[:], func=AF.Silu), reads=["xc"], writes=["BTf"])
                                P.op("dve", lambda e, ct=ct: e.tensor_copy(out=BT[:, ct - 2, :], in_=BTf[:, ct - 2, :]), reads=["BTf"], writes=["BT"])
                            else:
                                P.op("act", lambda e, ct=ct: e.activation(out=CT[:, ct - 4, :], in_=xc[:], func=AF.Silu), reads=["xc"], writes=["CT"])
                        P.op("act", lambda e: e.activation(out=av[:], in_=prm(f"ssd_alog_{l}"), func=AF.Exp), reads=["pp"], writes=["av"])
                        P.op("dve", lambda e: e.tensor_scalar(out=av[:], in0=av[:], scalar1=-1.0, scalar2=None, op0=ALU.mult), reads=["av"], writes=["av"])
                        for c in range(NCH):
                            tsl = slice(HW + c * 128, HW + (c + 1) * 128)
                            for kc in range(KC):
                                P.op("pe", lambda e, kc=kc, tsl=tsl: e.matmul(ptm[:, 0:256], lhsT=xn2[:, kc, tsl], rhs=wl[:, kc, 0:256],
                                                                              start=(kc == 0), stop=(kc == KC - 1)), reads=[wkey] + xn2_keys, writes=["ptm"])
                            P.op("act", lambda e, c=c: e.activation(out=zt[:, c, :], in_=ptm[:, 0:256], func=AF.Silu), reads=["ptm"], writes=["zt"])
                            for kc in range(KC):
                                P.op("pe", lambda e, kc=kc, tsl=tsl: e.matmul(pa[:, 0:4], lhsT=xn2[:, kc, tsl], rhs=wl[:, kc, 1024:1028],
                                                                              start=(kc == 0), stop=(kc == KC - 1)), reads=[wkey] + xn2_keys, writes=["pa"])
                            P.op("dve", lambda e, c=c: e.tensor_tensor(out=dtt[:, c, :], in0=pa[:, 0:4], in1=prm(f"ssd_dtb_{l}"), op=ALU.add),
                                 reads=["pa", "pp"], writes=["dtt"])
                            for ct in range(2):
                                P.op("pe", lambda e, c=c, ct=ct: e.transpose(out=pb_[:, ct * 128:(ct + 1) * 128], in_=xsT[:, ct, c * 128:(c + 1) * 128], identity=identf),
                                     reads=["xsT", "pp"], writes=["pb"])
                                P.op("pe", lambda e, c=c, ct=ct: e.transpose(out=pb_[:, 256 + ct * 128:256 + (ct + 1) * 128], in_=BTf[:, ct, c * 128:(c + 1) * 128], identity=identf),
                                     reads=["BTf", "pp"], writes=["pb"])
                            P.op("act", lambda e, c=c: e.activation(out=xt[:, c, :], in_=pb_[:, 0:256], func=AF.Copy), reads=["pb"], writes=["xt"])
                            P.op("act", lambda e, c=c: e.activation(out=Bt[:, c, :], in_=pb_[:, 256:512], func=AF.Copy), reads=["pb"], writes=["Bt"])
                        P.barrier()
                        ew.close()
                        yd = sb1("yd", [128, NCH, 256], F32)
                        ya = sb1("ya", [128, NCH, 256], F32)
                        stl = sb1("stl", [128, NCH, 256], F32)
                        hs = sb1("hs", [128, 256], F32)
                        hsb = sb1("hsb", [128, 256], BF16)
                        dAb = sb1("dAb", [128, 128], F32)
                        tmpm = sb1("tmpm", [128, 128], F32)
                        decT = sb1("decT", [128, 128], F32)
                        MT = sb1("MT", [128, 128], BF16)
                        xde = sb1("xde", [128, 256], BF16)
                        y = sb1("y", [128, 2, T], BF16)
                        P.op("act", lambda e: e.activation(out=dtt[:], in_=dtt[:], func=AF.Exp), reads=["dtt"], writes=["dtt"])
                        P.op("act", lambda e: e.activation(out=dtt[:], in_=dtt[:], func=AF.Ln, bias=1.0), reads=["dtt"], writes=["dtt"])
                        P.op("dve", lambda e: e.tensor_tensor(out=dA[:], in0=dtt[:], in1=av[:].unsqueeze(1).to_broadcast([128, NCH, 4]), op=ALU.mult),
                             reads=["dtt", "av"], writes=["dA"])
                        xt4 = xt[:].rearrange("p c (h v) -> p (c h) v", v=64)
                        xdt4 = xdt[:].rearrange("p c (h v) -> p (c h) v", v=64)
                        P.op("dve", lambda e: e.tensor_tensor(out=xdt4, in0=xt4, in1=dtt[:].rearrange("p c h -> p (c h)").unsqueeze(2).to_broadcast([128, NCH * 4, 64]),
                                                              op=ALU.mult), reads=["xt", "dtt"], writes=["xdt"])
                        for c in range(NCH):
                            P.op("pe", lambda e, c=c: e.matmul(pa[:, 0:4], lhsT=prm("utri"), rhs=dA[:, c, :], start=True, stop=True), reads=["dA", "pp"], writes=["pa"])
                            P.op("pe", lambda e, c=c: e.matmul(pa[:, 8:12], lhsT=prm("ones"), rhs=dA[:, c, :], start=True, stop=True), reads=["dA", "pp"], writes=["pa"])
                            P.op("act", lambda e, c=c: e.activation(out=cs[:, c, :], in_=pa[:, 0:4], func=AF.Copy), reads=["pa"], writes=["cs"])
                            P.op("act", lambda e, c=c: e.activation(out=csl[:, c, :], in_=pa[:, 8:12], func=AF.Copy), reads=["pa"], writes=["csl"])
                        P.op("dve", lambda e: e.tensor_scalar(out=ncs[:], in0=cs[:], scalar1=-1.0, scalar2=None, op0=ALU.mult), reads=["cs"], writes=["ncs"])
                        P.op("act", lambda e: e.activation(out=ecs[:], in_=cs[:], func=AF.Exp), reads=["cs"], writes=["ecs"])
                        P.op("act", lambda e: e.activation(out=ecsl[:], in_=csl[:], func=AF.Exp), reads=["csl"], writes=["ecsl"])
                        P.op("dve", lambda e: e.tensor_tensor(out=dend[:], in0=csl[:], in1=cs[:], op=ALU.subtract), reads=["cs", "csl"], writes=["dend"])
                        P.op("act", lambda e: e.activation(out=dend[:], in_=dend[:], func=AF.Exp), reads=["dend"], writes=["dend"])
                        for c in range(NCH):
                            cs_ = slice(c * 128, (c + 1) * 128)
                            for g in range(2):
                                P.op("pe", lambda e, g=g, cs_=cs_: e.matmul(pb_[:, g * 128:(g + 1) * 128], lhsT=BT[:, g, cs_], rhs=CT[:, g, cs_], start=True, stop=True),
                                     reads=["BT", "CT"], writes=["pb"])
                            for h in range(4):
                                g = h // 2
                                P.op("dve", lambda e, c=c, h=h: e.tensor_scalar(out=dAb[:], in0=prm("ones"), scalar1=dA[:, c, h:h + 1], scalar2=None, op0=ALU.mult),
                                     reads=["dA", "pp"], writes=["dAb"])
                                P.op("pe", lambda e: e.matmul(pc[:, 0:128], lhsT=dAb[:], rhs=prm("utri"), start=True, stop=True), reads=["dAb", "pp"], writes=["pc"])
                                P.op("dve", lambda e: e.tensor_tensor(out=tmpm[:], in0=pc[:, 0:128], in1=prm("negmask"), op=ALU.add), reads=["pc", "pp"], writes=["tmpm"])
                                P.op("act", lambda e, c=c, h=h: e.activation(out=decT[:], in_=tmpm[:], func=AF.Exp, bias=ncs[:, c, h:h + 1]),
                                     reads=["tmpm", "ncs"], writes=["decT"])
                                P.op("dve", lambda e, g=g: e.tensor_tensor(out=MT[:], in0=pb_[:, g * 128:(g + 1) * 128], in1=decT[:], op=ALU.mult),
                                     reads=["pb", "decT"], writes=["MT"])
                                P.op("pe", lambda e, c=c, h=h: e.matmul(pc[:, 128 + h * 64:128 + (h + 1) * 64], lhsT=MT[:], rhs=xdt[:, c, h * 64:(h + 1) * 64],
                                                                        start=True, stop=True), reads=["MT", "xdt"], writes=["pc"])
                                P.op("dve", lambda e, c=c, h=h: e.tensor_scalar(out=xde[:, h * 64:(h + 1) * 64], in0=xdt[:, c, h * 64:(h + 1) * 64],
                                                                                scalar1=dend[:, c, h:h + 1], scalar2=None, op0=ALU.mult),
                                     reads=["xdt", "dend"], writes=["xde"])
                                P.op("pe", lambda e, c=c, h=h, g=g: e.matmul(pa[:, 128 + h * 64:128 + (h + 1) * 64], lhsT=Bt[:, c, g * 128:(g + 1) * 128],
                                                                             rhs=xde[:, h * 64:(h + 1) * 64], start=True, stop=True),
                                     reads=["Bt", "xde"], writes=["pa"])
                            P.op("act", lambda e, c=c: e.activation(out=yd[:, c, :], in_=pc[:, 128:384], func=AF.Copy), reads=["pc"], writes=["yd"])
                            P.op("act", lambda e, c=c: e.activation(out=stl[:, c, :], in_=pa[:, 128:384], func=AF.Copy), reads=["pa"], writes=["stl"])
                        P.op("dve", lambda e: e.memset(hs[:], 0.0), writes=["hs"])
                        for rnd in range(4):
                            P.op("act", lambda e: e.activation(out=hsb[:], in_=hs[:], func=AF.Copy), reads=["hs"], writes=["hsb"])
                            for c in range(NCH):
                                cs_ = slice(c * 128, (c + 1) * 128)
                                for g in range(2):
                                    P.op("pe", lambda e, g=g, cs_=cs_: e.matmul(ptm[:, g * 128:(g + 1) * 128], lhsT=CT[:, g, cs_], rhs=hsb[:, g * 128:(g + 1) * 128],
                                                                                start=True, stop=True), reads=["CT", "hsb"], writes=["ptm"])
                                for h in range(4):
                                    hc = slice(h * 64, (h + 1) * 64)
                                    P.op("dve", lambda e, c=c, h=h, hc=hc: e.scalar_tensor_tensor(out=ya[:, c, hc], in0=ptm[:, hc], scalar=ecs[:, c, h:h + 1], in1=yd[:, c, hc],
                                                                                                  op0=ALU.mult, op1=ALU.add), reads=["ptm", "ecs", "yd"], writes=["ya"])
                                    P.op("dve", lambda e, c=c, h=h, hc=hc: e.scalar_tensor_tensor(out=hs[:, hc], in0=hs[:, hc], scalar=ecsl[:, c, h:h + 1], in1=stl[:, c, hc],
                                                                                                  op0=ALU.mult, op1=ALU.add), reads=["hs", "ecsl", "stl"], writes=["hs"])
                                P.op("act", lambda e: e.activation(out=hsb[:], in_=hs[:], func=AF.Copy), reads=["hs"], writes=["hsb"])
                            if rnd < 3:
                                res, rkey = exchange(sb1, "hs", hs[:], 256)
                                P.op("dve", lambda e, res=res: e.tensor_copy(out=hs[:], in_=res[:]), reads=[rkey], writes=["hs"])
                        ya4 = ya[:].rearrange("p c (h v) -> p (c h) v", v=64)
                        yd4 = yd[:].rearrange("p c (h v) -> p (c h) v", v=64)
                        P.op("dve", lambda e: e.tensor_tensor(out=yd[:].rearrange("p c (h v) -> p c h v", v=64), in0=xt[:].rearrange("p c (h v) -> p c h v", v=64), in1=prm(f"ssd_d_{l}").unsqueeze(1).unsqueeze(3).to_broadcast([128, NCH, 4, 64]),
                                                              op=ALU.mult), reads=["xt", "pp", "yd"], writes=["yd"])
                        P.op("dve", lambda e: e.tensor_tensor(out=ya[:], in0=ya[:], in1=yd[:], op=ALU.add), reads=["ya", "yd"], writes=["ya"])
                        P.op("dve", lambda e: e.tensor_tensor(out=ya[:], in0=ya[:], in1=zt[:], op=ALU.mult), reads=["ya", "zt"], writes=["ya"])
                        rsd = sb1("rsd", [128, NCH * 2], F32)
                        ya2 = ya[:].rearrange("p c (g v) -> p (c g) v", v=128)
                        yd2 = yd[:].rearrange("p c (g v) -> p (c g) v", v=128)
                        P.op("dve", lambda e: e.tensor_tensor(out=yd[:], in0=ya[:], in1=ya[:], op=ALU.mult), reads=["ya"], writes=["yd"])
                        P.op("dve", lambda e: e.tensor_reduce(out=rsd[:], in_=yd2, axis=AX.X, op=ALU.add), reads=["yd"], writes=["rsd"])
                        P.op("act", lambda e: e.activation(out=rsd[:], in_=rsd[:], func=AF.Sqrt, scale=1.0 / 128.0, bias=1e-5), reads=["rsd"], writes=["rsd"])
                        P.op("dve", lambda e: e.reciprocal(out=rsd[:], in_=rsd[:]), reads=["rsd"], writes=["rsd"])
                        P.op("dve", lambda e: e.tensor_tensor(out=ya2, in0=ya2, in1=rsd[:].unsqueeze(2).to_broadcast([128, NCH * 2, 128]), op=ALU.mult),
                             reads=["ya", "rsd"], writes=["ya"])
                        nw = prm(f"ssd_nw_{l}")
                        for c in range(NCH):
                            for ct in range(2):
                                P.op("pe", lambda e, c=c, ct=ct: e.transpose(out=pb_[:, 0:128], in_=ya[:, c, ct * 128:(ct + 1) * 128], identity=identf),
                                     reads=["ya", "pp"], writes=["pb"])
                                P.op("act", lambda e, c=c, ct=ct: e.activation(out=y[:, ct, c * 128:(c + 1) * 128], in_=pb_[:, 0:128], func=AF.Copy, scale=nw[:, ct:ct + 1]),
                                     reads=["pb", "pp"], writes=["yssd"])
                        out_proj(sb1, 3, y, ["yssd"])
                        P.barrier()

                if "ssd" in mixers:
                    ssd()
                P.barrier()

                def rw():
                    NC = T // 64
                    CB = 1296

                    def V(fn, r, w):
                        P.op("dve", fn, reads=r, writes=w)

                    def A(fn, r, w):
                        P.op("act", fn, reads=r, writes=w)

                    def MM(fn, r, w):
                        P.op("pe", fn, reads=r, writes=w)
                    with ExitStack() as e0:
                        def sb0(name, shape, dt):
                            return e0.enter_context(nc.sbuf_tensor(f"rw{l}_{name}", shape, dt))
                        vfb = sb0("vfb", [128, 2, T], F32)
                        t1 = sb0("t1", [8, T], F32)
                        e00 = e0.enter_context(ExitStack())
                        raw0 = e00.enter_context(nc.sbuf_tensor(f"rw{l}_raw0", [128, HW + T], F32))
                        wv = e00.enter_context(nc.sbuf_tensor(f"rw{l}_wv", [128, KC, 256], BF16))
                        P.op("poolq", lambda e: e.dma_start(out=wv[:], in_=win_d[l, :, :, CB + 512:CB + 768]), writes=["rw_wv"], lane="win")

                        def proj_lerp(raw, wbuf, wkey, col, Mr, mu_ap, dst, dkey):
                            proj_fm(wbuf, wkey, col, Mr,
                                    lambda p_ap, tt, rk: A(lambda e: e.activation(out=raw[0:Mr, HW + tt * 512:HW + (tt + 1) * 512], in_=p_ap, func=AF.Copy), rk, ["rw_raw"]),
                                    lambda p_ap, rk: A(lambda e: e.activation(out=raw[0:Mr, 0:HW], in_=p_ap, func=AF.Copy), rk, ["rw_raw"]))
                            V(lambda e: e.tensor_tensor(out=dst, in0=raw[0:Mr, HW - 1:HW - 1 + T], in1=raw[0:Mr, HW:HW + T], op=ALU.subtract), ["rw_raw"], [dkey])
                            V(lambda e: e.scalar_tensor_tensor(out=dst, in0=dst, scalar=mu_ap, in1=raw[0:Mr, HW:HW + T], op0=ALU.mult, op1=ALU.add),
                              ["rw_raw", dkey, "pp"], [dkey])
                        for c2 in range(2):
                            proj_lerp(raw0, wv, "rw_wv", c2 * 128, 128, prm(f"rw_muv_{l}")[:, c2:c2 + 1], vfb[:, c2, :], f"vfb{c2}")
                        if l == 0:
                            for c2 in range(2):
                                P.op("sp", lambda e, c2=c2: e.dma_start(out=vfirst_d[:, c2, :], in_=vfb[:, c2, :]), reads=[f"vfb{c2}"], writes=[f"vfd{c2}"], lane=f"vf{c2}")
                        else:
                            v1p = prm(f"rw_v1_{l}")
                            for tt in range(NT):
                                pb = tt % 2
                                for c2 in range(2):
                                    MM(lambda e, c2=c2, tt=tt, pb=pb: e.matmul(pj[0:8, pb, :], lhsT=v1p[:, c2 * 8:(c2 + 1) * 8], rhs=vfb[:, c2, tt * 512:(tt + 1) * 512],
                                                                                start=(c2 == 0), stop=(c2 == 1)), ["pp", "vfb0", "vfb1"], [f"pj{pb}"])
                                A(lambda e, tt=tt, pb=pb: e.activation(out=t1[:, tt * 512:(tt + 1) * 512], in_=pj[0:8, pb, :], func=AF.Copy), [f"pj{pb}"], ["rw_t1"])
                        P.barrier()
                        e00.close()
                        for ct in range(2):
                            rw_ct(ct, sb0, vfb, t1, NC, CB, V, A, MM)
                        P.barrier()

                def rw_ct(ct, sb0, vfb, t1, NC, CB, V, A, MM):
                    with ExitStack() as eR:
                        def sbR(name, shape, dt):
                            return eR.enter_context(nc.sbuf_tensor(f"rw{l}_{ct}_{name}", shape, dt))
                        at = sbR("at", [128, T], BF16)
                        bt = sbR("bt", [128, T], BF16)
                        kt = sbR("kt", [128, T], BF16)
                        rt = sbR("rt", [128, T], BF16)
                        bh = sbR("bh", [128, T], BF16)
                        kh = sbR("kh", [128, T], BF16)
                        vb = sbR("vb", [128, T], BF16)
                        gfm = sbR("gfm", [128, T], BF16)
                        bon = sbR("bon", [128, T], F32)
                        PC = sbR("PC", [128, NC], F32)
                        bst = sbR("bst", [128, NC], F32)
                        sm = sbR("sm", [128, 4], F32)
                        identb2 = sbR("identb2", [128, 64], BF16)
                        V(lambda e: e.tensor_copy(out=identb2[:], in_=prm("ident2")), ["pp"], ["identb2"])
                        with ExitStack() as eP:
                            def sbP(name, shape, dt):
                                return eP.enter_context(nc.sbuf_tensor(f"rw{l}_{ct}_{name}", shape, dt))
                            wr = sbP("wr", [128, KC, 128], BF16)
                            wk = sbP("wk", [128, KC, 128], BF16)
                            ws = sbP("ws", [128, KC, 64], BF16)
                            P.op("poolq", lambda e: e.dma_start(out=wr[:], in_=win_d[l, :, :, CB + ct * 128:CB + (ct + 1) * 128]), writes=["rw_wr"], lane="win")
                            P.op("poolq", lambda e: e.dma_start(out=wk[:], in_=win_d[l, :, :, CB + 256 + ct * 128:CB + 256 + (ct + 1) * 128]), writes=["rw_wk"], lane="win")
                            P.op("poolq", lambda e: e.dma_start(out=ws[:], in_=win_d[l, :, :, CB + 768:CB + 832]), writes=["rw_ws"], lane="win")
                            raw = sbP("raw", [128, HW + T], F32)
                            stb = sbP("stb", [32, T], F32)
                            rf = sbP("rf", [128, T], F32)
                            kf = sbP("kf", [128, T], F32)
                            lw = sbP("lw", [128, T], F32)
                            af = sbP("af", [128, T], F32)
                            t2 = sbP("t2", [128, T], F32)
                            t3 = sbP("t3", [128, T], F32)
                            vv = vfb[:, ct, :]
                            vkey = f"vfb{ct}"
                            lw_tmp = t2

                            def proj_lerp(wbuf, wkey, col, Mr, mu_ap, dst, dkey):
                                proj_fm(wbuf, wkey, col, Mr,
                                        lambda p_ap, tt, rk: A(lambda e: e.activation(out=raw[0:Mr, HW + tt * 512:HW + (tt + 1) * 512], in_=p_ap, func=AF.Copy), rk, ["raw"]),
                                        lambda p_ap, rk: A(lambda e: e.activation(out=raw[0:Mr, 0:HW], in_=p_ap, func=AF.Copy), rk, ["raw"]))
                                V(lambda e: e.tensor_tensor(out=dst, in0=raw[0:Mr, HW - 1:HW - 1 + T], in1=raw[0:Mr, HW:HW + T], op=ALU.subtract), ["raw"], [dkey])
                                V(lambda e: e.scalar_tensor_tensor(out=dst, in0=dst, scalar=mu_ap, in1=raw[0:Mr, HW:HW + T], op0=ALU.mult, op1=ALU.add),
                                  ["raw", dkey, "pp"], [dkey])

                            def lowrank(src_rows, src_key, wname, dst, dkey, func, bias_ap):
                                wmat = prm(wname)
                                for tt in range(NT):
                                    pb = tt % 2
                                    MM(lambda e, tt=tt, pb=pb: e.matmul(pj[:, pb, :], lhsT=wmat[0:src_rows, ct * 128:(ct + 1) * 128],
                                                                        rhs=stb[0:src_rows, tt * 512:(tt + 1) * 512], start=True, stop=True),
                                       ["pp", src_key], [f"pj{pb}"])
                                    if bias_ap is None:
                                        A(lambda e, tt=tt, pb=pb: e.activation(out=dst[:, tt * 512:(tt + 1) * 512], in_=pj[:, pb, :], func=func), [f"pj{pb}"], [dkey])
                                    else:
                                        A(lambda e, tt=tt, pb=pb: e.activation(out=dst[:, tt * 512:(tt + 1) * 512], in_=pj[:, pb, :], func=func, bias=bias_ap),
                                          [f"pj{pb}", "pp"], [dkey])
                            proj_lerp(wr, "rw_wr", 0, 128, prm(f"rw_mur_{l}")[:, ct:ct + 1], rf[:], "rf")
                            proj_lerp(wk, "rw_wk", 0, 128, prm(f"rw_muk_{l}")[:, ct:ct + 1], kf[:], "kf")
                            proj_lerp(ws, "rw_ws", 0, 16, prm(f"rw_musw_{l}")[0:16, 0:1], stb[0:16, :], "stb")
                            A(lambda e: e.activation(out=stb[0:16, :], in_=stb[0:16, :], func=AF.Tanh), ["stb"], ["stb"])
                            lowrank(16, "stb", f"rw_w2_{l}", lw, "lw", AF.Sigmoid, prm(f"rw_w0_{l}")[:, ct:ct + 1])
                            V(lambda e: e.tensor_scalar(out=lw[:], in0=lw[:], scalar1=-float(np.exp(-0.5)), scalar2=None, op0=ALU.mult), ["lw"], ["lw"])
                            proj_lerp(ws, "rw_ws", 16, 16, prm(f"rw_musa_{l}")[0:16, 0:1], stb[0:16, :], "stb")
                            lowrank(16, "stb", f"rw_a2_{l}", af, "af", AF.Sigmoid, prm(f"rw_a0_{l}")[:, ct:ct + 1])
                            proj_lerp(ws, "rw_ws", 32, 32, prm(f"rw_musg_{l}")[0:32, 0:1], stb[0:32, :], "stb")
                            A(lambda e: e.activation(out=stb[0:32, :], in_=stb[0:32, :], func=AF.Sigmoid), ["stb"], ["stb"])
                            lowrank(32, "stb", f"rw_g2_{l}", gfm, "gfm", AF.Copy, None)
                            if l > 0:
                                P.op("sp", lambda e: e.dma_start(out=raw[:, 0:T], in_=vfirst_d[:, ct, :]), reads=[f"vfd{ct}", "raw"], writes=["raw"], lane="vfl")
                                v2p = prm(f"rw_v2_{l}")
                                for tt in range(NT):
                                    pb = tt % 2
                                    MM(lambda e, tt=tt, pb=pb: e.matmul(pj[:, pb, :], lhsT=v2p[0:8, ct * 128:(ct + 1) * 128], rhs=t1[0:8, tt * 512:(tt + 1) * 512],
                                                                        start=True, stop=True), ["pp", "rw_t1"], [f"pj{pb}"])
                                    A(lambda e, tt=tt, pb=pb: e.activation(out=t2[:, tt * 512:(tt + 1) * 512], in_=pj[:, pb, :], func=AF.Sigmoid,
                                                                           bias=prm(f"rw_v0_{l}")[:, ct:ct + 1]), [f"pj{pb}", "pp"], ["t2"])
                                V(lambda e: e.tensor_tensor(out=t3[:], in0=raw[:, 0:T], in1=vv, op=ALU.subtract), ["raw", vkey], ["t3"])
                                V(lambda e: e.tensor_tensor(out=t3[:], in0=t3[:], in1=t2[:], op=ALU.mult), ["t3", "t2"], ["t3"])
                                V(lambda e: e.tensor_tensor(out=t2[:], in0=vv, in1=t3[:], op=ALU.add), ["t3", vkey], ["t2"])
                                vuse, vukey = t2, "t2"
                                V(lambda e: e.tensor_copy(out=vb[:], in_=t2[:]), ["t2"], ["vb"])
                            else:
                                V(lambda e: e.tensor_copy(out=vb[:], in_=vv), [vkey], ["vb"])
                            kkc = prm(f"rw_k_k_{l}")[:, ct:ct + 1]
                            kac = prm(f"rw_k_a_{l}")[:, ct:ct + 1]
                            V(lambda e: e.tensor_scalar(out=sm[:, 0:1], in0=kac, scalar1=-1.0, scalar2=1.0, op0=ALU.mult, op1=ALU.add), ["pp"], ["sm"])
                            V(lambda e: e.tensor_scalar(out=raw[:, 0:T], in0=kf[:], scalar1=kkc, scalar2=None, op0=ALU.mult), ["kf", "pp", "raw"], ["raw"])
                            V(lambda e: e.tensor_tensor(out=t3[:], in0=raw[:, 0:T], in1=raw[:, 0:T], op=ALU.mult), ["raw", "vb"], ["t3"])
                            for tt in range(NT):
                                pb = tt % 2
                                MM(lambda e, tt=tt, pb=pb: e.matmul(pj[:, pb, :], lhsT=prm("bones"), rhs=t3[:, tt * 512:(tt + 1) * 512], start=True, stop=True),
                                   ["pp", "t3"], [f"pj{pb}"])
                                A(lambda e, tt=tt, pb=pb: e.activation(out=lw_tmp[:, tt * 512:(tt + 1) * 512], in_=pj[:, pb, :], func=AF.Sqrt), [f"pj{pb}"], ["t2"])
                            V(lambda e: e.tensor_scalar(out=lw_tmp[:], in0=lw_tmp[:], scalar1=1e-12, scalar2=None, op0=ALU.max), ["t2"], ["t2"])
                            V(lambda e: e.reciprocal(out=lw_tmp[:], in_=lw_tmp[:]), ["t2"], ["t2"])
                            V(lambda e: e.tensor_tensor(out=raw[:, 0:T], in0=raw[:, 0:T], in1=lw_tmp[:], op=ALU.mult), ["raw", "t2"], ["raw"])
                            V(lambda e: e.tensor_scalar(out=t3[:], in0=af[:], scalar1=kac, scalar2=sm[:, 0:1], op0=ALU.mult, op1=ALU.add), ["af", "pp", "sm"], ["t3"])
                            V(lambda e: e.tensor_tensor(out=kf[:], in0=kf[:], in1=t3[:], op=ALU.mult), ["kf", "t3"], ["kf"])
                            V(lambda e: e.tensor_tensor(out=t3[:], in0=rf[:], in1=kf[:], op=ALU.mult), ["rf", "kf"], ["t3"])
                            V(lambda e: e.tensor_scalar(out=t3[:], in0=t3[:], scalar1=prm(f"rw_r_k_{l}")[:, ct:ct + 1], scalar2=None, op0=ALU.mult), ["t3", "pp"], ["t3"])
                            for tt in range(NT):
                                pb = tt % 2
                                MM(lambda e, tt=tt, pb=pb: e.matmul(pj[:, pb, :], lhsT=prm("bones"), rhs=t3[:, tt * 512:(tt + 1) * 512], start=True, stop=True),
                                   ["pp", "t3"], [f"pj{pb}"])
                                V(lambda e, tt=tt, pb=pb: e.tensor_tensor(out=bon[:, tt * 512:(tt + 1) * 512], in0=pj[:, pb, :], in1=vb[:, tt * 512:(tt + 1) * 512], op=ALU.mult),
                                  [f"pj{pb}", "vb"], ["bon"])
                            V(lambda e: e.tensor_tensor(out=af[:], in0=af[:], in1=raw[:, 0:T], op=ALU.mult), ["af", "raw"], ["af"])
                            V(lambda e: e.memset(lw_tmp[:], 1.0), ["t2"], ["t2"])
                            V(lambda e: e.tensor_tensor_scan(out=t3[:], data0=lw_tmp[:], data1=lw[:], initial=0.0, op0=ALU.mult, op1=ALU.add), ["t2", "lw"], ["t3"])
                            t33 = t3[:].rearrange("p (c i) -> p c i", i=64)
                            V(lambda e: e.memset(bst[:, 0:1], 0.0), [], ["bst"])
                            V(lambda e: e.tensor_copy(out=bst[:, 1:NC], in_=t33[:, 0:NC - 1, 63]), ["t3"], ["bst"])
                            V(lambda e: e.tensor_tensor(out=t33, in0=t33, in1=bst[:].unsqueeze(2).to_broadcast([128, NC, 64]), op=ALU.subtract), ["t3", "bst"], ["t3"])
                            A(lambda e: e.activation(out=PC[:], in_=t33[:, :, 63], func=AF.Exp), ["t3"], ["PC"])
                            V(lambda e: e.tensor_tensor(out=lw[:], in0=t3[:], in1=lw[:], op=ALU.subtract), ["t3", "lw"], ["lw"])
                            A(lambda e: e.activation(out=lw[:], in_=lw[:], func=AF.Exp), ["lw"], ["lw"])
                            V(lambda e: e.scalar_tensor_tensor(out=at[:], in0=raw[:, 0:T], scalar=-1.0, in1=lw[:], op0=ALU.mult, op1=ALU.mult), ["raw", "lw"], ["at"])
                            A(lambda e: e.activation(out=lw[:], in_=t3[:], func=AF.Exp), ["t3", "lw", "at"], ["lw"])
                            V(lambda e: e.tensor_tensor(out=rt[:], in0=rf[:], in1=lw[:], op=ALU.mult), ["rf", "lw"], ["rt"])
                            A(lambda e: e.activation(out=t3[:], in_=t3[:], func=AF.Exp, scale=-1.0), ["t3", "rt"], ["t3"])
                            PCb = PC[:].unsqueeze(2).to_broadcast([128, NC, 64])
                            V(lambda e: e.tensor_tensor(out=af[:], in0=af[:], in1=t3[:], op=ALU.mult), ["af", "t3"], ["af"])
                            V(lambda e: e.tensor_copy(out=bt[:], in_=af[:]), ["af"], ["bt"])
                            V(lambda e: e.tensor_tensor(out=bh[:].rearrange("p (c i) -> p c i", i=64), in0=af[:].rearrange("p (c i) -> p c i", i=64), in1=PCb, op=ALU.mult),
                              ["af", "PC"], ["bh"])
                            V(lambda e: e.tensor_tensor(out=kf[:], in0=kf[:], in1=t3[:], op=ALU.mult), ["kf", "t3", "bon"], ["kf"])
                            V(lambda e: e.tensor_copy(out=kt[:], in_=kf[:]), ["kf"], ["kt"])
                            V(lambda e: e.tensor_tensor(out=kh[:].rearrange("p (c i) -> p c i", i=64), in0=kf[:].rearrange("p (c i) -> p c i", i=64), in1=PCb, op=ALU.mult),
                              ["kf", "PC"], ["kh"])
                            P.barrier()
                        import os
                        if int(os.environ.get("RW_STAGE", "9")) >= 2:
                            rw_chunks(ct, sbR, at, bt, kt, rt, bh, kh, vb, gfm, bon, PC, identb2, NC, V, A, MM)
                        P.barrier()

                def rw_chunks(ct, sbR, at, bt, kt, rt, bh, kh, vb, gfm, bon, PC, identb2, NC, V, A, MM):
                    with ExitStack() as eC:
                        def sbC(name, shape, dt):
                            return eC.enter_context(nc.sbuf_tensor(f"rwc{l}_{ct}_{name}", shape, dt))

                        def psC(name, shape, dt=F32):
                            return eC.enter_context(nc.psum_tensor(f"rwc{l}_{ct}_{name}", shape, dt))
                        GTb = sbC("GTb", [128, NC, 64], BF16)
                        Jst = sbC("Jst", [128, NC, 64], F32)
                        CoefTb = sbC("CoefTb", [128, NC, 64], BF16)
                        Yc = sbC("Yc", [128, NC, 64], F32)
                        A01 = sbC("A01", [128, 2, 128], F32)
                        V(lambda e: e.memset(A01[:], 0.0), [], ["A01"])
                        A23 = sbC("A23", [128, 3, 64], BF16)
                        PsPt = sbC("PsPt", [128, 2, 128], F32)
                        Tt = sbC("Tt", [128, 128], F32)
                        Ttb = sbC("Ttb", [128, 64], BF16)
                        tmx = sbC("tmx", [128, 4, 64], BF16)
                        TpX = sbC("TpX", [128, 2, 64], BF16)
                        Tppb = sbC("Tppb", [128, 64], BF16)
                        H = sbC("H", [128, 64], F32)
                        Hb = sbC("Hb", [128, 64], BF16)
                        bankA = (pj[:, 0, :], pj[:, 1, :])
                        bankC = (psC("bankC0", [128, 512]), psC("bankC1", [128, 512]))
                        bankJ = (psC("bankJ0", [128, 512]), psC("bankJ1", [128, 512]))
                        ptx = (psC("ptx0", [128, 1024], BF16), psC("ptx1", [128, 1024], BF16))

                        def V2(fn, r, w):
                            for hh_, ps_ in enumerate(HS):
                                P.op("dve", lambda e, hh_=hh_, ps_=ps_: fn(e, hh_, ps_), reads=[k.format(hh=hh_) for k in r], writes=[k.format(hh=hh_) for k in w])

                        def A2(fn, r, w):
                            for hh_, ps_ in enumerate(HS):
                                P.op("act", lambda e, hh_=hh_, ps_=ps_: fn(e, hh_, ps_), reads=[k.format(hh=hh_) for k in r], writes=[k.format(hh=hh_) for k in w])
                        HS = (slice(0, 64), slice(64, 128))

                        def MT(fn, r, w):
                            P.op("pe", fn, reads=[k.format(hh=hh) for k in r], writes=[k.format(hh=hh) for k in w], mode="t64")
                        maskA = prm("maskA")
                        ident2 = prm("ident2")
                        for c in range(NC):
                            cs = slice(c * 64, (c + 1) * 64)
                            for hh, ps_ in enumerate(HS):
                                for k_, (lt_, rh_) in enumerate(((bt, at), (at, bt), (kt, at), (bt, rt), (kt, rt))):
                                    MT(lambda e, ps_=ps_, hh=hh, k_=k_, lt_=lt_, rh_=rh_: e.matmul(bankA[hh][ps_, k_ * 64:(k_ + 1) * 64], lhsT=lt_[ps_, cs], rhs=rh_[ps_, cs],
                                                                                          start=True, stop=True), ["at", "bt", "kt", "rt"], ["pj{hh}"])
                                for k_, X in enumerate(() if os.environ.get("RW_NOTR") else (at, bh, kh, vb)):
                                    MT(lambda e, ps_=ps_, hh=hh, k_=k_, X=X: e.transpose(out=ptx[hh][ps_, k_ * 64:(k_ + 1) * 64], in_=X[ps_, cs], identity=identb2[ps_, :]),
                                       ["at", "bh", "kh", "vb", "identb2"], ["ptx{hh}"])
                            V2(lambda e, hh, ps_: e.tensor_tensor(out=A01[ps_, :, hh * 64:(hh + 1) * 64], in0=bankA[hh][ps_, 0:128].rearrange("p (a b) -> p a b", b=64),
                                                                  in1=maskA[ps_, 0:128].rearrange("p (a b) -> p a b", b=64), op=ALU.mult),
                               ["pj{hh}", "pp"], ["A01"])
                            V2(lambda e, hh, ps_: e.tensor_tensor(out=A23[ps_].rearrange("p a b -> p (a b)"), in0=bankA[hh][ps_, 128:320], in1=maskA[ps_, 128:320], op=ALU.mult),
                               ["pj{hh}", "pp"], ["A23"])
                            A2(lambda e, hh, ps_: e.activation(out=tmx[ps_].rearrange("p a b -> p (a b)"), in_=ptx[hh][ps_, 0:256], func=AF.Copy), ["ptx{hh}"], ["tmx"])
                            V(lambda e: e.tensor_tensor(out=Tt[:], in0=A01[:, 0, :], in1=prm("ident"), op=ALU.add), ["A01", "pp"], ["Tt"])
                            cur_s, cur_t, ckey = A01[:, 1, :], A01[:, 0, :], "A01"
                            for lev in range(0 if os.environ.get("RW_NOINV") else 5):
                                MM(lambda e, cur_s=cur_s, cur_t=cur_t: e.matmul(bankJ[0][:, 128:256], lhsT=cur_t, rhs=cur_s, start=True, stop=True), [ckey], ["bJ0"])
                                MM(lambda e, cur_s=cur_s, cur_t=cur_t: e.matmul(bankJ[0][:, 256:384], lhsT=cur_s, rhs=cur_t, start=True, stop=True), [ckey], ["bJ0"])
                                A(lambda e: e.activation(out=PsPt[:].rearrange("p a b -> p (a b)"), in_=bankJ[0][:, 128:384], func=AF.Copy), ["bJ0"], ["PsPt"])
                                cur_s, cur_t, ckey = PsPt[:, 0, :], PsPt[:, 1, :], "PsPt"
                                MM(lambda e, cur_s=cur_s: e.matmul(bankJ[0][:, 384:512], lhsT=cur_s, rhs=Tt[:], start=True, stop=True), ["PsPt", "Tt"], ["bJ0"])
                                V(lambda e: e.tensor_tensor(out=Tt[:], in0=bankJ[0][:, 384:512], in1=Tt[:], op=ALU.add), ["bJ0", "Tt"], ["Tt"])
                            if int(os.environ.get("RW_STAGE", "9")) == 2:
                                continue
                            V2(lambda e, hh, ps_: e.tensor_copy(out=Ttb[ps_], in_=Tt[ps_, hh * 64:(hh + 1) * 64]), ["Tt"], ["Ttb"])
                            for hh, ps_ in enumerate(HS):
                                MT(lambda e, ps_=ps_, hh=hh: e.matmul(bankC[hh][ps_, 0:64], lhsT=Ttb[ps_], rhs=tmx[ps_, 0, :], start=True, stop=True), ["Ttb", "tmx"], ["bC{hh}"])
                                MT(lambda e, ps_=ps_, hh=hh: e.matmul(bankC[hh][ps_, 64:128], lhsT=A23[ps_, 0, :], rhs=tmx[ps_, 3, :], start=True, stop=True), ["A23", "tmx"], ["bC{hh}"])
                            A2(lambda e, hh, ps_: e.activation(out=TpX[ps_].rearrange("p a b -> p (a b)"), in_=bankC[hh][ps_, 0:128], func=AF.Copy), ["bC{hh}"], ["TpX"])
                            for hh, ps_ in enumerate(HS):
                                MT(lambda e, ps_=ps_, hh=hh: e.matmul(bankC[hh][ps_, 128:192], lhsT=Ttb[ps_], rhs=TpX[ps_, 1, :], start=True, stop=True), ["Ttb", "TpX"], ["bC{hh}"])
                            A2(lambda e, hh, ps_: e.activation(out=Tppb[ps_], in_=bankC[hh][ps_, 128:192], func=AF.Copy), ["bC{hh}"], ["Tppb"])
                            for hh, ps_ in enumerate(HS):
                                MT(lambda e, ps_=ps_, hh=hh: e.matmul(bankC[hh][ps_, 192:256], lhsT=TpX[ps_, 0, :], rhs=tmx[ps_, 1, :], start=True, stop=True), ["TpX", "tmx"], ["bC{hh}"])
                                MT(lambda e, ps_=ps_, hh=hh: e.matmul(bankC[hh][ps_, 256:320], lhsT=TpX[ps_, 0, :], rhs=A23[ps_, 1, :], start=True, stop=True), ["TpX", "A23"], ["bC{hh}"])
                                MT(lambda e, ps_=ps_, hh=hh: e.matmul(bankJ[hh][ps_, 0:64], lhsT=tmx[ps_, 1, :], rhs=Tppb[ps_], start=True, stop=False), ["tmx", "Tppb"], ["bJ{hh}"])
                                MT(lambda e, ps_=ps_, hh=hh: e.matmul(bankJ[hh][ps_, 0:64], lhsT=tmx[ps_, 2, :], rhs=tmx[ps_, 3, :], start=False, stop=True), ["tmx"], ["bJ{hh}"])
                                MT(lambda e, ps_=ps_, hh=hh: e.matmul(bankJ[hh][ps_, 64:128], lhsT=A23[ps_, 1, :], rhs=Tppb[ps_], start=True, stop=False), ["A23", "Tppb"], ["bJ{hh}"])
                                MT(lambda e, ps_=ps_, hh=hh: e.matmul(bankJ[hh][ps_, 64:128], lhsT=A23[ps_, 2, :], rhs=tmx[ps_, 3, :], start=False, stop=True), ["A23", "tmx"], ["bJ{hh}"])
                            V2(lambda e, hh, ps_, c=c: e.scalar_tensor_tensor(out=GTb[ps_, c, :], in0=ident2[ps_], scalar=PC[ps_, c:c + 1], in1=bankC[hh][ps_, 192:256],
                                                                                op0=ALU.mult, op1=ALU.add), ["bC{hh}", "pp", "PC"], [f"GT{c}"])
                            V2(lambda e, hh, ps_, c=c, cs=cs: e.tensor_tensor(out=CoefTb[ps_, c, :], in0=bankC[hh][ps_, 256:320], in1=rt[ps_, cs], op=ALU.add), ["bC{hh}", "rt"], [f"Cf{c}"])
                            A2(lambda e, hh, ps_, c=c: e.activation(out=Jst[ps_, c, :], in_=bankJ[hh][ps_, 0:64], func=AF.Copy), ["bJ{hh}"], [f"J{c}"])
                            A2(lambda e, hh, ps_, c=c: e.activation(out=Yc[ps_, c, :], in_=bankJ[hh][ps_, 64:128], func=AF.Copy), ["bJ{hh}"], [f"Yc{c}"])
                        if int(os.environ.get("RW_STAGE", "9")) <= 3:
                            return
                        V(lambda e: e.memset(H[:], 0.0), [], ["H"])
                        for rnd in range(4):
                            last = rnd == 3
                            A(lambda e: e.activation(out=Hb[:], in_=H[:], func=AF.Copy), ["H"], ["Hb"])
                            for c in range(NC):
                                for hh, ps_ in enumerate(HS):
                                    if last:
                                        MT(lambda e, ps_=ps_, hh=hh, c=c: e.matmul(bankC[hh][ps_, 320:384], lhsT=CoefTb[ps_, c, :], rhs=Hb[ps_], start=True, stop=True),
                                           [f"Cf{c}", "Hb"], ["bC{hh}"])
                                    MT(lambda e, ps_=ps_, hh=hh, c=c: e.matmul(bankC[hh][ps_, 384:448], lhsT=GTb[ps_, c, :], rhs=Hb[ps_], start=True, stop=True),
                                       [f"GT{c}", "Hb"], ["bC{hh}"])
                                if last:
                                    V2(lambda e, hh, ps_, c=c: e.tensor_tensor(out=Yc[ps_, c, :], in0=bankC[hh][ps_, 320:384], in1=Yc[ps_, c, :], op=ALU.add), ["bC{hh}", f"Yc{c}"], [f"Yc{c}"])
                                V2(lambda e, hh, ps_, c=c: e.tensor_tensor(out=H[ps_], in0=bankC[hh][ps_, 384:448], in1=Jst[ps_, c, :], op=ALU.add), ["bC{hh}", f"J{c}"], ["H"])
                                A(lambda e: e.activation(out=Hb[:], in_=H[:], func=AF.Copy), ["H"], ["Hb"])
                            if not last:
                                res, rkey = exchange(sbC, "H", H[:], 64)
                                V(lambda e, res=res: e.tensor_copy(out=H[:], in_=res[:]), [rkey], ["H"])
                        ykeys = [f"Yc{c}" for c in range(NC)]
                        jkeys = [f"J{c}" for c in range(NC)]
                        st1 = sbC("st1", [128, NC], F32)
                        st2 = sbC("st2", [128, NC], F32)
                        ynb = sbC("ynb", [128, NC, 64], BF16)
                        yT = sbC("yT", [128, T], F32)
                        y = sbC("y", [128, 1, T], BF16)
                        V(lambda e: e.tensor_reduce(out=st1[:], in_=Yc[:], axis=AX.X, op=ALU.add), ykeys, ["st1"])
                        V(lambda e: e.tensor_scalar(out=st1[:], in0=st1[:], scalar1=1.0 / 64.0, scalar2=None, op0=ALU.mult), ["st1"], ["st1"])
                        V(lambda e: e.tensor_tensor(out=Yc[:], in0=Yc[:], in1=st1[:].unsqueeze(2).to_broadcast([128, NC, 64]), op=ALU.subtract), ykeys + ["st1"], ykeys)
                        V(lambda e: e.tensor_tensor(out=Jst[:], in0=Yc[:], in1=Yc[:], op=ALU.mult), ykeys + jkeys, jkeys)
                        V(lambda e: e.tensor_reduce(out=st2[:], in_=Jst[:], axis=AX.X, op=ALU.add), jkeys, ["st2"])
                        A(lambda e: e.activation(out=st2[:], in_=st2[:], func=AF.Sqrt, scale=1.0 / 64.0, bias=64e-5), ["st2"], ["st2"])
                        V(lambda e: e.reciprocal(out=st2[:], in_=st2[:]), ["st2"], ["st2"])
                        V(lambda e: e.tensor_tensor(out=ynb[:], in0=Yc[:], in1=st2[:].unsqueeze(2).to_broadcast([128, NC, 64]), op=ALU.mult), ykeys + ["st2"], ["ynb"])
                        gnw = prm(f"rw_gn_w_{l}")[:, ct:ct + 1]
                        gnb = prm(f"rw_gn_b_{l}")[:, ct:ct + 1]
                        for c0 in range(0, NC, 4):
                            for c in range(c0, c0 + 4):
                                for hh, ps_ in enumerate(HS):
                                    MT(lambda e, ps_=ps_, hh=hh, c=c, c0=c0: e.transpose(out=ptx[hh][ps_, (c - c0) * 64:(c - c0 + 1) * 64], in_=ynb[ps_, c, :], identity=identb2[ps_, :]),
                                       ["ynb", "identb2"], ["ptx{hh}"])
                            A2(lambda e, hh, ps_, c0=c0: e.activation(out=yT[ps_, c0 * 64:(c0 + 4) * 64], in_=ptx[hh][ps_, 0:256], func=AF.Identity, scale=gnw[ps_], bias=gnb[ps_]),
                               ["ptx{hh}", "pp"], ["yT"])
                        V(lambda e: e.tensor_tensor(out=yT[:], in0=yT[:], in1=bon[:], op=ALU.add), ["yT", "bon"], ["yT"])
                        V(lambda e: e.tensor_tensor(out=y[:, 0, :], in0=yT[:], in1=gfm[:], op=ALU.mult), ["yT", "gfm"], [f"yrw{ct}"])
                        out_proj(sbC, 20 + ct, y, [f"yrw{ct}"], cc0=4 + ct, ncc=1)
                        P.barrier()

                if "rw" in mixers:
                    rw()
                P.barrier()

        for l in range(L):
            with ExitStack() as ex:
                x = ex.enter_context(nc.sbuf_tensor(f"x_{l}", [128, KC, T], F32))
                load_x(xT_d if l == 0 else xd, l == 0)
                if l > 0:
                    ffn(l - 1, 1, f"n2_{l - 1}")
                ffn(l, 0, f"n1_{l}")
                sq = ex.enter_context(nc.sbuf_tensor(f"nsq_{l}", [128, 2, 512], BF16))
                rs = ex.enter_context(nc.sbuf_tensor(f"nrs_{l}", [128, 2, 512], F32))
                ssp = ex.enter_context(nc.psum_tensor(f"nssp_{l}", [128, 512], F32))
                rmsnorm(ex, (sq, rs, ssp), f"nm_{l}", list(range(NT)),
                        lambda kc, tt: xn2[:, kc, HW + tt * 512:HW + (tt + 1) * 512],
                        lambda kc, tt: f"xn2_{kc}_{tt}")
                store_x()
            mixer_phase(l)
            P.barrier()
        xfin_es = es.enter_context(ExitStack())
        x = xfin_es.enter_context(nc.sbuf_tensor("x_fin", [128, KC, T], F32))
        load_x(xd, False)
        ffn(L - 1, 1, f"n2_{L - 1}")

        with ExitStack() as es2:
            ob = es2.enter_context(nc.sbuf_tensor("o_ob", [128, 2, KC, 512], F32))
            sq = es2.enter_context(nc.sbuf_tensor("o_sq", [128, 2, 512], BF16))
            rs = es2.enter_context(nc.sbuf_tensor("o_rs", [128, 2, 512], F32))
            ssp = es2.enter_context(nc.psum_tensor("o_ssp", [128, 512], F32))
            for tt in range(NT):
                b = tt % 2
                rmsnorm(es2, (sq, rs, ssp), "nf", [tt], lambda kc, t2: ob[:, b, kc, :], lambda kc, t2: f"ob{b}_{kc}")
                P.op("sp", lambda e, b=b, tt=tt: e.dma_start(out=out_d[:, :, tt * 512:(tt + 1) * 512], in_=ob[:, b, :, :]),
                     reads=[f"ob{b}_{kc}" for kc in range(KC)], writes=[f"out{tt}"], lane=f"out{b}")
            P.final_wait("sp")
        print("instr counts", P.cnt, "waits", P.nwaits)
    return nc


def prep_weights(inp, L):
    wgu = np.empty((L, 2, NJ, 128, 2, KC, 128), np.float32)
    wd = np.empty((L, 2, KC, 128, NJ, 128), np.float32)
    for f, pre in enumerate(("ffn1", "ffn2")):
        for gi, nm in enumerate(("w_gate", "w_up")):
            w = np.asarray(inp[f"{pre}_{nm}"], np.float32)[:L]
            w = w.reshape(L, KC, 128, NJ, 128)
            wgu[:, f, :, :, gi] = w.transpose(0, 3, 2, 1, 4)
        w = np.asarray(inp[f"{pre}_w_down"], np.float32)[:L]
        w = w.reshape(L, NJ, 128, KC, 128)
        wd[:, f] = w.transpose(0, 3, 2, 1, 4)
    win = np.ascontiguousarray(np.asarray(inp["w_in"], np.float32)[:L].reshape(L, KC, 128, NIN).transpose(0, 2, 1, 3))
    wout = np.ascontiguousarray(np.asarray(inp["w_out"], np.float32)[:L].reshape(L, KC, 128, D).transpose(0, 2, 1, 3))
    return {"wgu": wgu.reshape(L * 2 * NJ, 128, 2 * KC * 128), "wd": wd.reshape(L * 2 * KC, 128, NJ * 128),
            "win": win, "wout": wout}


def run(inp, T, L, mixers=()):
    x = np.asarray(inp["x"], np.float32)
    B, S, _ = x.shape
    nseg = S // T
    ncores = B * nseg
    assert ncores == 8
    wts = prep_weights(inp, L)
    in_maps = []
    offs = None
    for c in range(ncores):
        b, s = divmod(c, nseg)
        xs = x[b, s * T:(s + 1) * T, :]
        xT = np.ascontiguousarray(xs.T.reshape(KC, 128, T).transpose(1, 0, 2))
        ppa, ppla, offs = pack_small(inp, L, s)
        m = {"xT": xT, "pp": ppa, "ppl": ppla}
        m.update(wts)
        in_maps.append(m)
    nc = build(T, L, offs, in_maps[0]["pp"].shape[1], in_maps[0]["ppl"].shape[2], mixers)
    print("pp cols", in_maps[0]["pp"].shape, in_maps[0]["ppl"].shape)
    res = run_bass_kernel_spmd(nc, in_maps, core_ids=list(range(ncores)))
    out = np.empty((B, S, D), np.float32)
    for c in range(ncores):
        b, s = divmod(c, nseg)
        oT = np.asarray(res.results[c]["outT"], np.float32)
        out[b, s * T:(s + 1) * T, :] = oT.transpose(2, 1, 0).reshape(T, D)
    return out


def kernel(**inputs):
    return run(inputs, 2048, L_FULL, mixers=("gla", "lru", "rw", "ssd"))
```

```python
import os
import numpy as np
from contextlib import ExitStack
import concourse.bass as bass
import concourse.mybir as mybir
from concourse.bass_utils import run_bass_kernel_spmd

F32 = mybir.dt.float32
BF16 = mybir.dt.bfloat16
AF = mybir.ActivationFunctionType
ALU = mybir.AluOpType
AX = mybir.AxisListType

D = 1024
KC = 8
DFF = 2816
NJ = 22
NIN = 3156
HW = 4
L_FULL = 4
NORM_EPS = 1e-6


class Prog:
    def __init__(self, nc, es):
        self.nc = nc
        self.es = es
        self.streams = {
            "pe": (nc.tensor, 1, 20000),
            "dve": (nc.vector, 1, 20000),
            "act": (nc.scalar, 1, 20000),
            "pool": (nc.gpsimd, 1, 20000),
            "sp": (nc.sync, 16, 1500),
            "poolq": (nc.gpsimd, 16, 1500),
            "cc": (nc.gpsimd, 1, 20000),
        }
        self.issuer = {"pe": "pe", "dve": "dve", "act": "act", "pool": "pool", "sp": "sp",
                       "poolq": "pool", "cc": "pool"}
        self.sems = {s: [] for s in self.streams}
        self.cnt = {s: 0 for s in self.streams}
        self.waited = {}
        self.lastw = {}
        self.readers = {}
        self.same_engine_sync = os.environ.get("KSYNC", "1") == "1"
        self.nwaits = 0

    def _sem(self, stream, epoch):
        lst = self.sems[stream]
        while len(lst) <= epoch:
            lst.append(self.es.enter_context(self.nc.semaphore(f"s_{stream}_{len(lst)}")))
        return lst[epoch]

    def _wait(self, issuer, stream, seq):
        key = (issuer, stream)
        if self.waited.get(key, 0) >= seq:
            return
        self.waited[key] = seq
        eng, inc, cap = self.streams[stream]
        epoch = (seq - 1) // cap
        val = ((seq - 1) % cap + 1) * inc
        self.streams[issuer][0].wait_ge(self._sem(stream, epoch), val)
        self.nwaits += 1

    def op(self, stream, fn, reads=(), writes=(), lane=None, mode="full"):
        if stream == "pe":
            if getattr(self, "pe_mode", "full") != mode and self.cnt["pe"] > 0:
                self._wait("pe", "pe", self.cnt["pe"])
            self.pe_mode = mode
        if lane is not None:
            base = stream
            stream = f"{base}.{lane}"
            if stream not in self.streams:
                self.streams[stream] = self.streams[base]
                self.issuer[stream] = self.issuer[base]
                self.sems[stream] = []
                self.cnt[stream] = 0
        issuer = self.issuer[stream]
        deps = set()
        for k in reads:
            if k in self.lastw:
                deps.add(self.lastw[k])
        for k in writes:
            if k in self.lastw:
                deps.add(self.lastw[k])
            for r in self.readers.get(k, ()):
                deps.add(r)
        for (s, q) in sorted(deps):
            if s == stream and (stream == "pe" or not self.same_engine_sync):
                continue
            if s == stream and stream in ("sp", "poolq"):
                pass
            self._wait(issuer, s, q)
        eng, inc, cap = self.streams[stream]
        ins = fn(eng)
        self.cnt[stream] += 1
        seq = self.cnt[stream]
        ins.then_inc(self._sem(stream, (seq - 1) // cap), inc)
        me = (stream, seq)
        for k in reads:
            self.readers.setdefault(k, []).append(me)
        for k in writes:
            self.lastw[k] = me
            self.readers[k] = []
        return me

    def barrier(self):
        for issuer in ("pe", "dve", "act", "pool", "sp"):
            for s in list(self.streams):
                if self.cnt[s] > 0:
                    self._wait(issuer, s, self.cnt[s])

    def final_wait(self, issuer="sp"):
        for s in list(self.streams):
            if self.cnt[s] > 0:
                self._wait(issuer, s, self.cnt[s])


def chan_pp(v, ntile):
    return np.ascontiguousarray(np.asarray(v, np.float32).reshape(ntile, 128).T)


def pack_small(inp, L, rank):
    cols = []
    offs = {}
    pos = [0]
    lcols = [[] for _ in range(L)]
    lpos = [0] * L

    def add(name, a, layer=None):
        a = np.asarray(a, np.float32)
        assert a.ndim == 2 and a.shape[0] <= 128, (name, a.shape)
        buf = np.zeros((128, a.shape[1]), np.float32)
        buf[: a.shape[0]] = a
        if layer is None:
            offs[name] = ("C", pos[0], a.shape[1])
            pos[0] += a.shape[1]
            cols.append(buf)
        else:
            offs[name] = ("L", lpos[layer], a.shape[1])
            lpos[layer] += a.shape[1]
            lcols[layer].append(buf)

    ident = np.eye(128, dtype=np.float32)
    add("ident", ident)
    add("ones", np.ones((128, 128), np.float32))
    selprev = np.zeros((128, 4), np.float32)
    if rank > 0:
        selprev[:, rank - 1] = 1.0
    add("selprev", selprev)
    for l in range(L):
        add(f"n1_{l}", chan_pp(inp["ffn1_norm"][l], 8))
        add(f"nm_{l}", chan_pp(inp["mix_norm"][l], 8))
        add(f"n2_{l}", chan_pp(inp["ffn2_norm"][l], 8))
    add("nf", chan_pp(inp["final_norm"], 8))
    jj = np.arange(128)
    causal = (jj[:, None] <= jj[None, :]).astype(np.float32)
    add("mask4", np.tile(causal, (1, 4)))
    hm = np.zeros((128, 4), np.float32)
    for h in range(4):
        hm[h * 32:(h + 1) * 32, h] = 1.0
    add("hm", hm)
    add("bm", np.repeat(hm, 64, axis=1))
    i64 = np.arange(128) % 64
    j64 = np.arange(64)
    m_lt = (i64[:, None] < j64[None, :]).astype(np.float32)
    m_gt = (j64[None, :] < i64[:, None]).astype(np.float32)
    m_le = (i64[:, None] <= j64[None, :]).astype(np.float32)
    add("maskA", np.concatenate([m_lt, m_gt, m_lt, m_le, m_le], axis=1))
    add("ident2", (i64[:, None] == j64[None, :]).astype(np.float32))
    p128 = np.arange(128)
    add("bones", (p128[:, None] // 64 == p128[None, :] // 64).astype(np.float32))
    for l in range(L):
        mu = np.asarray(inp["rw_mu"][l], np.float32)
        add(f"rw_mur_{l}", chan_pp(mu[0:256], 2), layer=l)
        add(f"rw_muk_{l}", chan_pp(mu[256:512], 2), layer=l)
        add(f"rw_muv_{l}", chan_pp(mu[512:768], 2), layer=l)
        add(f"rw_musw_{l}", mu[768:784][:, None], layer=l)
        add(f"rw_musa_{l}", mu[784:800][:, None], layer=l)
        add(f"rw_musg_{l}", mu[800:832][:, None], layer=l)
        rwn = {"w0": inp["rw_w0"], "a0": inp["rw_a0"], "k_k": inp["rw_k_k"], "k_a": inp["rw_k_a"], "gn_w": inp["rw_gn_w"], "gn_b": inp["rw_gn_b"]}
        for nm in ("w0", "a0", "k_k", "k_a", "gn_w", "gn_b"):
            add(f"rw_{nm}_{l}", chan_pp(rwn[nm][l], 2), layer=l)
        add(f"rw_r_k_{l}", chan_pp(np.asarray(inp["rw_r_k"][l], np.float32).reshape(256), 2), layer=l)
        add(f"rw_w2_{l}", np.asarray(inp["rw_w2"][l], np.float32), layer=l)
        add(f"rw_a2_{l}", np.asarray(inp["rw_a2"][l], np.float32), layer=l)
        add(f"rw_g2_{l}", np.asarray(inp["rw_g2"][l], np.float32), layer=l)
        if l > 0:
            add(f"rw_v0_{l}", chan_pp(inp["rw_v0"][l - 1], 2), layer=l)
            v1 = np.asarray(inp["rw_v1"][l - 1], np.float32)
            add(f"rw_v1_{l}", np.concatenate([v1[0:128], v1[128:256]], axis=1), layer=l)
            add(f"rw_v2_{l}", np.asarray(inp["rw_v2"][l - 1], np.float32), layer=l)
        else:
            add(f"rw_v0_{l}", np.zeros((128, 2), np.float32), layer=l)
            add(f"rw_v1_{l}", np.zeros((128, 16), np.float32), layer=l)
            add(f"rw_v2_{l}", np.zeros((8, 256), np.float32), layer=l)
    add("utri", causal)
    add("negmask", (causal - 1.0) * 30000.0)
    for l in range(L):
        cw = np.asarray(inp["ssd_conv_w"][l], np.float32)
        add(f"ssd_cw_{l}", np.concatenate([cw[:, ct * 128:(ct + 1) * 128].T for ct in range(6)], axis=1), layer=l)
        add(f"ssd_cb_{l}", chan_pp(inp["ssd_conv_b"][l], 6), layer=l)
        add(f"ssd_dtb_{l}", np.tile(np.asarray(inp["ssd_dt_bias"][l], np.float32)[None, :], (128, 1)), layer=l)
        add(f"ssd_alog_{l}", np.tile(np.asarray(inp["ssd_a_log"][l], np.float32)[None, :], (128, 1)), layer=l)
        add(f"ssd_d_{l}", np.tile(np.asarray(inp["ssd_d"][l], np.float32)[None, :], (128, 1)), layer=l)
        add(f"ssd_nw_{l}", chan_pp(inp["ssd_norm"][l], 2), layer=l)
    for l in range(L):
        add(f"gla_aup_{l}", np.asarray(inp["gla_alpha_up"][l], np.float32), layer=l)
        add(f"gla_ab_{l}", chan_pp(inp["gla_alpha_bias"][l], 1), layer=l)
        add(f"gla_nw_{l}", np.tile(np.asarray(inp["gla_norm"][l], np.float32)[None, :], (128, 1)), layer=l)
    for l in range(L):
        cw = np.asarray(inp["lru_conv_w"][l], np.float32)
        add(f"lru_cw_{l}", np.concatenate([cw[:, ct * 128:(ct + 1) * 128].T for ct in range(2)], axis=1), layer=l)
        add(f"lru_cb_{l}", chan_pp(inp["lru_conv_b"][l], 2), layer=l)
        add(f"lru_ba_{l}", chan_pp(inp["lru_b_a"][l], 2), layer=l)
        add(f"lru_bx_{l}", chan_pp(inp["lru_b_x"][l], 2), layer=l)
        add(f"lru_lam_{l}", chan_pp(inp["lru_lambda"][l], 2), layer=l)
        for nm, key in (("wa", "lru_w_a"), ("wx", "lru_w_x")):
            w = np.asarray(inp[key][l], np.float32)
            for ct in range(2):
                bd = np.zeros((128, 128), np.float32)
                for nn in range(2):
                    bd[nn * 64:(nn + 1) * 64, nn * 64:(nn + 1) * 64] = w[2 * ct + nn]
                add(f"lru_{nm}_{l}_{ct}", bd, layer=l)
    assert len(set(lpos)) == 1, lpos
    ppl = np.stack([np.concatenate(c, axis=1) for c in lcols], axis=0)
    return np.concatenate(cols, axis=1), ppl, offs


def build(T, L, offs, npp, nppl, mixers=()):
    nc = bass.Bass("TRN2", target_bir_lowering=False)
    NT = T // 512
    FG = min(T, 1024)
    xT_d = nc.dram_tensor("xT", [128, KC, T], F32, kind="ExternalInput").ap()
    out_d = nc.dram_tensor("outT", [128, KC, T], F32, kind="ExternalOutput").ap()
    pp_d = nc.dram_tensor("pp", [128, npp], F32, kind="ExternalInput").ap()
    ppl_d = nc.dram_tensor("ppl", [L, 128, nppl], F32, kind="ExternalInput").ap()
    wgu_d = nc.dram_tensor("wgu", [L * 2 * NJ, 128, 2 * KC * 128], F32, kind="ExternalInput").ap()
    wd_d = nc.dram_tensor("wd", [L * 2 * KC, 128, NJ * 128], F32, kind="ExternalInput").ap()
    win_d = nc.dram_tensor("win", [L, 128, KC, NIN], F32, kind="ExternalInput").ap()
    wout_d = nc.dram_tensor("wout", [L, 128, KC, D], F32, kind="ExternalInput").ap()
    uid = [0]

    with ExitStack() as es:
        P = Prog(nc, es)

        def sb(name, shape, dt):
            return es.enter_context(nc.sbuf_tensor("s_" + name, shape, dt))

        def ps(name, shape, dt=F32):
            return es.enter_context(nc.psum_tensor(name, shape, dt))

        xd = nc.dram_tensor("xd_scratch", [128, KC, T], F32, kind="Internal").ap()
        vfirst_d = nc.dram_tensor("vfirst_scratch", [128, 2, T], F32, kind="Internal").ap()
        xn2 = sb("xn2", [128, KC, HW + T], BF16)
        x = None
        pp = sb("pp", [128, npp], F32)
        ppl = sb("ppl", [128, nppl], F32)
        onesb = sb("onesb", [128, 128], BF16)

        def prm(name):
            kind, o, w = offs[name]
            return (pp if kind == "C" else ppl)[:, o:o + w]

        P.op("sp", lambda e: e.dma_start(out=pp[:], in_=pp_d[:, :]), writes=["pp"])
        P.barrier()

        def load_x(src, first):
            for kc in range(KC):
                P.op("sp", lambda e, kc=kc: e.dma_start(out=x[:, kc, :], in_=src[:, kc, :]),
                     reads=([] if first else [f"xd{kc}_{t}" for t in range(NT)]), writes=[f"x{kc}_{t}" for t in range(NT)])
            P.barrier()

        def store_x():
            for kc in range(KC):
                P.op("sp", lambda e, kc=kc: e.dma_start(out=xd[:, kc, :], in_=x[:, kc, :]),
                     reads=[f"x{kc}_{t}" for t in range(NT)], writes=[f"xd{kc}_{t}" for t in range(NT)])
            P.barrier()
        P.op("dve", lambda e: e.tensor_copy(out=onesb[:], in_=prm("ones")), reads=["pp"], writes=["onesb"])

        def rmsnorm(es2, pool, wname, tok_tiles, dst_fn, dst_key_fn):
            sq, rs, ssp = pool
            for i, tt in enumerate(tok_tiles):
                c0 = tt * 512
                for kc in range(KC):
                    b = (i * KC + kc) % 2
                    P.op("act", lambda e, kc=kc, b=b: e.activation(out=sq[:, b, :], in_=x[:, kc, c0:c0 + 512],
                                                                    func=AF.Square),
                         reads=[f"x{kc}_{tt}"], writes=[f"sq{b}"])
                    P.op("pe", lambda e, kc=kc, b=b: e.matmul(ssp[:, :], lhsT=onesb[:, :], rhs=sq[:, b, :],
                                                               start=(kc == 0), stop=(kc == KC - 1)),
                         reads=[f"sq{b}", "onesb"], writes=["ssp"])
                P.op("act", lambda e: e.activation(out=rs[:, 0, :], in_=ssp[:, :], func=AF.Sqrt,
                                                   scale=1.0 / D, bias=NORM_EPS),
                     reads=["ssp"], writes=["rs0"])
                P.op("dve", lambda e: e.reciprocal(out=rs[:, 1, :], in_=rs[:, 0, :]), reads=["rs0"], writes=["rs1"])
                for kc in range(KC):
                    P.op("dve", lambda e, kc=kc: e.scalar_tensor_tensor(
                        out=dst_fn(kc, tt), in0=x[:, kc, c0:c0 + 512], scalar=prm(wname)[:, kc:kc + 1],
                        in1=rs[:, 1, :], op0=ALU.mult, op1=ALU.mult),
                        reads=[f"x{kc}_{tt}", "rs1", "pp"], writes=[dst_key_fn(kc, tt)])

        def ffn(l, f, wname):
            with ExitStack() as es2:
                def sb2(name, shape, dt):
                    return es2.enter_context(nc.sbuf_tensor(f"{name}_{l}_{f}", shape, dt))

                def ps2(name, shape, dt=F32):
                    return es2.enter_context(nc.psum_tensor(f"{name}_{l}_{f}", shape, dt))
                xn = sb2("f_xn", [128, KC, FG], BF16)
                h = sb2("f_h", [128, NJ, FG], BF16)
                wgu = sb2("f_wgu", [128, 2, 2 * KC * 128], BF16)
                wd = sb2("f_wd", [128, 2, NJ * 128], BF16)
                sq = sb2("f_sq", [128, 2, 512], BF16)
                rs = sb2("f_rs", [128, 2, 512], F32)
                sg = sb2("f_sg", [128, 2, 512], F32)
                ssp = ps2("f_ssp", [128, 512])
                pg = ps2("f_pg", [128, 2, 512])
                pu = ps2("f_pu", [128, 2, 512])
                pd = ps2("f_pd", [128, 2, 512])
                nsub = FG // 512
                for g in range(T // FG):
                    tiles = [g * nsub + s for s in range(nsub)]
                    rmsnorm(es2, (sq, rs, ssp), wname, tiles,
                            lambda kc, tt: xn[:, kc, (tt - g * nsub) * 512:(tt - g * nsub + 1) * 512],
                            lambda kc, tt: f"xn{kc}_{tt - g * nsub}")
                    cnt = 0
                    for j in range(NJ):
                        wb = j % 2
                        row = (l * 2 + f) * NJ + j
                        P.op("poolq", lambda e, wb=wb, row=row: e.dma_start(out=wgu[:, wb, :], in_=wgu_d[row, :, :]),
                             writes=[f"wgu{wb}"], lane=f"wgu{wb}")
                        for s in range(nsub):
                            pb = cnt % 2
                            cnt += 1
                            for gi, pt in ((0, pg), (1, pu)):
                                for kc in range(KC):
                                    o = (gi * KC + kc) * 128
                                    P.op("pe", lambda e, pt=pt, o=o, kc=kc, s=s, pb=pb, wb=wb: e.matmul(
                                        pt[:, pb, :], lhsT=wgu[:, wb, o:o + 128], rhs=xn[:, kc, s * 512:(s + 1) * 512],
                                        start=(kc == 0), stop=(kc == KC - 1)),
                                        reads=[f"wgu{wb}", f"xn{kc}_{s}"], writes=[f"p{gi}_{pb}"])
                            P.op("act", lambda e, pb=pb: e.activation(out=sg[:, pb, :], in_=pg[:, pb, :], func=AF.Silu),
                                 reads=[f"p0_{pb}"], writes=[f"sg{pb}"])
                            P.op("dve", lambda e, pb=pb, j=j, s=s: e.tensor_tensor(
                                out=h[:, j, s * 512:(s + 1) * 512], in0=pu[:, pb, :], in1=sg[:, pb, :], op=ALU.mult),
                                reads=[f"p1_{pb}", f"sg{pb}"], writes=[f"h{j}_{s}"])
                    cnt = 0
                    for m in range(KC):
                        wb = m % 2
                        row = (l * 2 + f) * KC + m
                        P.op("poolq", lambda e, wb=wb, row=row: e.dma_start(out=wd[:, wb, :], in_=wd_d[row, :, :]),
                             writes=[f"wd{wb}"], lane=f"wd{wb}")
                        for s in range(nsub):
                            pb = cnt % 2
                            cnt += 1
                            tt = g * nsub + s
                            for j in range(NJ):
                                P.op("pe", lambda e, j=j, s=s, pb=pb, wb=wb: e.matmul(
                                    pd[:, pb, :], lhsT=wd[:, wb, j * 128:(j + 1) * 128], rhs=h[:, j, s * 512:(s + 1) * 512],
                                    start=(j == 0), stop=(j == NJ - 1)),
                                    reads=[f"wd{wb}", f"h{j}_{s}"], writes=[f"pd{pb}"])
                            P.op("dve", lambda e, m=m, pb=pb, tt=tt: e.scalar_tensor_tensor(
                                out=x[:, m, tt * 512:(tt + 1) * 512], in0=pd[:, pb, :], scalar=0.5,
                                in1=x[:, m, tt * 512:(tt + 1) * 512], op0=ALU.mult, op1=ALU.add),
                                reads=[f"pd{pb}", f"x{m}_{tt}"], writes=[f"x{m}_{tt}"])
                P.barrier()


        def exchange(sbm, tag, src_ap, W):
            uid[0] += 1
            u = uid[0]
            bounce = nc.dram_tensor(f"bnc_{u}", [128, W], F32, kind="Internal").ap()
            gath = nc.dram_tensor(f"gth_{u}", [512, W], F32, kind="Internal").ap()
            cache = sbm.__dict__.setdefault("xcache", {})
            if W not in cache:
                cache[W] = (sbm(f"hg_{u}", [128, 4, W], F32), sbm(f"hr_{u}", [128, W], F32), u)
            hg, res, u0 = cache[W]
            P.op("poolq", lambda e: e.dma_start(out=bounce[:, :], in_=src_ap, allow_slow_non_contiguous=True), reads=[tag], writes=[f"bnc{u}"])
            P.op("cc", lambda e: e.collective_compute("AllGather", ALU.bypass, replica_groups=[[0, 1, 2, 3], [4, 5, 6, 7]],
                                                      ins=[bounce[:, :]], outs=[gath[:, :]]),
                 reads=[f"bnc{u}"], writes=[f"gth{u}"])
            P.op("poolq", lambda e: e.dma_start(out=hg[:], in_=gath.rearrange("(r p) c -> p r c", p=128), allow_slow_non_contiguous=True),
                 reads=[f"gth{u}"], writes=[f"hg{u0}"])
            sel = prm("selprev")
            P.op("dve", lambda e: e.tensor_scalar(out=res[:], in0=hg[:, 0, :], scalar1=sel[:, 0:1], scalar2=None, op0=ALU.mult),
                 reads=[f"hg{u0}", "pp"], writes=[f"hr{u0}"])
            for j in range(1, 4):
                P.op("dve", lambda e, j=j: e.scalar_tensor_tensor(out=res[:], in0=hg[:, j, :], scalar=sel[:, j:j + 1], in1=res[:],
                                                                   op0=ALU.mult, op1=ALU.add),
                     reads=[f"hg{u0}", f"hr{u0}", "pp"], writes=[f"hr{u0}"])
            return res, f"hr{u0}"

        def mixer_phase(l):
            P.op("sp", lambda e: e.dma_start(out=ppl[:], in_=ppl_d[l, :, :]), writes=["pp"], lane="ppl")
            with ExitStack() as esm:
                def sbm(name, shape, dt):
                    return esm.enter_context(nc.sbuf_tensor(f"m{l}_{name}", shape, dt))

                def psm(name, shape, dt=F32):
                    return esm.enter_context(nc.psum_tensor(f"m{l}_{name}", shape, dt))
                pj = psm("pj", [128, 2, 512])
                po = pj
                hst = sbm("hst", [128, KC, HW], F32)
                P.op("dve", lambda e: e.tensor_copy(out=hst[:], in_=xn2[:, :, T:T + HW]),
                     reads=[f"xn2_{kc}_{NT - 1}" for kc in range(KC)], writes=["hst"])
                hres, hkey = exchange(sbm, "hst", hst[:].rearrange("p a b -> p (a b)"), KC * HW)
                P.op("dve", lambda e: e.tensor_copy(out=xn2[:, :, 0:HW], in_=hres[:].rearrange("p (a b) -> p a b", b=HW)),
                     reads=[hkey], writes=["xn2_halo"])
                xn2_keys = [f"xn2_{kc}_{tt}" for kc in range(KC) for tt in range(NT)]

                pcount = [0]

                def proj_fm(wbuf, wkey, col, M, evac, halo_evac=None):
                    for tt in range(NT):
                        pb = pcount[0] % 2
                        pcount[0] += 1
                        for kc in range(KC):
                            P.op("pe", lambda e, kc=kc, pb=pb, tt=tt: e.matmul(
                                pj[0:M, pb, :], lhsT=wbuf[:, kc, col:col + M], rhs=xn2[:, kc, HW + tt * 512:HW + (tt + 1) * 512],
                                start=(kc == 0), stop=(kc == KC - 1)),
                                reads=[wkey, f"xn2_{kc}_{tt}"], writes=[f"pj{pb}"])
                        evac(pj[0:M, pb, :], tt, [f"pj{pb}"])
                    if halo_evac is not None:
                        pb = pcount[0] % 2
                        pcount[0] += 1
                        for kc in range(KC):
                            P.op("pe", lambda e, kc=kc, pb=pb: e.matmul(
                                pj[0:M, pb, 0:HW], lhsT=wbuf[:, kc, col:col + M], rhs=xn2[:, kc, 0:HW],
                                start=(kc == 0), stop=(kc == KC - 1)),
                                reads=[wkey, "xn2_halo"], writes=[f"pj{pb}"])
                        halo_evac(pj[0:M, pb, 0:HW], [f"pj{pb}"])

                def load_win(sbx, name, c0, n):
                    wb = sbx(name, [128, KC, n], BF16)
                    P.op("poolq", lambda e: e.dma_start(out=wb[:], in_=win_d[l, :, :, c0:c0 + n]), writes=[name], lane="win")
                    return wb

                def out_proj(sbx, mi, y, ykeys, cc0=None, ncc=2):
                    if cc0 is None:
                        cc0 = 2 * mi
                    wo = sbx(f"wo{mi}", [128, ncc, D], BF16)
                    ost = sbx(f"ost{mi}", [128, 2, 512], F32)
                    P.op("poolq", lambda e: e.dma_start(out=wo[:], in_=wout_d[l, :, cc0:cc0 + ncc, :]), writes=[f"wo{mi}"], lane="wo")
                    cnt = 0
                    for dm in range(KC):
                        for tt in range(NT):
                            pb = cnt % 2
                            cnt += 1
                            for c2 in range(ncc):
                                P.op("pe", lambda e, c2=c2, dm=dm, tt=tt, pb=pb: e.matmul(
                                    po[:, pb, :], lhsT=wo[:, c2, dm * 128:(dm + 1) * 128], rhs=y[:, c2, tt * 512:(tt + 1) * 512],
                                    start=(c2 == 0), stop=(c2 == ncc - 1)),
                                    reads=[f"wo{mi}"] + ykeys, writes=[f"pj{pb}"])
                            P.op("act", lambda e, pb=pb: e.activation(out=ost[:, pb, :], in_=po[:, pb, :], func=AF.Copy),
                                 reads=[f"pj{pb}"], writes=[f"ost{mi}_{pb}"])
                            P.op("poolq", lambda e, dm=dm, tt=tt, pb=pb: e.dma_start(out=xd[:, dm, tt * 512:(tt + 1) * 512], in_=ost[:, pb, :], accum_op=ALU.add),
                                 reads=[f"ost{mi}_{pb}", f"xd{dm}_{tt}"], writes=[f"xd{dm}_{tt}"], lane=f"xacc{pb}")

                def lru():
                    with ExitStack() as e1:
                        def sb1(name, shape, dt):
                            return e1.enter_context(nc.sbuf_tensor(f"lru{l}_{name}", shape, dt))
                        wl = load_win(sb1, f"lru{l}_w", 784, 512)
                        y = sb1("y", [128, 2, T], BF16)
                        pr = e1.enter_context(nc.psum_tensor(f"lru{l}_pr", [128, 2, 512], F32))
                        for ct in range(2):
                            with ExitStack() as e2:
                                def sb3(name, shape, dt):
                                    return e2.enter_context(nc.sbuf_tensor(f"lru{l}_{ct}_{name}", shape, dt))
                                xb = sb3("xb", [128, HW + T], F32)
                                gt = sb3("gt", [128, T], BF16)
                                xc = sb3("xc", [128, T], F32)
                                rr = sb3("rr", [128, T], F32)
                                ii = sb3("ii", [128, T], F32)
                                tmp = sb3("tmp", [128, T], F32)
                                hh = sb3("hh", [128, T], F32)
                                sm = sb3("sm", [128, 8], F32)
                                hin = sb3("hin", [128, 1], F32)
                                proj_fm(wl, f"lru{l}_w", ct * 128, 128,
                                        lambda p_ap, tt, rk: P.op("act", lambda e: e.activation(out=xb[:, HW + tt * 512:HW + (tt + 1) * 512], in_=p_ap, func=AF.Copy),
                                                                  reads=rk, writes=["xb"]),
                                        lambda p_ap, rk: P.op("act", lambda e: e.activation(out=xb[:, 0:HW], in_=p_ap, func=AF.Copy),
                                                              reads=rk, writes=["xb"]))
                                proj_fm(wl, f"lru{l}_w", 256 + ct * 128, 128,
                                        lambda p_ap, tt, rk: P.op("act", lambda e: e.activation(out=gt[:, tt * 512:(tt + 1) * 512], in_=p_ap, func=AF.Gelu),
                                                                  reads=rk, writes=["gt"]))
                                cw = prm(f"lru_cw_{l}")[:, ct * 4:(ct + 1) * 4]
                                cb = prm(f"lru_cb_{l}")[:, ct:ct + 1]
                                P.op("dve", lambda e: e.tensor_scalar(out=xc[:], in0=xb[:, 1:1 + T], scalar1=cw[:, 0:1], scalar2=cb,
                                                                      op0=ALU.mult, op1=ALU.add), reads=["xb", "pp"], writes=["xc"])
                                for k in range(1, 4):
                                    P.op("dve", lambda e, k=k: e.scalar_tensor_tensor(out=xc[:], in0=xb[:, k + 1:k + 1 + T], scalar=cw[:, k:k + 1],
                                                                                      in1=xc[:], op0=ALU.mult, op1=ALU.add),
                                         reads=["xb", "xc", "pp"], writes=["xc"])
                                for (dst, dkey, wnm, bnm) in ((rr, "rr", "wa", "ba"), (ii, "ii", "wx", "bx")):
                                    wmat = prm(f"lru_{wnm}_{l}_{ct}")
                                    bcol = prm(f"lru_{bnm}_{l}")[:, ct:ct + 1]
                                    for tt in range(NT):
                                        pb = tt % 2
                                        P.op("pe", lambda e, tt=tt, pb=pb, wmat=wmat: e.matmul(pr[:, pb, :], lhsT=wmat, rhs=xc[:, tt * 512:(tt + 1) * 512],
                                                                                               start=True, stop=True),
                                             reads=["xc", "pp"], writes=[f"pr{pb}"])
                                        P.op("act", lambda e, tt=tt, pb=pb, dst=dst, bcol=bcol: e.activation(
                                            out=dst[:, tt * 512:(tt + 1) * 512], in_=pr[:, pb, :], func=AF.Sigmoid, bias=bcol),
                                            reads=[f"pr{pb}", "pp"], writes=[dkey])
                                lam = prm(f"lru_lam_{l}")[:, ct:ct + 1]
                                P.op("act", lambda e: e.activation(out=sm[:, 0:1], in_=lam, func=AF.Exp, scale=-1.0), reads=["pp"], writes=["sm"])
                                P.op("act", lambda e: e.activation(out=sm[:, 1:2], in_=sm[:, 0:1], func=AF.Ln, bias=1.0), reads=["sm"], writes=["sm"])
                                P.op("dve", lambda e: e.tensor_scalar(out=sm[:, 2:3], in0=sm[:, 1:2], scalar1=-8.0, scalar2=None, op0=ALU.mult),
                                     reads=["sm"], writes=["sm"])
                                P.op("act", lambda e: e.activation(out=rr[:], in_=rr[:], func=AF.Exp, scale=sm[:, 2:3]), reads=["rr", "sm"], writes=["rr"])
                                P.op("dve", lambda e: e.tensor_tensor(out=tmp[:], in0=rr[:], in1=rr[:], op=ALU.mult), reads=["rr"], writes=["tmp"])
                                P.op("act", lambda e: e.activation(out=tmp[:], in_=tmp[:], func=AF.Sqrt, scale=-1.0, bias=1.0), reads=["tmp"], writes=["tmp"])
                                P.op("dve", lambda e: e.tensor_tensor(out=ii[:], in0=ii[:], in1=tmp[:], op=ALU.mult), reads=["ii", "tmp"], writes=["ii"])
                                P.op("dve", lambda e: e.tensor_tensor(out=ii[:], in0=ii[:], in1=xc[:], op=ALU.mult), reads=["ii", "xc"], writes=["ii"])
                                P.op("dve", lambda e: e.memset(hin[:], 0.0), writes=["hin"])
                                for rnd in range(4):
                                    P.op("dve", lambda e: e.tensor_tensor_scan(out=hh[:], data0=rr[:], data1=ii[:], initial=hin[:, 0:1],
                                                                               op0=ALU.mult, op1=ALU.add),
                                         reads=["rr", "ii", "hin"], writes=["hh"])
                                    if rnd < 3:
                                        res, rkey = exchange(sb3, "hh", hh[:, T - 1:T], 1)
                                        P.op("dve", lambda e, res=res: e.tensor_copy(out=hin[:], in_=res[:]), reads=[rkey], writes=["hin"])
                                P.op("dve", lambda e: e.tensor_tensor(out=y[:, ct, :], in0=hh[:], in1=gt[:], op=ALU.mult),
                                     reads=["hh", "gt"], writes=[f"ylru{ct}"])
                                P.barrier()
                        out_proj(sb1, 1, y, ["ylru0", "ylru1"])
                        P.barrier()

                if "lru" in mixers:
                    lru()
                P.barrier()

                def gla():
                    NCH = T // 128
                    with ExitStack() as e1:
                        def sb1(name, shape, dt):
                            return e1.enter_context(nc.sbuf_tensor(f"gla{l}_{name}", shape, dt))

                        def ps1(name, shape, dt=F32):
                            return e1.enter_context(nc.psum_tensor(f"gla{l}_{name}", shape, dt))
                        wkey = f"gla{l}_w"
                        qr = sb1("qr", [128, T], BF16)
                        kr = sb1("kr", [128, T], BF16)
                        st = sb1("st", [16, T], F32)
                        vt = sb1("vt", [128, NCH, 256], BF16)
                        gt = sb1("gt", [128, NCH, 256], BF16)
                        bc = sb1("bc", [128, T], F32)
                        br = sb1("br", [128, T], F32)
                        one_t = sb1("one", [128, T], F32)
                        qd = sb1("qd", [128, T], BF16)
                        kd = sb1("kd", [128, T], BF16)
                        sm = sb1("sm", [128, 4 * NCH + 4], F32)
                        identb = sb1("identb", [128, 128], BF16)
                        ptm = ps1("ptm", [128, 512])
                        psc = ps1("psc", [128, 512])
                        pkv = ps1("pkv", [128, 512])
                        ptr = ps1("ptr", [128, 1024], BF16)
                        ew = e1.enter_context(ExitStack())
                        wl = load_win(lambda name, shape, dt: ew.enter_context(nc.sbuf_tensor(name, shape, dt)), wkey, 0, 784)
                        P.op("dve", lambda e: e.tensor_copy(out=identb[:], in_=prm("ident")), reads=["pp"], writes=["identb"])
                        P.op("dve", lambda e: e.memset(one_t[:], 1.0), writes=["one"])
                        sc = 32.0 ** -0.5
                        proj_fm(wl, wkey, 0, 128, lambda p_ap, tt, rk: P.op(
                            "act", lambda e: e.activation(out=qr[:, tt * 512:(tt + 1) * 512], in_=p_ap, func=AF.Copy, scale=sc), reads=rk, writes=["qr"]))
                        proj_fm(wl, wkey, 128, 128, lambda p_ap, tt, rk: P.op(
                            "act", lambda e: e.activation(out=kr[:, tt * 512:(tt + 1) * 512], in_=p_ap, func=AF.Copy), reads=rk, writes=["kr"]))
                        proj_fm(wl, wkey, 768, 16, lambda p_ap, tt, rk: P.op(
                            "act", lambda e: e.activation(out=st[:, tt * 512:(tt + 1) * 512], in_=p_ap, func=AF.Copy), reads=rk, writes=["st"]))
                        for c in range(NCH):
                            for kc in range(KC):
                                P.op("pe", lambda e, kc=kc, c=c: e.matmul(ptm[:, :], lhsT=xn2[:, kc, HW + c * 128:HW + (c + 1) * 128], rhs=wl[:, kc, 256:768],
                                                                           start=(kc == 0), stop=(kc == KC - 1)),
                                     reads=[wkey] + xn2_keys, writes=["ptm"])
                            P.op("act", lambda e, c=c: e.activation(out=vt[:, c, :], in_=ptm[:, 0:256], func=AF.Copy), reads=["ptm"], writes=[f"vt{c}"])
                            P.op("act", lambda e, c=c: e.activation(out=gt[:, c, :], in_=ptm[:, 256:512], func=AF.Silu), reads=["ptm"], writes=[f"gt{c}"])
                        P.barrier()
                        ew.close()
                        PTs = sb1("PTs", [128, NCH, 512], BF16)
                        ktm = sb1("ktm", [128, NCH, 128], BF16)
                        kp = sb1("kp", [128, 128], BF16)
                        qx = sb1("qx", [128, 512], BF16)
                        S = sb1("S", [128, 256], F32)
                        Sb = sb1("Sb", [128, 256], BF16)
                        tkv = sb1("tkv", [128, 256], F32)
                        oa = sb1("oa", [128, NCH, 256], F32)
                        ob_ = sb1("ob", [128, NCH, 256], BF16)
                        y = sb1("y", [128, 2, T], BF16)
                        aup = prm(f"gla_aup_{l}")
                        P.op("dve", lambda e: e.tensor_scalar(out=sm[:, 0:1], in0=prm(f"gla_ab_{l}")[:, 0:1], scalar1=-1.0, scalar2=None, op0=ALU.mult),
                             reads=["pp"], writes=["sm"])
                        for tt in range(NT):
                            P.op("pe", lambda e, tt=tt: e.matmul(psc[:, :], lhsT=aup[0:16, :], rhs=st[:, tt * 512:(tt + 1) * 512], start=True, stop=True),
                                 reads=["pp", "st"], writes=["psc"])
                            P.op("act", lambda e, tt=tt: e.activation(out=br[:, tt * 512:(tt + 1) * 512], in_=psc[:, :], func=AF.Exp, scale=-1.0, bias=sm[:, 0:1]),
                                 reads=["psc", "sm"], writes=["br"])
                        P.op("act", lambda e: e.activation(out=bc[:], in_=br[:], func=AF.Ln, bias=1.0), reads=["br"], writes=["bc"])
                        P.op("dve", lambda e: e.tensor_scalar(out=bc[:], in0=bc[:], scalar1=-1.0 / 16.0, scalar2=None, op0=ALU.mult), reads=["bc"], writes=["bc"])
                        P.op("dve", lambda e: e.tensor_tensor_scan(out=br[:], data0=one_t[:], data1=bc[:], initial=0.0, op0=ALU.mult, op1=ALU.add),
                             reads=["bc", "one"], writes=["br"])
                        br3 = br[:].rearrange("p (c i) -> p c i", i=128)
                        bc3 = bc[:].rearrange("p (c i) -> p c i", i=128)
                        bst = sm[:, 4:4 + NCH]
                        edec = sm[:, 4 + NCH:4 + 2 * NCH]
                        P.op("dve", lambda e: e.memset(sm[:, 4:5], 0.0), writes=["sm"])
                        if NCH > 1:
                            P.op("dve", lambda e: e.tensor_copy(out=sm[:, 5:4 + NCH], in_=br3[:, 0:NCH - 1, 127]), reads=["br"], writes=["sm"])
                        P.op("dve", lambda e: e.tensor_tensor(out=bc3, in0=br3, in1=bst.unsqueeze(2).to_broadcast([128, NCH, 128]), op=ALU.subtract),
                             reads=["br", "sm"], writes=["bc"])
                        P.op("act", lambda e: e.activation(out=edec, in_=bc3[:, :, 127], func=AF.Exp), reads=["bc"], writes=["sm"])
                        P.op("act", lambda e: e.activation(out=br[:], in_=bc[:], func=AF.Exp), reads=["bc"], writes=["br"])
                        P.op("dve", lambda e: e.tensor_tensor(out=qd[:], in0=qr[:], in1=br[:], op=ALU.mult), reads=["qr", "br"], writes=["qd"])
                        P.op("act", lambda e: e.activation(out=bc[:], in_=bc[:], func=AF.Exp, scale=-1.0), reads=["bc"], writes=["bc"])
                        P.op("dve", lambda e: e.tensor_tensor(out=kd[:], in0=kr[:], in1=bc[:], op=ALU.mult), reads=["kr", "bc"], writes=["kd"])
                        hm = prm("hm")
                        vkeys = [f"vt{c}" for c in range(NCH)]
                        P.op("dve", lambda e: e.memset(S[:], 0.0), writes=["S"])
                        for rnd in range(4):
                            P.op("act", lambda e: e.activation(out=Sb[:], in_=S[:], func=AF.Copy), reads=["S"], writes=["Sb"])
                            for c in range(NCH):
                                cs = slice(c * 128, (c + 1) * 128)
                                if rnd == 0:
                                    P.op("dve", lambda e, c=c, cs=cs: e.tensor_scalar(out=kp[:], in0=kd[:, cs], scalar1=edec[:, c:c + 1], scalar2=None, op0=ALU.mult),
                                         reads=["kd", "sm"], writes=["kp"])
                                    P.op("pe", lambda e: e.transpose(out=ptr[:, 0:128], in_=kp[:], identity=identb[:]), reads=["kp", "identb"], writes=["ptr"])
                                    P.op("act", lambda e, c=c: e.activation(out=ktm[:, c, :], in_=ptr[:, 0:128], func=AF.Copy), reads=["ptr"], writes=[f"ktm{c}"])
                                    for h in range(4):
                                        P.op("dve", lambda e, h=h, cs=cs: e.tensor_scalar(out=qx[:, h * 128:(h + 1) * 128], in0=qd[:, cs], scalar1=hm[:, h:h + 1],
                                                                                          scalar2=None, op0=ALU.mult),
                                             reads=["qd", "pp"], writes=["qx"])
                                    P.op("pe", lambda e, cs=cs: e.matmul(psc[:, :], lhsT=kd[:, cs], rhs=qx[:], start=True, stop=True),
                                         reads=["kd", "qx"], writes=["psc"])
                                    P.op("dve", lambda e, c=c: e.tensor_tensor(out=PTs[:, c, :], in0=psc[:, :], in1=prm("mask4"), op=ALU.mult),
                                         reads=["psc", "pp"], writes=[f"PT{c}"])
                                if rnd == 3:
                                    P.op("pe", lambda e, cs=cs: e.matmul(ptm[:, 0:256], lhsT=qd[:, cs], rhs=Sb[:], start=True, stop=False),
                                         reads=["qd", "Sb"], writes=["ptm"])
                                    for h in range(4):
                                        P.op("pe", lambda e, h=h, c=c: e.matmul(ptm[:, h * 64:(h + 1) * 64], lhsT=PTs[:, c, h * 128:(h + 1) * 128],
                                                                                rhs=vt[:, c, h * 64:(h + 1) * 64], start=False, stop=(h == 3)),
                                             reads=[f"PT{c}", f"vt{c}"], writes=["ptm"])
                                    P.op("act", lambda e, c=c: e.activation(out=oa[:, c, :], in_=ptm[:, 0:256], func=AF.Copy), reads=["ptm"], writes=[f"oa{c}"])
                                P.op("pe", lambda e, c=c: e.matmul(pkv[:, 0:256], lhsT=ktm[:, c, :], rhs=vt[:, c, :], start=True, stop=True),
                                     reads=[f"ktm{c}", f"vt{c}"], writes=["pkv"])
                                P.op("dve", lambda e: e.tensor_tensor(out=tkv[:], in0=pkv[:, 0:256], in1=prm("bm"), op=ALU.mult), reads=["pkv", "pp"], writes=["tkv"])
                                P.op("dve", lambda e, c=c: e.scalar_tensor_tensor(out=S[:], in0=S[:], scalar=edec[:, c:c + 1], in1=tkv[:], op0=ALU.mult, op1=ALU.add),
                                     reads=["S", "tkv", "sm"], writes=["S"])
                                P.op("act", lambda e: e.activation(out=Sb[:], in_=S[:], func=AF.Copy), reads=["S"], writes=["Sb"])
                            if rnd < 3:
                                res, rkey = exchange(sb1, "S", S[:], 256)
                                P.op("dve", lambda e, res=res: e.tensor_copy(out=S[:], in_=res[:]), reads=[rkey], writes=["S"])
                        okeys = [f"oa{c}" for c in range(NCH)]
                        oa4 = oa[:].rearrange("p c (h v) -> p (c h) v", v=64)
                        ob4 = ob_[:].rearrange("p c (h v) -> p (c h) v", v=64)
                        sq_ = sb1("sqv", [128, NCH, 256], F32)
                        sq4 = sq_[:].rearrange("p c (h v) -> p (c h) v", v=64)
                        rsd = sb1("rsd", [128, NCH * 4], F32)
                        P.op("dve", lambda e: e.tensor_tensor(out=sq_[:], in0=oa[:], in1=oa[:], op=ALU.mult), reads=okeys, writes=["sqv"])
                        P.op("dve", lambda e: e.tensor_reduce(out=rsd[:], in_=sq4, axis=AX.X, op=ALU.add), reads=["sqv"], writes=["rsd"])
                        P.op("act", lambda e: e.activation(out=rsd[:], in_=rsd[:], func=AF.Sqrt, scale=1.0 / 64.0, bias=1e-5), reads=["rsd"], writes=["rsd"])
                        P.op("dve", lambda e: e.reciprocal(out=rsd[:], in_=rsd[:]), reads=["rsd"], writes=["rsd"])
                        P.op("dve", lambda e: e.tensor_tensor(out=sq4, in0=oa4, in1=rsd[:].unsqueeze(2).to_broadcast([128, NCH * 4, 64]), op=ALU.mult),
                             reads=okeys + ["rsd"], writes=["sqv"])
                        P.op("dve", lambda e: e.tensor_tensor(out=sq4, in0=sq4, in1=prm(f"gla_nw_{l}").unsqueeze(1).to_broadcast([128, NCH * 4, 64]), op=ALU.mult),
                             reads=["sqv", "pp"], writes=["sqv"])
                        P.op("dve", lambda e: e.tensor_tensor(out=ob_[:], in0=sq_[:], in1=gt[:], op=ALU.mult),
                             reads=["sqv"] + [f"gt{c}" for c in range(NCH)], writes=["ob"])
                        for c in range(NCH):
                            for ct in range(2):
                                P.op("pe", lambda e, c=c, ct=ct: e.transpose(out=ptr[:, 0:128], in_=ob_[:, c, ct * 128:(ct + 1) * 128], identity=identb[:]),
                                     reads=["ob", "identb"], writes=["ptr"])
                                P.op("act", lambda e, c=c, ct=ct: e.activation(out=y[:, ct, c * 128:(c + 1) * 128], in_=ptr[:, 0:128], func=AF.Copy),
                                     reads=["ptr"], writes=["ygla"])
                        out_proj(sb1, 0, y, ["ygla"])
                        P.barrier()

                if "gla" in mixers:
                    gla()
                P.barrier()

                def ssd():
                    NCH = T // 128
                    with ExitStack() as e1:
                        def sb1(name, shape, dt):
                            return e1.enter_context(nc.sbuf_tensor(f"ssd{l}_{name}", shape, dt))

                        def ps1(name, shape, dt=F32):
                            return e1.enter_context(nc.psum_tensor(f"ssd{l}_{name}", shape, dt))
                        wkey = f"ssd{l}_w"
                        BT = sb1("BT", [128, 2, T], BF16)
                        CT = sb1("CT", [128, 2, T], BF16)
                        zt = sb1("zt", [128, NCH, 256], BF16)
                        dtt = sb1("dtt", [128, NCH, 4], F32)
                        dA = sb1("dA", [128, NCH, 4], F32)
                        cs = sb1("cs", [128, NCH, 4], F32)
                        ncs = sb1("ncs", [128, NCH, 4], F32)
                        csl = sb1("csl", [128, NCH, 4], F32)
                        ecs = sb1("ecs", [128, NCH, 4], F32)
                        dend = sb1("dend", [128, NCH, 4], F32)
                        ecsl = sb1("ecsl", [128, NCH, 4], F32)
                        av = sb1("av", [128, 4], F32)
                        xt = sb1("xt", [128, NCH, 256], F32)
                        xdt = sb1("xdt", [128, NCH, 256], BF16)
                        Bt = sb1("Bt", [128, NCH, 256], BF16)
                        identf = prm("ident")
                        ptm = ps1("ptm", [128, 512])
                        pa = ps1("pa", [128, 512])
                        pb_ = ps1("pb", [128, 512])
                        pc = ps1("pc", [128, 512])
                        ew = e1.enter_context(ExitStack())

                        def sbw(name, shape, dt):
                            return ew.enter_context(nc.sbuf_tensor(f"ssd{l}_{name}" if not name.startswith("ssd") else name, shape, dt))
                        wl = load_win(sbw, wkey, 2128, 1028)
                        xb = sbw("xb", [128, HW + T], F32)
                        xc = sbw("xc", [128, T], F32)
                        xsT = sbw("xsT", [128, 2, T], F32)
                        BTf = sbw("BTf", [128, 2, T], F32)
                        for ct in range(6):
                            proj_fm(wl, wkey, 256 + ct * 128, 128,
                                    lambda p_ap, tt, rk: P.op("act", lambda e: e.activation(out=xb[:, HW + tt * 512:HW + (tt + 1) * 512], in_=p_ap, func=AF.Copy),
                                                              reads=rk, writes=["xb"]),
                                    lambda p_ap, rk: P.op("act", lambda e: e.activation(out=xb[:, 0:HW], in_=p_ap, func=AF.Copy), reads=rk, writes=["xb"]))
                            cw = prm(f"ssd_cw_{l}")[:, ct * 4:(ct + 1) * 4]
                            cb = prm(f"ssd_cb_{l}")[:, ct:ct + 1]
                            P.op("dve", lambda e: e.tensor_scalar(out=xc[:], in0=xb[:, 1:1 + T], scalar1=cw[:, 0:1], scalar2=cb, op0=ALU.mult, op1=ALU.add),
                                 reads=["xb", "pp"], writes=["xc"])
                            for k in range(1, 4):
                                P.op("dve", lambda e, k=k: e.scalar_tensor_tensor(out=xc[:], in0=xb[:, k + 1:k + 1 + T], scalar=cw[:, k:k + 1], in1=xc[:],
                                                                                  op0=ALU.mult, op1=ALU.add), reads=["xb", "xc", "pp"], writes=["xc"])
                            if ct < 2:
                                P.op("act", lambda e, ct=ct: e.activation(out=xsT[:, ct, :], in_=xc[:], func=AF.Silu), reads=["xc"], writes=["xsT"])
                            elif ct < 4:
                                P.op("act", lambda e, ct=ct: e.activation(out=BTf[:, ct - 2, :], in_=xc[:], func=AF.Silu), reads=["xc"], writes=["BTf"])
                                P.op("dve", lambda e, ct=ct: e.tensor_copy(out=BT[:, ct - 2, :], in_=BTf[:, ct - 2, :]), reads=["BTf"], writes=["BT"])
                            else:
                                P.op("act", lambda e, ct=ct: e.activation(out=CT[:, ct - 4, :], in_=xc[:], func=AF.Silu), reads=["xc"], writes=["CT"])
                        P.op("act", lambda e: e.activation(out=av[:], in_=prm(f"ssd_alog_{l}"), func=AF.Exp), reads=["pp"], writes=["av"])
                        P.op("dve", lambda e: e.tensor_scalar(out=av[:], in0=av[:], scalar1=-1.0, scalar2=None, op0=ALU.mult), reads=["av"], writes=["av"])
                        for c in range(NCH):
                            tsl = slice(HW + c * 128, HW + (c + 1) * 128)
                            for kc in range(KC):
                                P.op("pe", lambda e, kc=kc, tsl=tsl: e.matmul(ptm[:, 0:256], lhsT=xn2[:, kc, tsl], rhs=wl[:, kc, 0:256],
                                                                              start=(kc == 0), stop=(kc == KC - 1)), reads=[wkey] + xn2_keys, writes=["ptm"])
                            P.op("act", lambda e, c=c: e.activation(out=zt[:, c, :], in_=ptm[:, 0:256], func=AF.Silu), reads=["ptm"], writes=["zt"])
                            for kc in range(KC):
                                P.op("pe", lambda e, kc=kc, tsl=tsl: e.matmul(pa[:, 0:4], lhsT=xn2[:, kc, tsl], rhs=wl[:, kc, 1024:1028],
                                                                              start=(kc == 0), stop=(kc == KC - 1)), reads=[wkey] + xn2_keys, writes=["pa"])
                            P.op("dve", lambda e, c=c: e.tensor_tensor(out=dtt[:, c, :], in0=pa[:, 0:4], in1=prm(f"ssd_dtb_{l}"), op=ALU.add),
                                 reads=["pa", "pp"], writes=["dtt"])
                            for ct in range(2):
                                P.op("pe", lambda e, c=c, ct=ct: e.transpose(out=pb_[:, ct * 128:(ct + 1) * 128], in_=xsT[:, ct, c * 128:(c + 1) * 128], identity=identf),
                                     reads=["xsT", "pp"], writes=["pb"])
                                P.op("pe", lambda e, c=c, ct=ct: e.transpose(out=pb_[:, 256 + ct * 128:256 + (ct + 1) * 128], in_=BTf[:, ct, c * 128:(c + 1) * 128], identity=identf),
                                     reads=["BTf", "pp"], writes=["pb"])
                            P.op("act", lambda e, c=c: e.activation(out=xt[:, c, :], in_=pb_[:, 0:256], func=AF.Copy), reads=["pb"], writes=["xt"])
                            P.op("act", lambda e, c=c: e.activation(out=Bt[:, c, :], in_=pb_[:, 256:512], func=AF.Copy), reads=["pb"], writes=["Bt"])
                        P.barrier()
                        ew.close()
                        yd = sb1("yd", [128, NCH, 256], F32)
                        ya = sb1("ya", [128, NCH, 256], F32)
                        stl = sb1("stl", [128, NCH, 256], F32)
                        hs = sb1("hs", [128, 256], F32)
                        hsb = sb1("hsb", [128, 256], BF16)
                        dAb = sb1("dAb", [128, 128], F32)
                        tmpm = sb1("tmpm", [128, 128], F32)
                        decT = sb1("decT", [128, 128], F32)
                        MT = sb1("MT", [128, 128], BF16)
                        xde = sb1("xde", [128, 256], BF16)
                        y = sb1("y", [128, 2, T], BF16)
                        P.op("act", lambda e: e.activation(out=dtt[:], in_=dtt[:], func=AF.Exp), reads=["dtt"], writes=["dtt"])
                        P.op("act", lambda e: e.activation(out=dtt[:], in_=dtt[:], func=AF.Ln, bias=1.0), reads=["dtt"], writes=["dtt"])
                        P.op("dve", lambda e: e.tensor_tensor(out=dA[:], in0=dtt[:], in1=av[:].unsqueeze(1).to_broadcast([128, NCH, 4]), op=ALU.mult),
                             reads=["dtt", "av"], writes=["dA"])
                        xt4 = xt[:].rearrange("p c (h v) -> p (c h) v", v=64)
                        xdt4 = xdt[:].rearrange("p c (h v) -> p (c h) v", v=64)
                        P.op("dve", lambda e: e.tensor_tensor(out=xdt4, in0=xt4, in1=dtt[:].rearrange("p c h -> p (c h)").unsqueeze(2).to_broadcast([128, NCH * 4, 64]),
                                                              op=ALU.mult), reads=["xt", "dtt"], writes=["xdt"])
                        for c in range(NCH):
                            P.op("pe", lambda e, c=c: e.matmul(pa[:, 0:4], lhsT=prm("utri"), rhs=dA[:, c, :], start=True, stop=True), reads=["dA", "pp"], writes=["pa"])
                            P.op("pe", lambda e, c=c: e.matmul(pa[:, 8:12], lhsT=prm("ones"), rhs=dA[:, c, :], start=True, stop=True), reads=["dA", "pp"], writes=["pa"])
                            P.op("act", lambda e, c=c: e.activation(out=cs[:, c, :], in_=pa[:, 0:4], func=AF.Copy), reads=["pa"], writes=["cs"])
                            P.op("act", lambda e, c=c: e.activation(out=csl[:, c, :], in_=pa[:, 8:12], func=AF.Copy), reads=["pa"], writes=["csl"])
                        P.op("dve", lambda e: e.tensor_scalar(out=ncs[:], in0=cs[:], scalar1=-1.0, scalar2=None, op0=ALU.mult), reads=["cs"], writes=["ncs"])
                        P.op("act", lambda e: e.activation(out=ecs[:], in_=cs[:], func=AF.Exp), reads=["cs"], writes=["ecs"])
                        P.op("act", lambda e: e.activation(out=ecsl[:], in_=csl[:], func=AF.Exp), reads=["csl"], writes=["ecsl"])
                        P.op("dve", lambda e: e.tensor_tensor(out=dend[:], in0=csl[:], in1=cs[:], op=ALU.subtract), reads=["cs", "csl"], writes=["dend"])
                        P.op("act", lambda e: e.activation(out=dend[:], in_=dend[:], func=AF.Exp), reads=["dend"], writes=["dend"])
                        for c in range(NCH):
                            cs_ = slice(c * 128, (c + 1) * 128)
                            for g in range(2):
                                P.op("pe", lambda e, g=g, cs_=cs_: e.matmul(pb_[:, g * 128:(g + 1) * 128], lhsT=BT[:, g, cs_], rhs=CT[:, g, cs_], start=True, stop=True),
                                     reads=["BT", "CT"], writes=["pb"])
                            for h in range(4):
                                g = h // 2
                                P.op("dve", lambda e, c=c, h=h: e.tensor_scalar(out=dAb[:], in0=prm("ones"), scalar1=dA[:, c, h:h + 1], scalar2=None, op0=ALU.mult),
                                     reads=["dA", "pp"], writes=["dAb"])
                                P.op("pe", lambda e: e.matmul(pc[:, 0:128], lhsT=dAb[:], rhs=prm("utri"), start=True, stop=True), reads=["dAb", "pp"], writes=["pc"])
                                P.op("dve", lambda e: e.tensor_tensor(out=tmpm[:], in0=pc[:, 0:128], in1=prm("negmask"), op=ALU.add), reads=["pc", "pp"], writes=["tmpm"])
                                P.op("act", lambda e, c=c, h=h: e.activation(out=decT[:], in_=tmpm[:], func=AF.Exp, bias=ncs[:, c, h:h + 1]),
                                     reads=["tmpm", "ncs"], writes=["decT"])
                                P.op("dve", lambda e, g=g: e.tensor_tensor(out=MT[:], in0=pb_[:, g * 128:(g + 1) * 128], in1=decT[:], op=ALU.mult),
                                     reads=["pb", "decT"], writes=["MT"])
                                P.op("pe", lambda e, c=c, h=h: e.matmul(pc[:, 128 + h * 64:128 + (h + 1) * 64], lhsT=MT[:], rhs=xdt[:, c, h * 64:(h + 1) * 64],
                                                                        start=True, stop=True), reads=["MT", "xdt"], writes=["pc"])
                                P.op("dve", lambda e, c=c, h=h: e.tensor_scalar(out=xde[:, h * 64:(h + 1) * 64], in0=xdt[:, c, h * 64:(h + 1) * 64],
                                                                                scalar1=dend[:, c, h:h + 1], scalar2=None, op0=ALU.mult),
                                     reads=["xdt", "dend"], writes=["xde"])
                                P.op("pe", lambda e, c=c, h=h, g=g: e.matmul(pa[:, 128 + h * 64:128 + (h + 1) * 64], lhsT=Bt[:, c, g * 128:(g + 1) * 128],
                                                                             rhs=xde[:, h * 64:(h + 1) * 64], start=True, stop=True),
                                     reads=["Bt", "xde"], writes=["pa"])
                            P.op("act", lambda e, c=c: e.activation(out=yd[:, c, :], in_=pc[:, 128:384], func=AF.Copy), reads=["pc"], writes=["yd"])
                            P.op("act", lambda e, c=c: e.activation(out=stl[:, c, :], in_=pa[:, 128:384], func=AF.Copy), reads=["pa"], writes=["stl"])
                        P.op("dve", lambda e: e.memset(hs[:], 0.0), writes=["hs"])
                        for rnd in range(4):
                            P.op("act", lambda e: e.activation(out=hsb[:], in_=hs[:], func=AF.Copy), reads=["hs"], writes=["hsb"])
                            for c in range(NCH):
                                cs_ = slice(c * 128, (c + 1) * 128)
                                for g in range(2):
                                    P.op("pe", lambda e, g=g, cs_=cs_: e.matmul(ptm[:, g * 128:(g + 1) * 128], lhsT=CT[:, g, cs_], rhs=hsb[:, g * 128:(g + 1) * 128],
                                                                                start=True, stop=True), reads=["CT", "hsb"], writes=["ptm"])
                                for h in range(4):
                                    hc = slice(h * 64, (h + 1) * 64)
                                    P.op("dve", lambda e, c=c, h=h, hc=hc: e.scalar_tensor_tensor(out=ya[:, c, hc], in0=ptm[:, hc], scalar=ecs[:, c, h:h + 1], in1=yd[:, c, hc],
                                                                                                  op0=ALU.mult, op1=ALU.add), reads=["ptm", "ecs", "yd"], writes=["ya"])
                                    P.op("dve", lambda e, c=c, h=h, hc=hc: e.scalar_tensor_tensor(out=hs[:, hc], in0=hs[:, hc], scalar=ecsl[:, c, h:h + 1], in1=stl[:, c, hc],
                                                                                                  op0=ALU.mult, op1=ALU.add), reads=["hs", "ecsl", "stl"], writes=["hs"])
                                P.op("act", lambda e: e.activation(out=hsb[:], in_=hs[:], func=AF.Copy), reads=["hs"], writes=["hsb"])
                            if rnd < 3:
                                res, rkey = exchange(sb1, "hs", hs[:], 256)
                                P.op("dve", lambda e, res=res: e.tensor_copy(out=hs[:], in_=res[:]), reads=[rkey], writes=["hs"])
                        ya4 = ya[:].rearrange("p c (h v) -> p (c h) v", v=64)
                        yd4 = yd[:].rearrange("p c (h v) -> p (c h) v", v=64)
                        P.op("dve", lambda e: e.tensor_tensor(out=yd[:].rearrange("p c (h v) -> p c h v", v=64), in0=xt[:].rearrange("p c (h v) -> p c h v", v=64), in1=prm(f"ssd_d_{l}").unsqueeze(1).unsqueeze(3).to_broadcast([128, NCH, 4, 64]),
                                                              op=ALU.mult), reads=["xt", "pp", "yd"], writes=["yd"])
                        P.op("dve", lambda e: e.tensor_tensor(out=ya[:], in0=ya[:], in1=yd[:], op=ALU.add), reads=["ya", "yd"], writes=["ya"])
                        P.op("dve", lambda e: e.tensor_tensor(out=ya[:], in0=ya[:], in1=zt[:], op=ALU.mult), reads=["ya", "zt"], writes=["ya"])
                        rsd = sb1("rsd", [128, NCH * 2], F32)
                        ya2 = ya[:].rearrange("p c (g v) -> p (c g) v", v=128)
                        yd2 = yd[:].rearrange("p c (g v) -> p (c g) v", v=128)
                        P.op("dve", lambda e: e.tensor_tensor(out=yd[:], in0=ya[:], in1=ya[:], op=ALU.mult), reads=["ya"], writes=["yd"])
                        P.op("dve", lambda e: e.tensor_reduce(out=rsd[:], in_=yd2, axis=AX.X, op=ALU.add), reads=["yd"], writes=["rsd"])
                        P.op("act", lambda e: e.activation(out=rsd[:], in_=rsd[:], func=AF.Sqrt, scale=1.0 / 128.0, bias=1e-5), reads=["rsd"], writes=["rsd"])
                        P.op("dve", lambda e: e.reciprocal(out=rsd[:], in_=rsd[:]), reads=["rsd"], writes=["rsd"])
                        P.op("dve", lambda e: e.tensor_tensor(out=ya2, in0=ya2, in1=rsd[:].unsqueeze(2).to_broadcast([128, NCH * 2, 128]), op=ALU.mult),
                             reads=["ya", "rsd"], writes=["ya"])
                        nw = prm(f"ssd_nw_{l}")
                        for c in range(NCH):
                            for ct in range(2):
                                P.op("pe", lambda e, c=c, ct=ct: e.transpose(out=pb_[:, 0:128], in_=ya[:, c, ct * 128:(ct + 1) * 128], identity=identf),
                                     reads=["ya", "pp"], writes=["pb"])
                                P.op("act", lambda e, c=c, ct=ct: e.activation(out=y[:, ct, c * 128:(c + 1) * 128], in_=pb_[:, 0:128], func=AF.Copy, scale=nw[:, ct:ct + 1]),
                                     reads=["pb", "pp"], writes=["yssd"])
                        out_proj(sb1, 3, y, ["yssd"])
                        P.barrier()

                if "ssd" in mixers:
                    ssd()
                P.barrier()

                def rw():
                    NC = T // 64
                    CB = 1296

                    def V(fn, r, w):
                        P.op("dve", fn, reads=r, writes=w)

                    def A(fn, r, w):
                        P.op("act", fn, reads=r, writes=w)

                    def MM(fn, r, w):
                        P.op("pe", fn, reads=r, writes=w)
                    with ExitStack() as e0:
                        def sb0(name, shape, dt):
                            return e0.enter_context(nc.sbuf_tensor(f"rw{l}_{name}", shape, dt))
                        vfb = sb0("vfb", [128, 2, T], F32)
                        t1 = sb0("t1", [8, T], F32)
                        e00 = e0.enter_context(ExitStack())
                        raw0 = e00.enter_context(nc.sbuf_tensor(f"rw{l}_raw0", [128, HW + T], F32))
                        wv = e00.enter_context(nc.sbuf_tensor(f"rw{l}_wv", [128, KC, 256], BF16))
                        P.op("poolq", lambda e: e.dma_start(out=wv[:], in_=win_d[l, :, :, CB + 512:CB + 768]), writes=["rw_wv"], lane="win")

                        def proj_lerp(raw, wbuf, wkey, col, Mr, mu_ap, dst, dkey):
                            proj_fm(wbuf, wkey, col, Mr,
                                    lambda p_ap, tt, rk: A(lambda e: e.activation(out=raw[0:Mr, HW + tt * 512:HW + (tt + 1) * 512], in_=p_ap, func=AF.Copy), rk, ["rw_raw"]),
                                    lambda p_ap, rk: A(lambda e: e.activation(out=raw[0:Mr, 0:HW], in_=p_ap, func=AF.Copy), rk, ["rw_raw"]))
                            V(lambda e: e.tensor_tensor(out=dst, in0=raw[0:Mr, HW - 1:HW - 1 + T], in1=raw[0:Mr, HW:HW + T], op=ALU.subtract), ["rw_raw"], [dkey])
                            V(lambda e: e.scalar_tensor_tensor(out=dst, in0=dst, scalar=mu_ap, in1=raw[0:Mr, HW:HW + T], op0=ALU.mult, op1=ALU.add),
                              ["rw_raw", dkey, "pp"], [dkey])
                        for c2 in range(2):
                            proj_lerp(raw0, wv, "rw_wv", c2 * 128, 128, prm(f"rw_muv_{l}")[:, c2:c2 + 1], vfb[:, c2, :], f"vfb{c2}")
                        if l == 0:
                            for c2 in range(2):
                                P.op("sp", lambda e, c2=c2: e.dma_start(out=vfirst_d[:, c2, :], in_=vfb[:, c2, :]), reads=[f"vfb{c2}"], writes=[f"vfd{c2}"], lane=f"vf{c2}")
                        else:
                            v1p = prm(f"rw_v1_{l}")
                            for tt in range(NT):
                                pb = tt % 2
                                for c2 in range(2):
                                    MM(lambda e, c2=c2, tt=tt, pb=pb: e.matmul(pj[0:8, pb, :], lhsT=v1p[:, c2 * 8:(c2 + 1) * 8], rhs=vfb[:, c2, tt * 512:(tt + 1) * 512],
                                                                                start=(c2 == 0), stop=(c2 == 1)), ["pp", "vfb0", "vfb1"], [f"pj{pb}"])
                                A(lambda e, tt=tt, pb=pb: e.activation(out=t1[:, tt * 512:(tt + 1) * 512], in_=pj[0:8, pb, :], func=AF.Copy), [f"pj{pb}"], ["rw_t1"])
                        P.barrier()
                        e00.close()
                        for ct in range(2):
                            rw_ct(ct, sb0, vfb, t1, NC, CB, V, A, MM)
                        P.barrier()

                def rw_ct(ct, sb0, vfb, t1, NC, CB, V, A, MM):
                    with ExitStack() as eR:
                        def sbR(name, shape, dt):
                            return eR.enter_context(nc.sbuf_tensor(f"rw{l}_{ct}_{name}", shape, dt))
                        at = sbR("at", [128, T], BF16)
                        bt = sbR("bt", [128, T], BF16)
                        kt = sbR("kt", [128, T], BF16)
                        rt = sbR("rt", [128, T], BF16)
                        bh = sbR("bh", [128, T], BF16)
                        kh = sbR("kh", [128, T], BF16)
                        vb = sbR("vb", [128, T], BF16)
                        gfm = sbR("gfm", [128, T], BF16)
                        bon = sbR("bon", [128, T], F32)
                        PC = sbR("PC", [128, NC], F32)
                        bst = sbR("bst", [128, NC], F32)
                        sm = sbR("sm", [128, 4], F32)
                        identb2 = sbR("identb2", [128, 64], BF16)
                        V(lambda e: e.tensor_copy(out=identb2[:], in_=prm("ident2")), ["pp"], ["identb2"])
                        with ExitStack() as eP:
                            def sbP(name, shape, dt):
                                return eP.enter_context(nc.sbuf_tensor(f"rw{l}_{ct}_{name}", shape, dt))
                            wr = sbP("wr", [128, KC, 128], BF16)
                            wk = sbP("wk", [128, KC, 128], BF16)
                            ws = sbP("ws", [128, KC, 64], BF16)
                            P.op("poolq", lambda e: e.dma_start(out=wr[:], in_=win_d[l, :, :, CB + ct * 128:CB + (ct + 1) * 128]), writes=["rw_wr"], lane="win")
                            P.op("poolq", lambda e: e.dma_start(out=wk[:], in_=win_d[l, :, :, CB + 256 + ct * 128:CB + 256 + (ct + 1) * 128]), writes=["rw_wk"], lane="win")
                            P.op("poolq", lambda e: e.dma_start(out=ws[:], in_=win_d[l, :, :, CB + 768:CB + 832]), writes=["rw_ws"], lane="win")
                            raw = sbP("raw", [128, HW + T], F32)
                            stb = sbP("stb", [32, T], F32)
                            rf = sbP("rf", [128, T], F32)
                            kf = sbP("kf", [128, T], F32)
                            lw = sbP("lw", [128, T], F32)
                            af = sbP("af", [128, T], F32)
                            t2 = sbP("t2", [128, T], F32)
                            t3 = sbP("t3", [128, T], F32)
                            vv = vfb[:, ct, :]
                            vkey = f"vfb{ct}"
                            lw_tmp = t2

                            def proj_lerp(wbuf, wkey, col, Mr, mu_ap, dst, dkey):
                                proj_fm(wbuf, wkey, col, Mr,
                                        lambda p_ap, tt, rk: A(lambda e: e.activation(out=raw[0:Mr, HW + tt * 512:HW + (tt + 1) * 512], in_=p_ap, func=AF.Copy), rk, ["raw"]),
                                        lambda p_ap, rk: A(lambda e: e.activation(out=raw[0:Mr, 0:HW], in_=p_ap, func=AF.Copy), rk, ["raw"]))
                                V(lambda e: e.tensor_tensor(out=dst, in0=raw[0:Mr, HW - 1:HW - 1 + T], in1=raw[0:Mr, HW:HW + T], op=ALU.subtract), ["raw"], [dkey])
                                V(lambda e: e.scalar_tensor_tensor(out=dst, in0=dst, scalar=mu_ap, in1=raw[0:Mr, HW:HW + T], op0=ALU.mult, op1=ALU.add),
                                  ["raw", dkey, "pp"], [dkey])

                            def lowrank(src_rows, src_key, wname, dst, dkey, func, bias_ap):
                                wmat = prm(wname)
                                for tt in range(NT):
                                    pb = tt % 2
                                    MM(lambda e, tt=tt, pb=pb: e.matmul(pj[:, pb, :], lhsT=wmat[0:src_rows, ct * 128:(ct + 1) * 128],
                                                                        rhs=stb[0:src_rows, tt * 512:(tt + 1) * 512], start=True, stop=True),
                                       ["pp", src_key], [f"pj{pb}"])
                                    if bias_ap is None:
                                        A(lambda e, tt=tt, pb=pb: e.activation(out=dst[:, tt * 512:(tt + 1) * 512], in_=pj[:, pb, :], func=func), [f"pj{pb}"], [dkey])
                                    else:
                                        A(lambda e, tt=tt, pb=pb: e.activation(out=dst[:, tt * 512:(tt + 1) * 512], in_=pj[:, pb, :], func=func, bias=bias_ap),
                                          [f"pj{pb}", "pp"], [dkey])
                            proj_lerp(wr, "rw_wr", 0, 128, prm(f"rw_mur_{l}")[:, ct:ct + 1], rf[:], "rf")
                            proj_lerp(wk, "rw_wk", 0, 128, prm(f"rw_muk_{l}")[:, ct:ct + 1], kf[:], "kf")
                            proj_lerp(ws, "rw_ws", 0, 16, prm(f"rw_musw_{l}")[0:16, 0:1], stb[0:16, :], "stb")
                            A(lambda e: e.activation(out=stb[0:16, :], in_=stb[0:16, :], func=AF.Tanh), ["stb"], ["stb"])
                            lowrank(16, "stb", f"rw_w2_{l}", lw, "lw", AF.Sigmoid, prm(f"rw_w0_{l}")[:, ct:ct + 1])
                            V(lambda e: e.tensor_scalar(out=lw[:], in0=lw[:], scalar1=-float(np.exp(-0.5)), scalar2=None, op0=ALU.mult), ["lw"], ["lw"])
                            proj_lerp(ws, "rw_ws", 16, 16, prm(f"rw_musa_{l}")[0:16, 0:1], stb[0:16, :], "stb")
                            lowrank(16, "stb", f"rw_a2_{l}", af, "af", AF.Sigmoid, prm(f"rw_a0_{l}")[:, ct:ct + 1])
                            proj_lerp(ws, "rw_ws", 32, 32, prm(f"rw_musg_{l}")[0:32, 0:1], stb[0:32, :], "stb")
                            A(lambda e: e.activation(out=stb[0:32, :], in_=stb[0:32, :], func=AF.Sigmoid), ["stb"], ["stb"])
                            lowrank(32, "stb", f"rw_g2_{l}", gfm, "gfm", AF.Copy, None)
                            if l > 0:
                                P.op("sp", lambda e: e.dma_start(out=raw[:, 0:T], in_=vfirst_d[:, ct, :]), reads=[f"vfd{ct}", "raw"], writes=["raw"], lane="vfl")
                                v2p = prm(f"rw_v2_{l}")
                                for tt in range(NT):
                                    pb = tt % 2
                                    MM(lambda e, tt=tt, pb=pb: e.matmul(pj[:, pb, :], lhsT=v2p[0:8, ct * 128:(ct + 1) * 128], rhs=t1[0:8, tt * 512:(tt + 1) * 512],
                                                                        start=True, stop=True), ["pp", "rw_t1"], [f"pj{pb}"])
                                    A(lambda e, tt=tt, pb=pb: e.activation(out=t2[:, tt * 512:(tt + 1) * 512], in_=pj[:, pb, :], func=AF.Sigmoid,
                                                                           bias=prm(f"rw_v0_{l}")[:, ct:ct + 1]), [f"pj{pb}", "pp"], ["t2"])
                                V(lambda e: e.tensor_tensor(out=t3[:], in0=raw[:, 0:T], in1=vv, op=ALU.subtract), ["raw", vkey], ["t3"])
                                V(lambda e: e.tensor_tensor(out=t3[:], in0=t3[:], in1=t2[:], op=ALU.mult), ["t3", "t2"], ["t3"])
                                V(lambda e: e.tensor_tensor(out=t2[:], in0=vv, in1=t3[:], op=ALU.add), ["t3", vkey], ["t2"])
                                vuse, vukey = t2, "t2"
                                V(lambda e: e.tensor_copy(out=vb[:], in_=t2[:]), ["t2"], ["vb"])
                            else:
                                V(lambda e: e.tensor_copy(out=vb[:], in_=vv), [vkey], ["vb"])
                            kkc = prm(f"rw_k_k_{l}")[:, ct:ct + 1]
                            kac = prm(f"rw_k_a_{l}")[:, ct:ct + 1]
                            V(lambda e: e.tensor_scalar(out=sm[:, 0:1], in0=kac, scalar1=-1.0, scalar2=1.0, op0=ALU.mult, op1=ALU.add), ["pp"], ["sm"])
                            V(lambda e: e.tensor_scalar(out=raw[:, 0:T], in0=kf[:], scalar1=kkc, scalar2=None, op0=ALU.mult), ["kf", "pp", "raw"], ["raw"])
                            V(lambda e: e.tensor_tensor(out=t3[:], in0=raw[:, 0:T], in1=raw[:, 0:T], op=ALU.mult), ["raw", "vb"], ["t3"])
                            for tt in range(NT):
                                pb = tt % 2
                                MM(lambda e, tt=tt, pb=pb: e.matmul(pj[:, pb, :], lhsT=prm("bones"), rhs=t3[:, tt * 512:(tt + 1) * 512], start=True, stop=True),
                                   ["pp", "t3"], [f"pj{pb}"])
                                A(lambda e, tt=tt, pb=pb: e.activation(out=lw_tmp[:, tt * 512:(tt + 1) * 512], in_=pj[:, pb, :], func=AF.Sqrt), [f"pj{pb}"], ["t2"])
                            V(lambda e: e.tensor_scalar(out=lw_tmp[:], in0=lw_tmp[:], scalar1=1e-12, scalar2=None, op0=ALU.max), ["t2"], ["t2"])
                            V(lambda e: e.reciprocal(out=lw_tmp[:], in_=lw_tmp[:]), ["t2"], ["t2"])
                            V(lambda e: e.tensor_tensor(out=raw[:, 0:T], in0=raw[:, 0:T], in1=lw_tmp[:], op=ALU.mult), ["raw", "t2"], ["raw"])
                            V(lambda e: e.tensor_scalar(out=t3[:], in0=af[:], scalar1=kac, scalar2=sm[:, 0:1], op0=ALU.mult, op1=ALU.add), ["af", "pp", "sm"], ["t3"])
                            V(lambda e: e.tensor_tensor(out=kf[:], in0=kf[:], in1=t3[:], op=ALU.mult), ["kf", "t3"], ["kf"])
                            V(lambda e: e.tensor_tensor(out=t3[:], in0=rf[:], in1=kf[:], op=ALU.mult), ["rf", "kf"], ["t3"])
                            V(lambda e: e.tensor_scalar(out=t3[:], in0=t3[:], scalar1=prm(f"rw_r_k_{l}")[:, ct:ct + 1], scalar2=None, op0=ALU.mult), ["t3", "pp"], ["t3"])
                            for tt in range(NT):
                                pb = tt % 2
                                MM(lambda e, tt=tt, pb=pb: e.matmul(pj[:, pb, :], lhsT=prm("bones"), rhs=t3[:, tt * 512:(tt + 1) * 512], start=True, stop=True),
                                   ["pp", "t3"], [f"pj{pb}"])
                                V(lambda e, tt=tt, pb=pb: e.tensor_tensor(out=bon[:, tt * 512:(tt + 1) * 512], in0=pj[:, pb, :], in1=vb[:, tt * 512:(tt + 1) * 512], op=ALU.mult),
                                  [f"pj{pb}", "vb"], ["bon"])
                            V(lambda e: e.tensor_tensor(out=af[:], in0=af[:], in1=raw[:, 0:T], op=ALU.mult), ["af", "raw"], ["af"])
                            V(lambda e: e.memset(lw_tmp[:], 1.0), ["t2"], ["t2"])
                            V(lambda e: e.tensor_tensor_scan(out=t3[:], data0=lw_tmp[:], data1=lw[:], initial=0.0, op0=ALU.mult, op1=ALU.add), ["t2", "lw"], ["t3"])
                            t33 = t3[:].rearrange("p (c i) -> p c i", i=64)
                            V(lambda e: e.memset(bst[:, 0:1], 0.0), [], ["bst"])
                            V(lambda e: e.tensor_copy(out=bst[:, 1:NC], in_=t33[:, 0:NC - 1, 63]), ["t3"], ["bst"])
                            V(lambda e: e.tensor_tensor(out=t33, in0=t33, in1=bst[:].unsqueeze(2).to_broadcast([128, NC, 64]), op=ALU.subtract), ["t3", "bst"], ["t3"])
                            A(lambda e: e.activation(out=PC[:], in_=t33[:, :, 63], func=AF.Exp), ["t3"], ["PC"])
                            V(lambda e: e.tensor_tensor(out=lw[:], in0=t3[:], in1=lw[:], op=ALU.subtract), ["t3", "lw"], ["lw"])
                            A(lambda e: e.activation(out=lw[:], in_=lw[:], func=AF.Exp), ["lw"], ["lw"])
                            V(lambda e: e.scalar_tensor_tensor(out=at[:], in0=raw[:, 0:T], scalar=-1.0, in1=lw[:], op0=ALU.mult, op1=ALU.mult), ["raw", "lw"], ["at"])
                            A(lambda e: e.activation(out=lw[:], in_=t3[:], func=AF.Exp), ["t3", "lw", "at"], ["lw"])
                            V(lambda e: e.tensor_tensor(out=rt[:], in0=rf[:], in1=lw[:], op=ALU.mult), ["rf", "lw"], ["rt"])
                            A(lambda e: e.activation(out=t3[:], in_=t3[:], func=AF.Exp, scale=-1.0), ["t3", "rt"], ["t3"])
                            PCb = PC[:].unsqueeze(2).to_broadcast([128, NC, 64])
                            V(lambda e: e.tensor_tensor(out=af[:], in0=af[:], in1=t3[:], op=ALU.mult), ["af", "t3"], ["af"])
                            V(lambda e: e.tensor_copy(out=bt[:], in_=af[:]), ["af"], ["bt"])
                            V(lambda e: e.tensor_tensor(out=bh[:].rearrange("p (c i) -> p c i", i=64), in0=af[:].rearrange("p (c i) -> p c i", i=64), in1=PCb, op=ALU.mult),
                              ["af", "PC"], ["bh"])
                            V(lambda e: e.tensor_tensor(out=kf[:], in0=kf[:], in1=t3[:], op=ALU.mult), ["kf", "t3", "bon"], ["kf"])
                            V(lambda e: e.tensor_copy(out=kt[:], in_=kf[:]), ["kf"], ["kt"])
                            V(lambda e: e.tensor_tensor(out=kh[:].rearrange("p (c i) -> p c i", i=64), in0=kf[:].rearrange("p (c i) -> p c i", i=64), in1=PCb, op=ALU.mult),
                              ["kf", "PC"], ["kh"])
                            P.barrier()
                        import os
                        if int(os.environ.get("RW_STAGE", "9")) >= 2:
                            rw_chunks(ct, sbR, at, bt, kt, rt, bh, kh, vb, gfm, bon, PC, identb2, NC, V, A, MM)
                        P.barrier()

                def rw_chunks(ct, sbR, at, bt, kt, rt, bh, kh, vb, gfm, bon, PC, identb2, NC, V, A, MM):
                    with ExitStack() as eC:
                        def sbC(name, shape, dt):
                            return eC.enter_context(nc.sbuf_tensor(f"rwc{l}_{ct}_{name}", shape, dt))

                        def psC(name, shape, dt=F32):
                            return eC.enter_context(nc.psum_tensor(f"rwc{l}_{ct}_{name}", shape, dt))
                        GTb = sbC("GTb", [128, NC, 64], BF16)
                        Jst = sbC("Jst", [128, NC, 64], F32)
                        CoefTb = sbC("CoefTb", [128, NC, 64], BF16)
                        Yc = sbC("Yc", [128, NC, 64], F32)
                        A01 = sbC("A01", [128, 2, 128], F32)
                        V(lambda e: e.memset(A01[:], 0.0), [], ["A01"])
                        A23 = sbC("A23", [128, 3, 64], BF16)
                        PsPt = sbC("PsPt", [128, 2, 128], F32)
                        Tt = sbC("Tt", [128, 128], F32)
                        Ttb = sbC("Ttb", [128, 64], BF16)
                        tmx = sbC("tmx", [128, 4, 64], BF16)
                        TpX = sbC("TpX", [128, 2, 64], BF16)
                        Tppb = sbC("Tppb", [128, 64], BF16)
                        H = sbC("H", [128, 64], F32)
                        Hb = sbC("Hb", [128, 64], BF16)
                        bankA = (pj[:, 0, :], pj[:, 1, :])
                        bankC = (psC("bankC0", [128, 512]), psC("bankC1", [128, 512]))
                        bankJ = (psC("bankJ0", [128, 512]), psC("bankJ1", [128, 512]))
                        ptx = (psC("ptx0", [128, 1024], BF16), psC("ptx1", [128, 1024], BF16))

                        def V2(fn, r, w):
                            for hh_, ps_ in enumerate(HS):
                                P.op("dve", lambda e, hh_=hh_, ps_=ps_: fn(e, hh_, ps_), reads=[k.format(hh=hh_) for k in r], writes=[k.format(hh=hh_) for k in w])

                        def A2(fn, r, w):
                            for hh_, ps_ in enumerate(HS):
                                P.op("act", lambda e, hh_=hh_, ps_=ps_: fn(e, hh_, ps_), reads=[k.format(hh=hh_) for k in r], writes=[k.format(hh=hh_) for k in w])
                        HS = (slice(0, 64), slice(64, 128))

                        def MT(fn, r, w):
                            P.op("pe", fn, reads=[k.format(hh=hh) for k in r], writes=[k.format(hh=hh) for k in w], mode="t64")
                        maskA = prm("maskA")
                        ident2 = prm("ident2")
                        for c in range(NC):
                            cs = slice(c * 64, (c + 1) * 64)
                            for hh, ps_ in enumerate(HS):
                                for k_, (lt_, rh_) in enumerate(((bt, at), (at, bt), (kt, at), (bt, rt), (kt, rt))):
                                    MT(lambda e, ps_=ps_, hh=hh, k_=k_, lt_=lt_, rh_=rh_: e.matmul(bankA[hh][ps_, k_ * 64:(k_ + 1) * 64], lhsT=lt_[ps_, cs], rhs=rh_[ps_, cs],
                                                                                          start=True, stop=True), ["at", "bt", "kt", "rt"], ["pj{hh}"])
                                for k_, X in enumerate(() if os.environ.get("RW_NOTR") else (at, bh, kh, vb)):
                                    MT(lambda e, ps_=ps_, hh=hh, k_=k_, X=X: e.transpose(out=ptx[hh][ps_, k_ * 64:(k_ + 1) * 64], in_=X[ps_, cs], identity=identb2[ps_, :]),
                                       ["at", "bh", "kh", "vb", "identb2"], ["ptx{hh}"])
                            V2(lambda e, hh, ps_: e.tensor_tensor(out=A01[ps_, :, hh * 64:(hh + 1) * 64], in0=bankA[hh][ps_, 0:128].rearrange("p (a b) -> p a b", b=64),
                                                                  in1=maskA[ps_, 0:128].rearrange("p (a b) -> p a b", b=64), op=ALU.mult),
                               ["pj{hh}", "pp"], ["A01"])
                            V2(lambda e, hh, ps_: e.tensor_tensor(out=A23[ps_].rearrange("p a b -> p (a b)"), in0=bankA[hh][ps_, 128:320], in1=maskA[ps_, 128:320], op=ALU.mult),
                               ["pj{hh}", "pp"], ["A23"])
                            A2(lambda e, hh, ps_: e.activation(out=tmx[ps_].rearrange("p a b -> p (a b)"), in_=ptx[hh][ps_, 0:256], func=AF.Copy), ["ptx{hh}"], ["tmx"])
                            V(lambda e: e.tensor_tensor(out=Tt[:], in0=A01[:, 0, :], in1=prm("ident"), op=ALU.add), ["A01", "pp"], ["Tt"])
                            cur_s, cur_t, ckey = A01[:, 1, :], A01[:, 0, :], "A01"
                            for lev in range(0 if os.environ.get("RW_NOINV") else 5):
                                MM(lambda e, cur_s=cur_s, cur_t=cur_t: e.matmul(bankJ[0][:, 128:256], lhsT=cur_t, rhs=cur_s, start=True, stop=True), [ckey], ["bJ0"])
                                MM(lambda e, cur_s=cur_s, cur_t=cur_t: e.matmul(bankJ[0][:, 256:384], lhsT=cur_s, rhs=cur_t, start=True, stop=True), [ckey], ["bJ0"])
                                A(lambda e: e.activation(out=PsPt[:].rearrange("p a b -> p (a b)"), in_=bankJ[0][:, 128:384], func=AF.Copy), ["bJ0"], ["PsPt"])
                                cur_s, cur_t, ckey = PsPt[:, 0, :], PsPt[:, 1, :], "PsPt"
                                MM(lambda e, cur_s=cur_s: e.matmul(bankJ[0][:, 384:512], lhsT=cur_s, rhs=Tt[:], start=True, stop=True), ["PsPt", "Tt"], ["bJ0"])
                                V(lambda e: e.tensor_tensor(out=Tt[:], in0=bankJ[0][:, 384:512], in1=Tt[:], op=ALU.add), ["bJ0", "Tt"], ["Tt"])
                            if int(os.environ.get("RW_STAGE", "9")) == 2:
                                continue
                            V2(lambda e, hh, ps_: e.tensor_copy(out=Ttb[ps_], in_=Tt[ps_, hh * 64:(hh + 1) * 64]), ["Tt"], ["Ttb"])
                            for hh, ps_ in enumerate(HS):
                                MT(lambda e, ps_=ps_, hh=hh: e.matmul(bankC[hh][ps_, 0:64], lhsT=Ttb[ps_], rhs=tmx[ps_, 0, :], start=True, stop=True), ["Ttb", "tmx"], ["bC{hh}"])
                                MT(lambda e, ps_=ps_, hh=hh: e.matmul(bankC[hh][ps_, 64:128], lhsT=A23[ps_, 0, :], rhs=tmx[ps_, 3, :], start=True, stop=True), ["A23", "tmx"], ["bC{hh}"])
                            A2(lambda e, hh, ps_: e.activation(out=TpX[ps_].rearrange("p a b -> p (a b)"), in_=bankC[hh][ps_, 0:128], func=AF.Copy), ["bC{hh}"], ["TpX"])
                            for hh, ps_ in enumerate(HS):
                                MT(lambda e, ps_=ps_, hh=hh: e.matmul(bankC[hh][ps_, 128:192], lhsT=Ttb[ps_], rhs=TpX[ps_, 1, :], start=True, stop=True), ["Ttb", "TpX"], ["bC{hh}"])
                            A2(lambda e, hh, ps_: e.activation(out=Tppb[ps_], in_=bankC[hh][ps_, 128:192], func=AF.Copy), ["bC{hh}"], ["Tppb"])
                            for hh, ps_ in enumerate(HS):
                                MT(lambda e, ps_=ps_, hh=hh: e.matmul(bankC[hh][ps_, 192:256], lhsT=TpX[ps_, 0, :], rhs=tmx[ps_, 1, :], start=True, stop=True), ["TpX", "tmx"], ["bC{hh}"])
                                MT(lambda e, ps_=ps_, hh=hh: e.matmul(bankC[hh][ps_, 256:320], lhsT=TpX[ps_, 0, :], rhs=A23[ps_, 1, :], start=True, stop=True), ["TpX", "A23"], ["bC{hh}"])
                                MT(lambda e, ps_=ps_, hh=hh: e.matmul(bankJ[hh][ps_, 0:64], lhsT=tmx[ps_, 1, :], rhs=Tppb[ps_], start=True, stop=False), ["tmx", "Tppb"], ["bJ{hh}"])
                                MT(lambda e, ps_=ps_, hh=hh: e.matmul(bankJ[hh][ps_, 0:64], lhsT=tmx[ps_, 2, :], rhs=tmx[ps_, 3, :], start=False, stop=True), ["tmx"], ["bJ{hh}"])
                                MT(lambda e, ps_=ps_, hh=hh: e.matmul(bankJ[hh][ps_, 64:128], lhsT=A23[ps_, 1, :], rhs=Tppb[ps_], start=True, stop=False), ["A23", "Tppb"], ["bJ{hh}"])
                                MT(lambda e, ps_=ps_, hh=hh: e.matmul(bankJ[hh][ps_, 64:128], lhsT=A23[ps_, 2, :], rhs=tmx[ps_, 3, :], start=False, stop=True), ["A23", "tmx"], ["bJ{hh}"])
                            V2(lambda e, hh, ps_, c=c: e.scalar_tensor_tensor(out=GTb[ps_, c, :], in0=ident2[ps_], scalar=PC[ps_, c:c + 1], in1=bankC[hh][ps_, 192:256],
                                                                                op0=ALU.mult, op1=ALU.add), ["bC{hh}", "pp", "PC"], [f"GT{c}"])
                            V2(lambda e, hh, ps_, c=c, cs=cs: e.tensor_tensor(out=CoefTb[ps_, c, :], in0=bankC[hh][ps_, 256:320], in1=rt[ps_, cs], op=ALU.add), ["bC{hh}", "rt"], [f"Cf{c}"])
                            A2(lambda e, hh, ps_, c=c: e.activation(out=Jst[ps_, c, :], in_=bankJ[hh][ps_, 0:64], func=AF.Copy), ["bJ{hh}"], [f"J{c}"])
                            A2(lambda e, hh, ps_, c=c: e.activation(out=Yc[ps_, c, :], in_=bankJ[hh][ps_, 64:128], func=AF.Copy), ["bJ{hh}"], [f"Yc{c}"])
                        if int(os.environ.get("RW_STAGE", "9")) <= 3:
                            return
                        V(lambda e: e.memset(H[:], 0.0), [], ["H"])
                        for rnd in range(4):
                            last = rnd == 3
                            A(lambda e: e.activation(out=Hb[:], in_=H[:], func=AF.Copy), ["H"], ["Hb"])
                            for c in range(NC):
                                for hh, ps_ in enumerate(HS):
                                    if last:
                                        MT(lambda e, ps_=ps_, hh=hh, c=c: e.matmul(bankC[hh][ps_, 320:384], lhsT=CoefTb[ps_, c, :], rhs=Hb[ps_], start=True, stop=True),
                                           [f"Cf{c}", "Hb"], ["bC{hh}"])
                                    MT(lambda e, ps_=ps_, hh=hh, c=c: e.matmul(bankC[hh][ps_, 384:448], lhsT=GTb[ps_, c, :], rhs=Hb[ps_], start=True, stop=True),
                                       [f"GT{c}", "Hb"], ["bC{hh}"])
                                if last:
                                    V2(lambda e, hh, ps_, c=c: e.tensor_tensor(out=Yc[ps_, c, :], in0=bankC[hh][ps_, 320:384], in1=Yc[ps_, c, :], op=ALU.add), ["bC{hh}", f"Yc{c}"], [f"Yc{c}"])
                                V2(lambda e, hh, ps_, c=c: e.tensor_tensor(out=Hb[ps_], in0=bankC[hh][ps_, 384:448], in1=Jst[ps_, c, :], op=ALU.add), ["bC{hh}", f"J{c}"], ["Hb"])
                                if c == NC - 1:
                                    V2(lambda e, hh, ps_, c=c: e.tensor_tensor(out=H[ps_], in0=bankC[hh][ps_, 384:448], in1=Jst[ps_, c, :], op=ALU.add), ["bC{hh}", f"J{c}"], ["H"])
                            if not last:
                                res, rkey = exchange(sbC, "H", H[:], 64)
                                V(lambda e, res=res: e.tensor_copy(out=H[:], in_=res[:]), [rkey], ["H"])
                        ykeys = [f"Yc{c}" for c in range(NC)]
                        jkeys = [f"J{c}" for c in range(NC)]
                        st1 = sbC("st1", [128, NC], F32)
                        st2 = sbC("st2", [128, NC], F32)
                        ynb = sbC("ynb", [128, NC, 64], BF16)
                        yT = sbC("yT", [128, T], F32)
                        y = sbC("y", [128, 1, T], BF16)
                        V(lambda e: e.tensor_reduce(out=st1[:], in_=Yc[:], axis=AX.X, op=ALU.add), ykeys, ["st1"])
                        V(lambda e: e.tensor_scalar(out=st1[:], in0=st1[:], scalar1=1.0 / 64.0, scalar2=None, op0=ALU.mult), ["st1"], ["st1"])
                        V(lambda e: e.tensor_tensor(out=Yc[:], in0=Yc[:], in1=st1[:].unsqueeze(2).to_broadcast([128, NC, 64]), op=ALU.subtract), ykeys + ["st1"], ykeys)
                        V(lambda e: e.tensor_tensor(out=Jst[:], in0=Yc[:], in1=Yc[:], op=ALU.mult), ykeys + jkeys, jkeys)
                        V(lambda e: e.tensor_reduce(out=st2[:], in_=Jst[:], axis=AX.X, op=ALU.add), jkeys, ["st2"])
                        A(lambda e: e.activation(out=st2[:], in_=st2[:], func=AF.Sqrt, scale=1.0 / 64.0, bias=64e-5), ["st2"], ["st2"])
                        V(lambda e: e.reciprocal(out=st2[:], in_=st2[:]), ["st2"], ["st2"])
                        V(lambda e: e.tensor_tensor(out=ynb[:], in0=Yc[:], in1=st2[:].unsqueeze(2).to_broadcast([128, NC, 64]), op=ALU.mult), ykeys + ["st2"], ["ynb"])
                        gnw = prm(f"rw_gn_w_{l}")[:, ct:ct + 1]
                        gnb = prm(f"rw_gn_b_{l}")[:, ct:ct + 1]
                        for c0 in range(0, NC, 4):
                            for c in range(c0, c0 + 4):
                                for hh, ps_ in enumerate(HS):
                                    MT(lambda e, ps_=ps_, hh=hh, c=c, c0=c0: e.transpose(out=ptx[hh][ps_, (c - c0) * 64:(c - c0 + 1) * 64], in_=ynb[ps_, c, :], identity=identb2[ps_, :]),
                                       ["ynb", "identb2"], ["ptx{hh}"])
                            A2(lambda e, hh, ps_, c0=c0: e.activation(out=yT[ps_, c0 * 64:(c0 + 4) * 64], in_=ptx[hh][ps_, 0:256], func=AF.Identity, scale=gnw[ps_], bias=gnb[ps_]),
                               ["ptx{hh}", "pp"], ["yT"])
                        V(lambda e: e.tensor_tensor(out=yT[:], in0=yT[:], in1=bon[:], op=ALU.add), ["yT", "bon"], ["yT"])
                        V(lambda e: e.tensor_tensor(out=y[:, 0, :], in0=yT[:], in1=gfm[:], op=ALU.mult), ["yT", "gfm"], [f"yrw{ct}"])
                        out_proj(sbC, 20 + ct, y, [f"yrw{ct}"], cc0=4 + ct, ncc=1)
                        P.barrier()

                if "rw" in mixers:
                    rw()
                P.barrier()

        for l in range(L):
            with ExitStack() as ex:
                x = ex.enter_context(nc.sbuf_tensor(f"x_{l}", [128, KC, T], F32))
                load_x(xT_d if l == 0 else xd, l == 0)
                if l > 0:
                    ffn(l - 1, 1, f"n2_{l - 1}")
                ffn(l, 0, f"n1_{l}")
                sq = ex.enter_context(nc.sbuf_tensor(f"nsq_{l}", [128, 2, 512], BF16))
                rs = ex.enter_context(nc.sbuf_tensor(f"nrs_{l}", [128, 2, 512], F32))
                ssp = ex.enter_context(nc.psum_tensor(f"nssp_{l}", [128, 512], F32))
                rmsnorm(ex, (sq, rs, ssp), f"nm_{l}", list(range(NT)),
                        lambda kc, tt: xn2[:, kc, HW + tt * 512:HW + (tt + 1) * 512],
                        lambda kc, tt: f"xn2_{kc}_{tt}")
                store_x()
            mixer_phase(l)
            P.barrier()
        xfin_es = es.enter_context(ExitStack())
        x = xfin_es.enter_context(nc.sbuf_tensor("x_fin", [128, KC, T], F32))
        load_x(xd, False)
        ffn(L - 1, 1, f"n2_{L - 1}")

        with ExitStack() as es2:
            ob = es2.enter_context(nc.sbuf_tensor("o_ob", [128, 2, KC, 512], F32))
            sq = es2.enter_context(nc.sbuf_tensor("o_sq", [128, 2, 512], BF16))
            rs = es2.enter_context(nc.sbuf_tensor("o_rs", [128, 2, 512], F32))
            ssp = es2.enter_context(nc.psum_tensor("o_ssp", [128, 512], F32))
            for tt in range(NT):
                b = tt % 2
                rmsnorm(es2, (sq, rs, ssp), "nf", [tt], lambda kc, t2: ob[:, b, kc, :], lambda kc, t2: f"ob{b}_{kc}")
                P.op("sp", lambda e, b=b, tt=tt: e.dma_start(out=out_d[:, :, tt * 512:(tt + 1) * 512], in_=ob[:, b, :, :]),
                     reads=[f"ob{b}_{kc}" for kc in range(KC)], writes=[f"out{tt}"], lane=f"out{b}")
            P.final_wait("sp")
        print("instr counts", P.cnt, "waits", P.nwaits)
    return nc


def prep_weights(inp, L):
    wgu = np.empty((L, 2, NJ, 128, 2, KC, 128), np.float32)
    wd = np.empty((L, 2, KC, 128, NJ, 128), np.float32)
    named = {"ffn1_w_gate": inp["ffn1_w_gate"], "ffn1_w_up": inp["ffn1_w_up"], "ffn1_w_down": inp["ffn1_w_down"],
             "ffn2_w_gate": inp["ffn2_w_gate"], "ffn2_w_up": inp["ffn2_w_up"], "ffn2_w_down": inp["ffn2_w_down"]}
    for f, pre in enumerate(("ffn1", "ffn2")):
        for gi, nm in enumerate(("w_gate", "w_up")):
            w = np.asarray(named[f"{pre}_{nm}"], np.float32)[:L]
            w = w.reshape(L, KC, 128, NJ, 128)
            wgu[:, f, :, :, gi] = w.transpose(0, 3, 2, 1, 4)
        w = np.asarray(named[f"{pre}_w_down"], np.float32)[:L]
        w = w.reshape(L, NJ, 128, KC, 128)
        wd[:, f] = w.transpose(0, 3, 2, 1, 4)
    win = np.ascontiguousarray(np.asarray(inp["w_in"], np.float32)[:L].reshape(L, KC, 128, NIN).transpose(0, 2, 1, 3))
    wout = np.ascontiguousarray(np.asarray(inp["w_out"], np.float32)[:L].reshape(L, KC, 128, D).transpose(0, 2, 1, 3))
    return {"wgu": wgu.reshape(L * 2 * NJ, 128, 2 * KC * 128), "wd": wd.reshape(L * 2 * KC, 128, NJ * 128),
            "win": win, "wout": wout}


def run(inp, T, L, mixers=()):
    x = np.asarray(inp["x"], np.float32)
    B, S, _ = x.shape
    nseg = S // T
    ncores = B * nseg
    assert ncores == 8
    wts = prep_weights(inp, L)
    in_maps = []
    offs = None
    for c in range(ncores):
        b, s = divmod(c, nseg)
        xs = x[b, s * T:(s + 1) * T, :]
        xT = np.ascontiguousarray(xs.T.reshape(KC, 128, T).transpose(1, 0, 2))
        ppa, ppla, offs = pack_small(inp, L, s)
        m = {"xT": xT, "pp": ppa, "ppl": ppla}
        m.update(wts)
        in_maps.append(m)
    nc = build(T, L, offs, in_maps[0]["pp"].shape[1], in_maps[0]["ppl"].shape[2], mixers)
    print("pp cols", in_maps[0]["pp"].shape, in_maps[0]["ppl"].shape)
    res = run_bass_kernel_spmd(nc, in_maps, core_ids=list(range(ncores)), trace=bool(os.environ.get("KTRACE")))
    if os.environ.get("KTRACE"):
        print("EXEC_TIME_NS", res.exec_time_ns)
    out = np.empty((B, S, D), np.float32)
    for c in range(ncores):
        b, s = divmod(c, nseg)
        oT = np.asarray(res.results[c]["outT"], np.float32)
        out[b, s * T:(s + 1) * T, :] = oT.transpose(2, 1, 0).reshape(T, D)
    return out


def kernel(**inputs):
    return run(inputs, 2048, L_FULL, mixers=("gla", "lru", "rw", "ssd"))
```

```python
import os
import numpy as np
from contextlib import ExitStack
import concourse.bass as bass
import concourse.mybir as mybir
from concourse.bass_utils import run_bass_kernel_spmd

F32 = mybir.dt.float32
BF16 = mybir.dt.bfloat16
AF = mybir.ActivationFunctionType
ALU = mybir.AluOpType
AX = mybir.AxisListType

D = 1024
KC = 8
DFF = 2816
NJ = 22
NIN = 3156
HW = 4
L_FULL = 4
NORM_EPS = 1e-6


class Prog:
    def __init__(self, nc, es):
        self.nc = nc
        self.es = es
        self.streams = {
            "pe": (nc.tensor, 1, 20000),
            "dve": (nc.vector, 1, 20000),
            "act": (nc.scalar, 1, 20000),
            "pool": (nc.gpsimd, 1, 20000),
            "sp": (nc.sync, 16, 1500),
            "poolq": (nc.gpsimd, 16, 1500),
            "cc": (nc.gpsimd, 1, 20000),
        }
        self.issuer = {"pe": "pe", "dve": "dve", "act": "act", "pool": "pool", "sp": "sp",
                       "poolq": "pool", "cc": "pool"}
        self.sems = {s: [] for s in self.streams}
        self.cnt = {s: 0 for s in self.streams}
        self.waited = {}
        self.lastw = {}
        self.readers = {}
        self.same_engine_sync = os.environ.get("KSYNC", "1") == "1"
        self.nwaits = 0

    def _sem(self, stream, epoch):
        lst = self.sems[stream]
        while len(lst) <= epoch:
            lst.append(self.es.enter_context(self.nc.semaphore(f"s_{stream}_{len(lst)}")))
        return lst[epoch]

    def _wait(self, issuer, stream, seq):
        key = (issuer, stream)
        if self.waited.get(key, 0) >= seq:
            return
        self.waited[key] = seq
        eng, inc, cap = self.streams[stream]
        epoch = (seq - 1) // cap
        val = ((seq - 1) % cap + 1) * inc
        self.streams[issuer][0].wait_ge(self._sem(stream, epoch), val)
        self.nwaits += 1

    def op(self, stream, fn, reads=(), writes=(), lane=None, mode="full"):
        if stream == "pe":
            if getattr(self, "pe_mode", "full") != mode and self.cnt["pe"] > 0:
                self._wait("pe", "pe", self.cnt["pe"])
            self.pe_mode = mode
        if lane is not None:
            base = stream
            stream = f"{base}.{lane}"
            if stream not in self.streams:
                self.streams[stream] = self.streams[base]
                self.issuer[stream] = self.issuer[base]
                self.sems[stream] = []
                self.cnt[stream] = 0
        issuer = self.issuer[stream]
        deps = set()
        for k in reads:
            if k in self.lastw:
                deps.add(self.lastw[k])
        for k in writes:
            if k in self.lastw:
                deps.add(self.lastw[k])
            for r in self.readers.get(k, ()):
                deps.add(r)
        for (s, q) in sorted(deps):
            if s == stream and (stream == "pe" or not self.same_engine_sync):
                continue
            if s == stream and stream in ("sp", "poolq"):
                pass
            self._wait(issuer, s, q)
        eng, inc, cap = self.streams[stream]
        ins = fn(eng)
        self.cnt[stream] += 1
        seq = self.cnt[stream]
        ins.then_inc(self._sem(stream, (seq - 1) // cap), inc)
        me = (stream, seq)
        for k in reads:
            self.readers.setdefault(k, []).append(me)
        for k in writes:
            self.lastw[k] = me
            self.readers[k] = []
        return me

    def barrier(self):
        for issuer in ("pe", "dve", "act", "pool", "sp"):
            for s in list(self.streams):
                if self.cnt[s] > 0:
                    self._wait(issuer, s, self.cnt[s])

    def final_wait(self, issuer="sp"):
        for s in list(self.streams):
            if self.cnt[s] > 0:
                self._wait(issuer, s, self.cnt[s])


def chan_pp(v, ntile):
    return np.ascontiguousarray(np.asarray(v, np.float32).reshape(ntile, 128).T)


def pack_small(inp, L, rank):
    cols = []
    offs = {}
    pos = [0]
    lcols = [[] for _ in range(L)]
    lpos = [0] * L

    def add(name, a, layer=None):
        a = np.asarray(a, np.float32)
        assert a.ndim == 2 and a.shape[0] <= 128, (name, a.shape)
        buf = np.zeros((128, a.shape[1]), np.float32)
        buf[: a.shape[0]] = a
        if layer is None:
            offs[name] = ("C", pos[0], a.shape[1])
            pos[0] += a.shape[1]
            cols.append(buf)
        else:
            offs[name] = ("L", lpos[layer], a.shape[1])
            lpos[layer] += a.shape[1]
            lcols[layer].append(buf)

    ident = np.eye(128, dtype=np.float32)
    add("ident", ident)
    add("ones", np.ones((128, 128), np.float32))
    selprev = np.zeros((128, 4), np.float32)
    if rank > 0:
        selprev[:, rank - 1] = 1.0
    add("selprev", selprev)
    for l in range(L):
        add(f"n1_{l}", chan_pp(inp["ffn1_norm"][l], 8))
        add(f"nm_{l}", chan_pp(inp["mix_norm"][l], 8))
        add(f"n2_{l}", chan_pp(inp["ffn2_norm"][l], 8))
    add("nf", chan_pp(inp["final_norm"], 8))
    jj = np.arange(128)
    causal = (jj[:, None] <= jj[None, :]).astype(np.float32)
    add("mask4", np.tile(causal, (1, 4)))
    hm = np.zeros((128, 4), np.float32)
    for h in range(4):
        hm[h * 32:(h + 1) * 32, h] = 1.0
    add("hm", hm)
    add("bm", np.repeat(hm, 64, axis=1))
    i64 = np.arange(128) % 64
    j64 = np.arange(64)
    m_lt = (i64[:, None] < j64[None, :]).astype(np.float32)
    m_gt = (j64[None, :] < i64[:, None]).astype(np.float32)
    m_le = (i64[:, None] <= j64[None, :]).astype(np.float32)
    add("maskA", np.concatenate([m_lt, m_gt, m_lt, m_le, m_le], axis=1))
    add("ident2", (i64[:, None] == j64[None, :]).astype(np.float32))
    p128 = np.arange(128)
    add("bones", (p128[:, None] // 64 == p128[None, :] // 64).astype(np.float32))
    for l in range(L):
        mu = np.asarray(inp["rw_mu"][l], np.float32)
        add(f"rw_mur_{l}", chan_pp(mu[0:256], 2), layer=l)
        add(f"rw_muk_{l}", chan_pp(mu[256:512], 2), layer=l)
        add(f"rw_muv_{l}", chan_pp(mu[512:768], 2), layer=l)
        add(f"rw_musw_{l}", mu[768:784][:, None], layer=l)
        add(f"rw_musa_{l}", mu[784:800][:, None], layer=l)
        add(f"rw_musg_{l}", mu[800:832][:, None], layer=l)
        rwn = {"w0": inp["rw_w0"], "a0": inp["rw_a0"], "k_k": inp["rw_k_k"], "k_a": inp["rw_k_a"], "gn_w": inp["rw_gn_w"], "gn_b": inp["rw_gn_b"]}
        for nm in ("w0", "a0", "k_k", "k_a", "gn_w", "gn_b"):
            add(f"rw_{nm}_{l}", chan_pp(rwn[nm][l], 2), layer=l)
        add(f"rw_r_k_{l}", chan_pp(np.asarray(inp["rw_r_k"][l], np.float32).reshape(256), 2), layer=l)
        add(f"rw_w2_{l}", np.asarray(inp["rw_w2"][l], np.float32), layer=l)
        add(f"rw_a2_{l}", np.asarray(inp["rw_a2"][l], np.float32), layer=l)
        add(f"rw_g2_{l}", np.asarray(inp["rw_g2"][l], np.float32), layer=l)
        if l > 0:
            add(f"rw_v0_{l}", chan_pp(inp["rw_v0"][l - 1], 2), layer=l)
            v1 = np.asarray(inp["rw_v1"][l - 1], np.float32)
            add(f"rw_v1_{l}", np.concatenate([v1[0:128], v1[128:256]], axis=1), layer=l)
            add(f"rw_v2_{l}", np.asarray(inp["rw_v2"][l - 1], np.float32), layer=l)
        else:
            add(f"rw_v0_{l}", np.zeros((128, 2), np.float32), layer=l)
            add(f"rw_v1_{l}", np.zeros((128, 16), np.float32), layer=l)
            add(f"rw_v2_{l}", np.zeros((8, 256), np.float32), layer=l)
    add("utri", causal)
    add("negmask", (causal - 1.0) * 30000.0)
    for l in range(L):
        cw = np.asarray(inp["ssd_conv_w"][l], np.float32)
        add(f"ssd_cw_{l}", np.concatenate([cw[:, ct * 128:(ct + 1) * 128].T for ct in range(6)], axis=1), layer=l)
        add(f"ssd_cb_{l}", chan_pp(inp["ssd_conv_b"][l], 6), layer=l)
        add(f"ssd_dtb_{l}", np.tile(np.asarray(inp["ssd_dt_bias"][l], np.float32)[None, :], (128, 1)), layer=l)
        add(f"ssd_alog_{l}", np.tile(np.asarray(inp["ssd_a_log"][l], np.float32)[None, :], (128, 1)), layer=l)
        add(f"ssd_d_{l}", np.tile(np.asarray(inp["ssd_d"][l], np.float32)[None, :], (128, 1)), layer=l)
        add(f"ssd_nw_{l}", chan_pp(inp["ssd_norm"][l], 2), layer=l)
    for l in range(L):
        add(f"gla_aup_{l}", np.asarray(inp["gla_alpha_up"][l], np.float32), layer=l)
        add(f"gla_ab_{l}", chan_pp(inp["gla_alpha_bias"][l], 1), layer=l)
        add(f"gla_nw_{l}", np.tile(np.asarray(inp["gla_norm"][l], np.float32)[None, :], (128, 1)), layer=l)
    for l in range(L):
        cw = np.asarray(inp["lru_conv_w"][l], np.float32)
        add(f"lru_cw_{l}", np.concatenate([cw[:, ct * 128:(ct + 1) * 128].T for ct in range(2)], axis=1), layer=l)
        add(f"lru_cb_{l}", chan_pp(inp["lru_conv_b"][l], 2), layer=l)
        add(f"lru_ba_{l}", chan_pp(inp["lru_b_a"][l], 2), layer=l)
        add(f"lru_bx_{l}", chan_pp(inp["lru_b_x"][l], 2), layer=l)
        add(f"lru_lam_{l}", chan_pp(inp["lru_lambda"][l], 2), layer=l)
        for nm, key in (("wa", "lru_w_a"), ("wx", "lru_w_x")):
            w = np.asarray(inp[key][l], np.float32)
            for ct in range(2):
                bd = np.zeros((128, 128), np.float32)
                for nn in range(2):
                    bd[nn * 64:(nn + 1) * 64, nn * 64:(nn + 1) * 64] = w[2 * ct + nn]
                add(f"lru_{nm}_{l}_{ct}", bd, layer=l)
    assert len(set(lpos)) == 1, lpos
    ppl = np.stack([np.concatenate(c, axis=1) for c in lcols], axis=0)
    return np.concatenate(cols, axis=1), ppl, offs


def build(T, L, offs, npp, nppl, mixers=()):
    nc = bass.Bass("TRN2", target_bir_lowering=False)
    NT = T // 512
    FG = min(T, 1024)
    xT_d = nc.dram_tensor("xT", [128, KC, T], F32, kind="ExternalInput").ap()
    out_d = nc.dram_tensor("outT", [128, KC, T], F32, kind="ExternalOutput").ap()
    pp_d = nc.dram_tensor("pp", [128, npp], F32, kind="ExternalInput").ap()
    ppl_d = nc.dram_tensor("ppl", [L, 128, nppl], F32, kind="ExternalInput").ap()
    wgu_d = nc.dram_tensor("wgu", [L * 2 * NJ, 128, 2 * KC * 128], F32, kind="ExternalInput").ap()
    wd_d = nc.dram_tensor("wd", [L * 2 * KC, 128, NJ * 128], F32, kind="ExternalInput").ap()
    win_d = nc.dram_tensor("win", [L, 128, KC, NIN], F32, kind="ExternalInput").ap()
    wout_d = nc.dram_tensor("wout", [L, 128, KC, D], F32, kind="ExternalInput").ap()
    uid = [0]

    with ExitStack() as es:
        P = Prog(nc, es)

        def sb(name, shape, dt):
            return es.enter_context(nc.sbuf_tensor("s_" + name, shape, dt))

        def ps(name, shape, dt=F32):
            return es.enter_context(nc.psum_tensor(name, shape, dt))

        xd = nc.dram_tensor("xd_scratch", [128, KC, T], F32, kind="Internal").ap()
        vfirst_d = nc.dram_tensor("vfirst_scratch", [128, 2, T], F32, kind="Internal").ap()
        xn2 = sb("xn2", [128, KC, HW + T], BF16)
        x = None
        pp = sb("pp", [128, npp], F32)
        ppl = sb("ppl", [128, nppl], F32)
        onesb = sb("onesb", [128, 128], BF16)

        def prm(name):
            kind, o, w = offs[name]
            return (pp if kind == "C" else ppl)[:, o:o + w]

        P.op("sp", lambda e: e.dma_start(out=pp[:], in_=pp_d[:, :]), writes=["pp"])
        P.barrier()

        def load_x(src, first):
            for kc in range(KC):
                P.op("sp", lambda e, kc=kc: e.dma_start(out=x[:, kc, :], in_=src[:, kc, :]),
                     reads=([] if first else [f"xd{kc}_{t}" for t in range(NT)]), writes=[f"x{kc}_{t}" for t in range(NT)])
            P.barrier()

        def store_x():
            for kc in range(KC):
                P.op("sp", lambda e, kc=kc: e.dma_start(out=xd[:, kc, :], in_=x[:, kc, :]),
                     reads=[f"x{kc}_{t}" for t in range(NT)], writes=[f"xd{kc}_{t}" for t in range(NT)])
            P.barrier()
        P.op("dve", lambda e: e.tensor_copy(out=onesb[:], in_=prm("ones")), reads=["pp"], writes=["onesb"])

        def rmsnorm(es2, pool, wname, tok_tiles, dst_fn, dst_key_fn):
            sq, rs, ssp = pool
            for i, tt in enumerate(tok_tiles):
                c0 = tt * 512
                for kc in range(KC):
                    b = (i * KC + kc) % 2
                    P.op("act", lambda e, kc=kc, b=b: e.activation(out=sq[:, b, :], in_=x[:, kc, c0:c0 + 512],
                                                                    func=AF.Square),
                         reads=[f"x{kc}_{tt}"], writes=[f"sq{b}"])
                    P.op("pe", lambda e, kc=kc, b=b: e.matmul(ssp[:, :], lhsT=onesb[:, :], rhs=sq[:, b, :],
                                                               start=(kc == 0), stop=(kc == KC - 1)),
                         reads=[f"sq{b}", "onesb"], writes=["ssp"])
                P.op("act", lambda e: e.activation(out=rs[:, 0, :], in_=ssp[:, :], func=AF.Sqrt,
                                                   scale=1.0 / D, bias=NORM_EPS),
                     reads=["ssp"], writes=["rs0"])
                P.op("dve", lambda e: e.reciprocal(out=rs[:, 1, :], in_=rs[:, 0, :]), reads=["rs0"], writes=["rs1"])
                for kc in range(KC):
                    P.op("dve", lambda e, kc=kc: e.scalar_tensor_tensor(
                        out=dst_fn(kc, tt), in0=x[:, kc, c0:c0 + 512], scalar=prm(wname)[:, kc:kc + 1],
                        in1=rs[:, 1, :], op0=ALU.mult, op1=ALU.mult),
                        reads=[f"x{kc}_{tt}", "rs1", "pp"], writes=[dst_key_fn(kc, tt)])

        def ffn(l, f, wname):
            with ExitStack() as es2:
                def sb2(name, shape, dt):
                    return es2.enter_context(nc.sbuf_tensor(f"{name}_{l}_{f}", shape, dt))

                def ps2(name, shape, dt=F32):
                    return es2.enter_context(nc.psum_tensor(f"{name}_{l}_{f}", shape, dt))
                xn = sb2("f_xn", [128, KC, FG], BF16)
                h = sb2("f_h", [128, NJ, FG], BF16)
                wgu = sb2("f_wgu", [128, 2, 2 * KC * 128], BF16)
                wd = sb2("f_wd", [128, 2, NJ * 128], BF16)
                sq = sb2("f_sq", [128, 2, 512], BF16)
                rs = sb2("f_rs", [128, 2, 512], F32)
                sg = sb2("f_sg", [128, 2, 512], F32)
                ssp = ps2("f_ssp", [128, 512])
                pg = ps2("f_pg", [128, 2, 512])
                pu = ps2("f_pu", [128, 2, 512])
                pd = ps2("f_pd", [128, 2, 512])
                nsub = FG // 512
                for g in range(T // FG):
                    tiles = [g * nsub + s for s in range(nsub)]
                    rmsnorm(es2, (sq, rs, ssp), wname, tiles,
                            lambda kc, tt: xn[:, kc, (tt - g * nsub) * 512:(tt - g * nsub + 1) * 512],
                            lambda kc, tt: f"xn{kc}_{tt - g * nsub}")
                    cnt = 0
                    for j in range(NJ):
                        wb = j % 2
                        row = (l * 2 + f) * NJ + j
                        P.op("poolq", lambda e, wb=wb, row=row: e.dma_start(out=wgu[:, wb, :], in_=wgu_d[row, :, :]),
                             writes=[f"wgu{wb}"], lane=f"wgu{wb}")
                        for s in range(nsub):
                            pb = cnt % 2
                            cnt += 1
                            for gi, pt in ((0, pg), (1, pu)):
                                for kc in range(KC):
                                    o = (gi * KC + kc) * 128
                                    P.op("pe", lambda e, pt=pt, o=o, kc=kc, s=s, pb=pb, wb=wb: e.matmul(
                                        pt[:, pb, :], lhsT=wgu[:, wb, o:o + 128], rhs=xn[:, kc, s * 512:(s + 1) * 512],
                                        start=(kc == 0), stop=(kc == KC - 1)),
                                        reads=[f"wgu{wb}", f"xn{kc}_{s}"], writes=[f"p{gi}_{pb}"])
                            P.op("act", lambda e, pb=pb: e.activation(out=sg[:, pb, :], in_=pg[:, pb, :], func=AF.Silu),
                                 reads=[f"p0_{pb}"], writes=[f"sg{pb}"])
                            P.op("dve", lambda e, pb=pb, j=j, s=s: e.tensor_tensor(
                                out=h[:, j, s * 512:(s + 1) * 512], in0=pu[:, pb, :], in1=sg[:, pb, :], op=ALU.mult),
                                reads=[f"p1_{pb}", f"sg{pb}"], writes=[f"h{j}_{s}"])
                    cnt = 0
                    for m in range(KC):
                        wb = m % 2
                        row = (l * 2 + f) * KC + m
                        P.op("poolq", lambda e, wb=wb, row=row: e.dma_start(out=wd[:, wb, :], in_=wd_d[row, :, :]),
                             writes=[f"wd{wb}"], lane=f"wd{wb}")
                        for s in range(nsub):
                            pb = cnt % 2
                            cnt += 1
                            tt = g * nsub + s
                            for j in range(NJ):
                                P.op("pe", lambda e, j=j, s=s, pb=pb, wb=wb: e.matmul(
                                    pd[:, pb, :], lhsT=wd[:, wb, j * 128:(j + 1) * 128], rhs=h[:, j, s * 512:(s + 1) * 512],
                                    start=(j == 0), stop=(j == NJ - 1)),
                                    reads=[f"wd{wb}", f"h{j}_{s}"], writes=[f"pd{pb}"])
                            P.op("dve", lambda e, m=m, pb=pb, tt=tt: e.scalar_tensor_tensor(
                                out=x[:, m, tt * 512:(tt + 1) * 512], in0=pd[:, pb, :], scalar=0.5,
                                in1=x[:, m, tt * 512:(tt + 1) * 512], op0=ALU.mult, op1=ALU.add),
                                reads=[f"pd{pb}", f"x{m}_{tt}"], writes=[f"x{m}_{tt}"])
                P.barrier()


        def exchange(sbm, tag, src_ap, W):
            uid[0] += 1
            u = uid[0]
            bounce = nc.dram_tensor(f"bnc_{u}", [128, W], F32, kind="Internal").ap()
            gath = nc.dram_tensor(f"gth_{u}", [512, W], F32, kind="Internal").ap()
            cache = sbm.__dict__.setdefault("xcache", {})
            if W not in cache:
                cache[W] = (sbm(f"hg_{u}", [128, 4, W], F32), sbm(f"hr_{u}", [128, W], F32), u)
            hg, res, u0 = cache[W]
            P.op("poolq", lambda e: e.dma_start(out=bounce[:, :], in_=src_ap, allow_slow_non_contiguous=True), reads=[tag], writes=[f"bnc{u}"])
            P.op("cc", lambda e: e.collective_compute("AllGather", ALU.bypass, replica_groups=[[0, 1, 2, 3], [4, 5, 6, 7]],
                                                      ins=[bounce[:, :]], outs=[gath[:, :]]),
                 reads=[f"bnc{u}"], writes=[f"gth{u}"])
            P.op("poolq", lambda e: e.dma_start(out=hg[:], in_=gath.rearrange("(r p) c -> p r c", p=128), allow_slow_non_contiguous=True),
                 reads=[f"gth{u}"], writes=[f"hg{u0}"])
            sel = prm("selprev")
            P.op("dve", lambda e: e.tensor_scalar(out=res[:], in0=hg[:, 0, :], scalar1=sel[:, 0:1], scalar2=None, op0=ALU.mult),
                 reads=[f"hg{u0}", "pp"], writes=[f"hr{u0}"])
            for j in range(1, 4):
                P.op("dve", lambda e, j=j: e.scalar_tensor_tensor(out=res[:], in0=hg[:, j, :], scalar=sel[:, j:j + 1], in1=res[:],
                                                                   op0=ALU.mult, op1=ALU.add),
                     reads=[f"hg{u0}", f"hr{u0}", "pp"], writes=[f"hr{u0}"])
            return res, f"hr{u0}"

        def mixer_phase(l):
            P.op("sp", lambda e: e.dma_start(out=ppl[:], in_=ppl_d[l, :, :]), writes=["pp"], lane="ppl")
            with ExitStack() as esm:
                def sbm(name, shape, dt):
                    return esm.enter_context(nc.sbuf_tensor(f"m{l}_{name}", shape, dt))

                def psm(name, shape, dt=F32):
                    return esm.enter_context(nc.psum_tensor(f"m{l}_{name}", shape, dt))
                pj = psm("pj", [128, 2, 512])
                po = pj
                hst = sbm("hst", [128, KC, HW], F32)
                P.op("dve", lambda e: e.tensor_copy(out=hst[:], in_=xn2[:, :, T:T + HW]),
                     reads=[f"xn2_{kc}_{NT - 1}" for kc in range(KC)], writes=["hst"])
                hres, hkey = exchange(sbm, "hst", hst[:].rearrange("p a b -> p (a b)"), KC * HW)
                P.op("dve", lambda e: e.tensor_copy(out=xn2[:, :, 0:HW], in_=hres[:].rearrange("p (a b) -> p a b", b=HW)),
                     reads=[hkey], writes=["xn2_halo"])
                xn2_keys = [f"xn2_{kc}_{tt}" for kc in range(KC) for tt in range(NT)]

                pcount = [0]

                def proj_fm(wbuf, wkey, col, M, evac, halo_evac=None):
                    for tt in range(NT):
                        pb = pcount[0] % 2
                        pcount[0] += 1
                        for kc in range(KC):
                            P.op("pe", lambda e, kc=kc, pb=pb, tt=tt: e.matmul(
                                pj[0:M, pb, :], lhsT=wbuf[:, kc, col:col + M], rhs=xn2[:, kc, HW + tt * 512:HW + (tt + 1) * 512],
                                start=(kc == 0), stop=(kc == KC - 1)),
                                reads=[wkey, f"xn2_{kc}_{tt}"], writes=[f"pj{pb}"])
                        evac(pj[0:M, pb, :], tt, [f"pj{pb}"])
                    if halo_evac is not None:
                        pb = pcount[0] % 2
                        pcount[0] += 1
                        for kc in range(KC):
                            P.op("pe", lambda e, kc=kc, pb=pb: e.matmul(
                                pj[0:M, pb, 0:HW], lhsT=wbuf[:, kc, col:col + M], rhs=xn2[:, kc, 0:HW],
                                start=(kc == 0), stop=(kc == KC - 1)),
                                reads=[wkey, "xn2_halo"], writes=[f"pj{pb}"])
                        halo_evac(pj[0:M, pb, 0:HW], [f"pj{pb}"])

                def load_win(sbx, name, c0, n):
                    wb = sbx(name, [128, KC, n], BF16)
                    P.op("poolq", lambda e: e.dma_start(out=wb[:], in_=win_d[l, :, :, c0:c0 + n]), writes=[name], lane="win")
                    return wb

                def out_proj(sbx, mi, y, ykeys, cc0=None, ncc=2):
                    if cc0 is None:
                        cc0 = 2 * mi
                    wo = sbx(f"wo{mi}", [128, ncc, D], BF16)
                    ost = sbx(f"ost{mi}", [128, 2, 512], F32)
                    P.op("poolq", lambda e: e.dma_start(out=wo[:], in_=wout_d[l, :, cc0:cc0 + ncc, :]), writes=[f"wo{mi}"], lane="wo")
                    cnt = 0
                    for dm in range(KC):
                        for tt in range(NT):
                            pb = cnt % 2
                            cnt += 1
                            for c2 in range(ncc):
                                P.op("pe", lambda e, c2=c2, dm=dm, tt=tt, pb=pb: e.matmul(
                                    po[:, pb, :], lhsT=wo[:, c2, dm * 128:(dm + 1) * 128], rhs=y[:, c2, tt * 512:(tt + 1) * 512],
                                    start=(c2 == 0), stop=(c2 == ncc - 1)),
                                    reads=[f"wo{mi}"] + ykeys, writes=[f"pj{pb}"])
                            P.op("act", lambda e, pb=pb: e.activation(out=ost[:, pb, :], in_=po[:, pb, :], func=AF.Copy),
                                 reads=[f"pj{pb}"], writes=[f"ost{mi}_{pb}"])
                            P.op("poolq", lambda e, dm=dm, tt=tt, pb=pb: e.dma_start(out=xd[:, dm, tt * 512:(tt + 1) * 512], in_=ost[:, pb, :], accum_op=ALU.add),
                                 reads=[f"ost{mi}_{pb}", f"xd{dm}_{tt}"], writes=[f"xd{dm}_{tt}"], lane=f"xacc{pb}")

                def lru():
                    with ExitStack() as e1:
                        def sb1(name, shape, dt):
                            return e1.enter_context(nc.sbuf_tensor(f"lru{l}_{name}", shape, dt))
                        wl = load_win(sb1, f"lru{l}_w", 784, 512)
                        y = sb1("y", [128, 2, T], BF16)
                        pr = e1.enter_context(nc.psum_tensor(f"lru{l}_pr", [128, 2, 512], F32))
                        for ct in range(2):
                            with ExitStack() as e2:
                                def sb3(name, shape, dt):
                                    return e2.enter_context(nc.sbuf_tensor(f"lru{l}_{ct}_{name}", shape, dt))
                                xb = sb3("xb", [128, HW + T], F32)
                                gt = sb3("gt", [128, T], BF16)
                                xc = sb3("xc", [128, T], F32)
                                rr = sb3("rr", [128, T], F32)
                                ii = sb3("ii", [128, T], F32)
                                tmp = sb3("tmp", [128, T], F32)
                                hh = sb3("hh", [128, T], F32)
                                sm = sb3("sm", [128, 8], F32)
                                hin = sb3("hin", [128, 1], F32)
                                proj_fm(wl, f"lru{l}_w", ct * 128, 128,
                                        lambda p_ap, tt, rk: P.op("act", lambda e: e.activation(out=xb[:, HW + tt * 512:HW + (tt + 1) * 512], in_=p_ap, func=AF.Copy),
                                                                  reads=rk, writes=["xb"]),
                                        lambda p_ap, rk: P.op("act", lambda e: e.activation(out=xb[:, 0:HW], in_=p_ap, func=AF.Copy),
                                                              reads=rk, writes=["xb"]))
                                proj_fm(wl, f"lru{l}_w", 256 + ct * 128, 128,
                                        lambda p_ap, tt, rk: P.op("act", lambda e: e.activation(out=gt[:, tt * 512:(tt + 1) * 512], in_=p_ap, func=AF.Gelu),
                                                                  reads=rk, writes=["gt"]))
                                cw = prm(f"lru_cw_{l}")[:, ct * 4:(ct + 1) * 4]
                                cb = prm(f"lru_cb_{l}")[:, ct:ct + 1]
                                P.op("dve", lambda e: e.tensor_scalar(out=xc[:], in0=xb[:, 1:1 + T], scalar1=cw[:, 0:1], scalar2=cb,
                                                                      op0=ALU.mult, op1=ALU.add), reads=["xb", "pp"], writes=["xc"])
                                for k in range(1, 4):
                                    P.op("dve", lambda e, k=k: e.scalar_tensor_tensor(out=xc[:], in0=xb[:, k + 1:k + 1 + T], scalar=cw[:, k:k + 1],
                                                                                      in1=xc[:], op0=ALU.mult, op1=ALU.add),
                                         reads=["xb", "xc", "pp"], writes=["xc"])
                                for (dst, dkey, wnm, bnm) in ((rr, "rr", "wa", "ba"), (ii, "ii", "wx", "bx")):
                                    wmat = prm(f"lru_{wnm}_{l}_{ct}")
                                    bcol = prm(f"lru_{bnm}_{l}")[:, ct:ct + 1]
                                    for tt in range(NT):
                                        pb = tt % 2
                                        P.op("pe", lambda e, tt=tt, pb=pb, wmat=wmat: e.matmul(pr[:, pb, :], lhsT=wmat, rhs=xc[:, tt * 512:(tt + 1) * 512],
                                                                                               start=True, stop=True),
                                             reads=["xc", "pp"], writes=[f"pr{pb}"])
                                        P.op("act", lambda e, tt=tt, pb=pb, dst=dst, bcol=bcol: e.activation(
                                            out=dst[:, tt * 512:(tt + 1) * 512], in_=pr[:, pb, :], func=AF.Sigmoid, bias=bcol),
                                            reads=[f"pr{pb}", "pp"], writes=[dkey])
                                lam = prm(f"lru_lam_{l}")[:, ct:ct + 1]
                                P.op("act", lambda e: e.activation(out=sm[:, 0:1], in_=lam, func=AF.Exp, scale=-1.0), reads=["pp"], writes=["sm"])
                                P.op("act", lambda e: e.activation(out=sm[:, 1:2], in_=sm[:, 0:1], func=AF.Ln, bias=1.0), reads=["sm"], writes=["sm"])
                                P.op("dve", lambda e: e.tensor_scalar(out=sm[:, 2:3], in0=sm[:, 1:2], scalar1=-8.0, scalar2=None, op0=ALU.mult),
                                     reads=["sm"], writes=["sm"])
                                P.op("act", lambda e: e.activation(out=rr[:], in_=rr[:], func=AF.Exp, scale=sm[:, 2:3]), reads=["rr", "sm"], writes=["rr"])
                                P.op("dve", lambda e: e.tensor_tensor(out=tmp[:], in0=rr[:], in1=rr[:], op=ALU.mult), reads=["rr"], writes=["tmp"])
                                P.op("act", lambda e: e.activation(out=tmp[:], in_=tmp[:], func=AF.Sqrt, scale=-1.0, bias=1.0), reads=["tmp"], writes=["tmp"])
                                P.op("dve", lambda e: e.tensor_tensor(out=ii[:], in0=ii[:], in1=tmp[:], op=ALU.mult), reads=["ii", "tmp"], writes=["ii"])
                                P.op("dve", lambda e: e.tensor_tensor(out=ii[:], in0=ii[:], in1=xc[:], op=ALU.mult), reads=["ii", "xc"], writes=["ii"])
                                P.op("dve", lambda e: e.memset(hin[:], 0.0), writes=["hin"])
                                for rnd in range(4):
                                    P.op("dve", lambda e: e.tensor_tensor_scan(out=hh[:], data0=rr[:], data1=ii[:], initial=hin[:, 0:1],
                                                                               op0=ALU.mult, op1=ALU.add),
                                         reads=["rr", "ii", "hin"], writes=["hh"])
                                    if rnd < 3:
                                        res, rkey = exchange(sb3, "hh", hh[:, T - 1:T], 1)
                                        P.op("dve", lambda e, res=res: e.tensor_copy(out=hin[:], in_=res[:]), reads=[rkey], writes=["hin"])
                                P.op("dve", lambda e: e.tensor_tensor(out=y[:, ct, :], in0=hh[:], in1=gt[:], op=ALU.mult),
                                     reads=["hh", "gt"], writes=[f"ylru{ct}"])
                                P.barrier()
                        out_proj(sb1, 1, y, ["ylru0", "ylru1"])
                        P.barrier()

                if "lru" in mixers:
                    lru()
                P.barrier()

                def gla():
                    NCH = T // 128
                    with ExitStack() as e1:
                        def sb1(name, shape, dt):
                            return e1.enter_context(nc.sbuf_tensor(f"gla{l}_{name}", shape, dt))

                        def ps1(name, shape, dt=F32):
                            return e1.enter_context(nc.psum_tensor(f"gla{l}_{name}", shape, dt))
                        wkey = f"gla{l}_w"
                        qr = sb1("qr", [128, T], BF16)
                        kr = sb1("kr", [128, T], BF16)
                        st = sb1("st", [16, T], F32)
                        vt = sb1("vt", [128, NCH, 256], BF16)
                        gt = sb1("gt", [128, NCH, 256], BF16)
                        bc = sb1("bc", [128, T], F32)
                        br = sb1("br", [128, T], F32)
                        one_t = sb1("one", [128, T], F32)
                        qd = sb1("qd", [128, T], BF16)
                        kd = sb1("kd", [128, T], BF16)
                        sm = sb1("sm", [128, 4 * NCH + 4], F32)
                        identb = sb1("identb", [128, 128], BF16)
                        ptm = ps1("ptm", [128, 512])
                        psc = ps1("psc", [128, 512])
                        pkv = ps1("pkv", [128, 512])
                        ptr = ps1("ptr", [128, 1024], BF16)
                        ew = e1.enter_context(ExitStack())
                        wl = load_win(lambda name, shape, dt: ew.enter_context(nc.sbuf_tensor(name, shape, dt)), wkey, 0, 784)
                        P.op("dve", lambda e: e.tensor_copy(out=identb[:], in_=prm("ident")), reads=["pp"], writes=["identb"])
                        P.op("dve", lambda e: e.memset(one_t[:], 1.0), writes=["one"])
                        sc = 32.0 ** -0.5
                        proj_fm(wl, wkey, 0, 128, lambda p_ap, tt, rk: P.op(
                            "act", lambda e: e.activation(out=qr[:, tt * 512:(tt + 1) * 512], in_=p_ap, func=AF.Copy, scale=sc), reads=rk, writes=["qr"]))
                        proj_fm(wl, wkey, 128, 128, lambda p_ap, tt, rk: P.op(
                            "act", lambda e: e.activation(out=kr[:, tt * 512:(tt + 1) * 512], in_=p_ap, func=AF.Copy), reads=rk, writes=["kr"]))
                        proj_fm(wl, wkey, 768, 16, lambda p_ap, tt, rk: P.op(
                            "act", lambda e: e.activation(out=st[:, tt * 512:(tt + 1) * 512], in_=p_ap, func=AF.Copy), reads=rk, writes=["st"]))
                        for c in range(NCH):
                            for kc in range(KC):
                                P.op("pe", lambda e, kc=kc, c=c: e.matmul(ptm[:, :], lhsT=xn2[:, kc, HW + c * 128:HW + (c + 1) * 128], rhs=wl[:, kc, 256:768],
                                                                           start=(kc == 0), stop=(kc == KC - 1)),
                                     reads=[wkey] + xn2_keys, writes=["ptm"])
                            P.op("act", lambda e, c=c: e.activation(out=vt[:, c, :], in_=ptm[:, 0:256], func=AF.Copy), reads=["ptm"], writes=[f"vt{c}"])
                            P.op("act", lambda e, c=c: e.activation(out=gt[:, c, :], in_=ptm[:, 256:512], func=AF.Silu), reads=["ptm"], writes=[f"gt{c}"])
                        P.barrier()
                        ew.close()
                        PTs = sb1("PTs", [128, NCH, 512], BF16)
                        ktm = sb1("ktm", [128, NCH, 128], BF16)
                        kp = sb1("kp", [128, 128], BF16)
                        qx = sb1("qx", [128, 512], BF16)
                        S = sb1("S", [128, 256], F32)
                        Sb = sb1("Sb", [128, 256], BF16)
                        tkv = sb1("tkv", [128, 256], F32)
                        oa = sb1("oa", [128, NCH, 256], F32)
                        ob_ = sb1("ob", [128, NCH, 256], BF16)
                        y = sb1("y", [128, 2, T], BF16)
                        aup = prm(f"gla_aup_{l}")
                        P.op("dve", lambda e: e.tensor_scalar(out=sm[:, 0:1], in0=prm(f"gla_ab_{l}")[:, 0:1], scalar1=-1.0, scalar2=None, op0=ALU.mult),
                             reads=["pp"], writes=["sm"])
                        for tt in range(NT):
                            P.op("pe", lambda e, tt=tt: e.matmul(psc[:, :], lhsT=aup[0:16, :], rhs=st[:, tt * 512:(tt + 1) * 512], start=True, stop=True),
                                 reads=["pp", "st"], writes=["psc"])
                            P.op("act", lambda e, tt=tt: e.activation(out=br[:, tt * 512:(tt + 1) * 512], in_=psc[:, :], func=AF.Exp, scale=-1.0, bias=sm[:, 0:1]),
                                 reads=["psc", "sm"], writes=["br"])
                        P.op("act", lambda e: e.activation(out=bc[:], in_=br[:], func=AF.Ln, bias=1.0), reads=["br"], writes=["bc"])
                        P.op("dve", lambda e: e.tensor_scalar(out=bc[:], in0=bc[:], scalar1=-1.0 / 16.0, scalar2=None, op0=ALU.mult), reads=["bc"], writes=["bc"])
                        P.op("dve", lambda e: e.tensor_tensor_scan(out=br[:], data0=one_t[:], data1=bc[:], initial=0.0, op0=ALU.mult, op1=ALU.add),
                             reads=["bc", "one"], writes=["br"])
                        br3 = br[:].rearrange("p (c i) -> p c i", i=128)
                        bc3 = bc[:].rearrange("p (c i) -> p c i", i=128)
                        bst = sm[:, 4:4 + NCH]
                        edec = sm[:, 4 + NCH:4 + 2 * NCH]
                        P.op("dve", lambda e: e.memset(sm[:, 4:5], 0.0), writes=["sm"])
                        if NCH > 1:
                            P.op("dve", lambda e: e.tensor_copy(out=sm[:, 5:4 + NCH], in_=br3[:, 0:NCH - 1, 127]), reads=["br"], writes=["sm"])
                        P.op("dve", lambda e: e.tensor_tensor(out=bc3, in0=br3, in1=bst.unsqueeze(2).to_broadcast([128, NCH, 128]), op=ALU.subtract),
                             reads=["br", "sm"], writes=["bc"])
                        P.op("act", lambda e: e.activation(out=edec, in_=bc3[:, :, 127], func=AF.Exp), reads=["bc"], writes=["sm"])
                        P.op("act", lambda e: e.activation(out=br[:], in_=bc[:], func=AF.Exp), reads=["bc"], writes=["br"])
                        P.op("dve", lambda e: e.tensor_tensor(out=qd[:], in0=qr[:], in1=br[:], op=ALU.mult), reads=["qr", "br"], writes=["qd"])
                        P.op("act", lambda e: e.activation(out=bc[:], in_=bc[:], func=AF.Exp, scale=-1.0), reads=["bc"], writes=["bc"])
                        P.op("dve", lambda e: e.tensor_tensor(out=kd[:], in0=kr[:], in1=bc[:], op=ALU.mult), reads=["kr", "bc"], writes=["kd"])
                        hm = prm("hm")
                        vkeys = [f"vt{c}" for c in range(NCH)]
                        P.op("dve", lambda e: e.memset(S[:], 0.0), writes=["S"])
                        for rnd in range(4):
                            P.op("act", lambda e: e.activation(out=Sb[:], in_=S[:], func=AF.Copy), reads=["S"], writes=["Sb"])
                            for c in range(NCH):
                                cs = slice(c * 128, (c + 1) * 128)
                                if rnd == 0:
                                    P.op("dve", lambda e, c=c, cs=cs: e.tensor_scalar(out=kp[:], in0=kd[:, cs], scalar1=edec[:, c:c + 1], scalar2=None, op0=ALU.mult),
                                         reads=["kd", "sm"], writes=["kp"])
                                    P.op("pe", lambda e: e.transpose(out=ptr[:, 0:128], in_=kp[:], identity=identb[:]), reads=["kp", "identb"], writes=["ptr"])
                                    P.op("act", lambda e, c=c: e.activation(out=ktm[:, c, :], in_=ptr[:, 0:128], func=AF.Copy), reads=["ptr"], writes=[f"ktm{c}"])
                                    for h in range(4):
                                        P.op("dve", lambda e, h=h, cs=cs: e.tensor_scalar(out=qx[:, h * 128:(h + 1) * 128], in0=qd[:, cs], scalar1=hm[:, h:h + 1],
                                                                                          scalar2=None, op0=ALU.mult),
                                             reads=["qd", "pp"], writes=["qx"])
                                    P.op("pe", lambda e, cs=cs: e.matmul(psc[:, :], lhsT=kd[:, cs], rhs=qx[:], start=True, stop=True),
                                         reads=["kd", "qx"], writes=["psc"])
                                    P.op("dve", lambda e, c=c: e.tensor_tensor(out=PTs[:, c, :], in0=psc[:, :], in1=prm("mask4"), op=ALU.mult),
                                         reads=["psc", "pp"], writes=[f"PT{c}"])
                                if rnd == 3:
                                    P.op("pe", lambda e, cs=cs: e.matmul(ptm[:, 0:256], lhsT=qd[:, cs], rhs=Sb[:], start=True, stop=False),
                                         reads=["qd", "Sb"], writes=["ptm"])
                                    for h in range(4):
                                        P.op("pe", lambda e, h=h, c=c: e.matmul(ptm[:, h * 64:(h + 1) * 64], lhsT=PTs[:, c, h * 128:(h + 1) * 128],
                                                                                rhs=vt[:, c, h * 64:(h + 1) * 64], start=False, stop=(h == 3)),
                                             reads=[f"PT{c}", f"vt{c}"], writes=["ptm"])
                                    P.op("act", lambda e, c=c: e.activation(out=oa[:, c, :], in_=ptm[:, 0:256], func=AF.Copy), reads=["ptm"], writes=[f"oa{c}"])
                                P.op("pe", lambda e, c=c: e.matmul(pkv[:, 0:256], lhsT=ktm[:, c, :], rhs=vt[:, c, :], start=True, stop=True),
                                     reads=[f"ktm{c}", f"vt{c}"], writes=["pkv"])
                                P.op("dve", lambda e: e.tensor_tensor(out=tkv[:], in0=pkv[:, 0:256], in1=prm("bm"), op=ALU.mult), reads=["pkv", "pp"], writes=["tkv"])
                                P.op("dve", lambda e, c=c: e.scalar_tensor_tensor(out=S[:], in0=S[:], scalar=edec[:, c:c + 1], in1=tkv[:], op0=ALU.mult, op1=ALU.add),
                                     reads=["S", "tkv", "sm"], writes=["S"])
                                P.op("act", lambda e: e.activation(out=Sb[:], in_=S[:], func=AF.Copy), reads=["S"], writes=["Sb"])
                            if rnd < 3:
                                res, rkey = exchange(sb1, "S", S[:], 256)
                                P.op("dve", lambda e, res=res: e.tensor_copy(out=S[:], in_=res[:]), reads=[rkey], writes=["S"])
                        okeys = [f"oa{c}" for c in range(NCH)]
                        oa4 = oa[:].rearrange("p c (h v) -> p (c h) v", v=64)
                        ob4 = ob_[:].rearrange("p c (h v) -> p (c h) v", v=64)
                        sq_ = sb1("sqv", [128, NCH, 256], F32)
                        sq4 = sq_[:].rearrange("p c (h v) -> p (c h) v", v=64)
                        rsd = sb1("rsd", [128, NCH * 4], F32)
                        P.op("dve", lambda e: e.tensor_tensor(out=sq_[:], in0=oa[:], in1=oa[:], op=ALU.mult), reads=okeys, writes=["sqv"])
                        P.op("dve", lambda e: e.tensor_reduce(out=rsd[:], in_=sq4, axis=AX.X, op=ALU.add), reads=["sqv"], writes=["rsd"])
                        P.op("act", lambda e: e.activation(out=rsd[:], in_=rsd[:], func=AF.Sqrt, scale=1.0 / 64.0, bias=1e-5), reads=["rsd"], writes=["rsd"])
                        P.op("dve", lambda e: e.reciprocal(out=rsd[:], in_=rsd[:]), reads=["rsd"], writes=["rsd"])
                        P.op("dve", lambda e: e.tensor_tensor(out=sq4, in0=oa4, in1=rsd[:].unsqueeze(2).to_broadcast([128, NCH * 4, 64]), op=ALU.mult),
                             reads=okeys + ["rsd"], writes=["sqv"])
                        P.op("dve", lambda e: e.tensor_tensor(out=sq4, in0=sq4, in1=prm(f"gla_nw_{l}").unsqueeze(1).to_broadcast([128, NCH * 4, 64]), op=ALU.mult),
                             reads=["sqv", "pp"], writes=["sqv"])
                        P.op("dve", lambda e: e.tensor_tensor(out=ob_[:], in0=sq_[:], in1=gt[:], op=ALU.mult),
                             reads=["sqv"] + [f"gt{c}" for c in range(NCH)], writes=["ob"])
                        for c in range(NCH):
                            for ct in range(2):
                                P.op("pe", lambda e, c=c, ct=ct: e.transpose(out=ptr[:, 0:128], in_=ob_[:, c, ct * 128:(ct + 1) * 128], identity=identb[:]),
                                     reads=["ob", "identb"], writes=["ptr"])
                                P.op("act", lambda e, c=c, ct=ct: e.activation(out=y[:, ct, c * 128:(c + 1) * 128], in_=ptr[:, 0:128], func=AF.Copy),
                                     reads=["ptr"], writes=["ygla"])
                        out_proj(sb1, 0, y, ["ygla"])
                        P.barrier()

                if "gla" in mixers:
                    gla()
                P.barrier()

                def ssd():
                    NCH = T // 128
                    with ExitStack() as e1:
                        def sb1(name, shape, dt):
                            return e1.enter_context(nc.sbuf_tensor(f"ssd{l}_{name}", shape, dt))

                        def ps1(name, shape, dt=F32):
                            return e1.enter_context(nc.psum_tensor(f"ssd{l}_{name}", shape, dt))
                        wkey = f"ssd{l}_w"
                        BT = sb1("BT", [128, 2, T], BF16)
                        CT = sb1("CT", [128, 2, T], BF16)
                        zt = sb1("zt", [128, NCH, 256], BF16)
                        dtt = sb1("dtt", [128, NCH, 4], F32)
                        dA = sb1("dA", [128, NCH, 4], F32)
                        cs = sb1("cs", [128, NCH, 4], F32)
                        ncs = sb1("ncs", [128, NCH, 4], F32)
                        csl = sb1("csl", [128, NCH, 4], F32)
                        ecs = sb1("ecs", [128, NCH, 4], F32)
                        dend = sb1("dend", [128, NCH, 4], F32)
                        ecsl = sb1("ecsl", [128, NCH, 4], F32)
                        av = sb1("av", [128, 4], F32)
                        xt = sb1("xt", [128, NCH, 256], F32)
                        xdt = sb1("xdt", [128, NCH, 256], BF16)
                        Bt = sb1("Bt", [128, NCH, 256], BF16)
                        identf = prm("ident")
                        ptm = ps1("ptm", [128, 512])
                        pa = ps1("pa", [128, 512])
                        pb_ = ps1("pb", [128, 512])
                        pc = ps1("pc", [128, 512])
                        ew = e1.enter_context(ExitStack())

                        def sbw(name, shape, dt):
                            return ew.enter_context(nc.sbuf_tensor(f"ssd{l}_{name}" if not name.startswith("ssd") else name, shape, dt))
                        wl = load_win(sbw, wkey, 2128, 1028)
                        xb = sbw("xb", [128, HW + T], F32)
                        xc = sbw("xc", [128, T], F32)
                        xsT = sbw("xsT", [128, 2, T], F32)
                        BTf = sbw("BTf", [128, 2, T], F32)
                        for ct in range(6):
                            proj_fm(wl, wkey, 256 + ct * 128, 128,
                                    lambda p_ap, tt, rk: P.op("act", lambda e: e.activation(out=xb[:, HW + tt * 512:HW + (tt + 1) * 512], in_=p_ap, func=AF.Copy),
                                                              reads=rk, writes=["xb"]),
                                    lambda p_ap, rk: P.op("act", lambda e: e.activation(out=xb[:, 0:HW], in_=p_ap, func=AF.Copy), reads=rk, writes=["xb"]))
                            cw = prm(f"ssd_cw_{l}")[:, ct * 4:(ct + 1) * 4]
                            cb = prm(f"ssd_cb_{l}")[:, ct:ct + 1]
                            P.op("dve", lambda e: e.tensor_scalar(out=xc[:], in0=xb[:, 1:1 + T], scalar1=cw[:, 0:1], scalar2=cb, op0=ALU.mult, op1=ALU.add),
                                 reads=["xb", "pp"], writes=["xc"])
                            for k in range(1, 4):
                                P.op("dve", lambda e, k=k: e.scalar_tensor_tensor(out=xc[:], in0=xb[:, k + 1:k + 1 + T], scalar=cw[:, k:k + 1], in1=xc[:],
                                                                                  op0=ALU.mult, op1=ALU.add), reads=["xb", "xc", "pp"], writes=["xc"])
                            if ct < 2:
                                P.op("act", lambda e, ct=ct: e.activation(out=xsT[:, ct, :], in_=xc[:], func=AF.Silu), reads=["xc"], writes=["xsT"])
                            elif ct < 4:
                                P.op("act", lambda e, ct=ct: e.activation(out=BTf[:, ct - 2, :], in_=xc[:], func=AF.Silu), reads=["xc"], writes=["BTf"])
                                P.op("dve", lambda e, ct=ct: e.tensor_copy(out=BT[:, ct - 2, :], in_=BTf[:, ct - 2, :]), reads=["BTf"], writes=["BT"])
                            else:
                                P.op("act", lambda e, ct=ct: e.activation(out=CT[:, ct - 4, :], in_=xc[:], func=AF.Silu), reads=["xc"], writes=["CT"])
                        P.op("act", lambda e: e.activation(out=av[:], in_=prm(f"ssd_alog_{l}"), func=AF.Exp), reads=["pp"], writes=["av"])
                        P.op("dve", lambda e: e.tensor_scalar(out=av[:], in0=av[:], scalar1=-1.0, scalar2=None, op0=ALU.mult), reads=["av"], writes=["av"])
                        for c in range(NCH):
                            tsl = slice(HW + c * 128, HW + (c + 1) * 128)
                            for kc in range(KC):
                                P.op("pe", lambda e, kc=kc, tsl=tsl: e.matmul(ptm[:, 0:256], lhsT=xn2[:, kc, tsl], rhs=wl[:, kc, 0:256],
                                                                              start=(kc == 0), stop=(kc == KC - 1)), reads=[wkey] + xn2_keys, writes=["ptm"])
                            P.op("act", lambda e, c=c: e.activation(out=zt[:, c, :], in_=ptm[:, 0:256], func=AF.Silu), reads=["ptm"], writes=["zt"])
                            for kc in range(KC):
                                P.op("pe", lambda e, kc=kc, tsl=tsl: e.matmul(pa[:, 0:4], lhsT=xn2[:, kc, tsl], rhs=wl[:, kc, 1024:1028],
                                                                              start=(kc == 0), stop=(kc == KC - 1)), reads=[wkey] + xn2_keys, writes=["pa"])
                            P.op("dve", lambda e, c=c: e.tensor_tensor(out=dtt[:, c, :], in0=pa[:, 0:4], in1=prm(f"ssd_dtb_{l}"), op=ALU.add),
                                 reads=["pa", "pp"], writes=["dtt"])
                            for ct in range(2):
                                P.op("pe", lambda e, c=c, ct=ct: e.transpose(out=pb_[:, ct * 128:(ct + 1) * 128], in_=xsT[:, ct, c * 128:(c + 1) * 128], identity=identf),
                                     reads=["xsT", "pp"], writes=["pb"])
                                P.op("pe", lambda e, c=c, ct=ct: e.transpose(out=pb_[:, 256 + ct * 128:256 + (ct + 1) * 128], in_=BTf[:, ct, c * 128:(c + 1) * 128], identity=identf),
                                     reads=["BTf", "pp"], writes=["pb"])
                            P.op("act", lambda e, c=c: e.activation(out=xt[:, c, :], in_=pb_[:, 0:256], func=AF.Copy), reads=["pb"], writes=["xt"])
                            P.op("act", lambda e, c=c: e.activation(out=Bt[:, c, :], in_=pb_[:, 256:512], func=AF.Copy), reads=["pb"], writes=["Bt"])
                        P.barrier()
                        ew.close()
                        yd = sb1("yd", [128, NCH, 256], F32)
                        ya = sb1("ya", [128, NCH, 256], F32)
                        stl = sb1("stl", [128, NCH, 256], F32)
                        hs = sb1("hs", [128, 256], F32)
                        hsb = sb1("hsb", [128, 256], BF16)
                        dAb = sb1("dAb", [128, 128], F32)
                        tmpm = sb1("tmpm", [128, 128], F32)
                        decT = sb1("decT", [128, 128], F32)
                        MT = sb1("MT", [128, 128], BF16)
                        xde = sb1("xde", [128, 256], BF16)
                        y = sb1("y", [128, 2, T], BF16)
                        P.op("act", lambda e: e.activation(out=dtt[:], in_=dtt[:], func=AF.Exp), reads=["dtt"], writes=["dtt"])
                        P.op("act", lambda e: e.activation(out=dtt[:], in_=dtt[:], func=AF.Ln, bias=1.0), reads=["dtt"], writes=["dtt"])
                        P.op("dve", lambda e: e.tensor_tensor(out=dA[:], in0=dtt[:], in1=av[:].unsqueeze(1).to_broadcast([128, NCH, 4]), op=ALU.mult),
                             reads=["dtt", "av"], writes=["dA"])
                        xt4 = xt[:].rearrange("p c (h v) -> p (c h) v", v=64)
                        xdt4 = xdt[:].rearrange("p c (h v) -> p (c h) v", v=64)
                        P.op("dve", lambda e: e.tensor_tensor(out=xdt4, in0=xt4, in1=dtt[:].rearrange("p c h -> p (c h)").unsqueeze(2).to_broadcast([128, NCH * 4, 64]),
                                                              op=ALU.mult), reads=["xt", "dtt"], writes=["xdt"])
                        for c in range(NCH):
                            P.op("pe", lambda e, c=c: e.matmul(pa[:, 0:4], lhsT=prm("utri"), rhs=dA[:, c, :], start=True, stop=True), reads=["dA", "pp"], writes=["pa"])
                            P.op("pe", lambda e, c=c: e.matmul(pa[:, 8:12], lhsT=prm("ones"), rhs=dA[:, c, :], start=True, stop=True), reads=["dA", "pp"], writes=["pa"])
                            P.op("act", lambda e, c=c: e.activation(out=cs[:, c, :], in_=pa[:, 0:4], func=AF.Copy), reads=["pa"], writes=["cs"])
                            P.op("act", lambda e, c=c: e.activation(out=csl[:, c, :], in_=pa[:, 8:12], func=AF.Copy), reads=["pa"], writes=["csl"])
                        P.op("dve", lambda e: e.tensor_scalar(out=ncs[:], in0=cs[:], scalar1=-1.0, scalar2=None, op0=ALU.mult), reads=["cs"], writes=["ncs"])
                        P.op("act", lambda e: e.activation(out=ecs[:], in_=cs[:], func=AF.Exp), reads=["cs"], writes=["ecs"])
                        P.op("act", lambda e: e.activation(out=ecsl[:], in_=csl[:], func=AF.Exp), reads=["csl"], writes=["ecsl"])
                        P.op("dve", lambda e: e.tensor_tensor(out=dend[:], in0=csl[:], in1=cs[:], op=ALU.subtract), reads=["cs", "csl"], writes=["dend"])
                        P.op("act", lambda e: e.activation(out=dend[:], in_=dend[:], func=AF.Exp), reads=["dend"], writes=["dend"])
                        for c in range(NCH):
                            cs_ = slice(c * 128, (c + 1) * 128)
                            for g in range(2):
                                P.op("pe", lambda e, g=g, cs_=cs_: e.matmul(pb_[:, g * 128:(g + 1) * 128], lhsT=BT[:, g, cs_], rhs=CT[:, g, cs_], start=True, stop=True),
                                     reads=["BT", "CT"], writes=["pb"])
                            for h in range(4):
                                g = h // 2
                                P.op("dve", lambda e, c=c, h=h: e.tensor_scalar(out=dAb[:], in0=prm("ones"), scalar1=dA[:, c, h:h + 1], scalar2=None, op0=ALU.mult),
                                     reads=["dA", "pp"], writes=["dAb"])
                                P.op("pe", lambda e: e.matmul(pc[:, 0:128], lhsT=dAb[:], rhs=prm("utri"), start=True, stop=True), reads=["dAb", "pp"], writes=["pc"])
                                P.op("dve", lambda e: e.tensor_tensor(out=tmpm[:], in0=pc[:, 0:128], in1=prm("negmask"), op=ALU.add), reads=["pc", "pp"], writes=["tmpm"])
                                P.op("act", lambda e, c=c, h=h: e.activation(out=decT[:], in_=tmpm[:], func=AF.Exp, bias=ncs[:, c, h:h + 1]),
                                     reads=["tmpm", "ncs"], writes=["decT"])
                                P.op("dve", lambda e, g=g: e.tensor_tensor(out=MT[:], in0=pb_[:, g * 128:(g + 1) * 128], in1=decT[:], op=ALU.mult),
                                     reads=["pb", "decT"], writes=["MT"])
                                P.op("pe", lambda e, c=c, h=h: e.matmul(pc[:, 128 + h * 64:128 + (h + 1) * 64], lhsT=MT[:], rhs=xdt[:, c, h * 64:(h + 1) * 64],
                                                                        start=True, stop=True), reads=["MT", "xdt"], writes=["pc"])
                                P.op("dve", lambda e, c=c, h=h: e.tensor_scalar(out=xde[:, h * 64:(h + 1) * 64], in0=xdt[:, c, h * 64:(h + 1) * 64],
                                                                                scalar1=dend[:, c, h:h + 1], scalar2=None, op0=ALU.mult),
                                     reads=["xdt", "dend"], writes=["xde"])
                                P.op("pe", lambda e, c=c, h=h, g=g: e.matmul(pa[:, 128 + h * 64:128 + (h + 1) * 64], lhsT=Bt[:, c, g * 128:(g + 1) * 128],
                                                                             rhs=xde[:, h * 64:(h + 1) * 64], start=True, stop=True),
                                     reads=["Bt", "xde"], writes=["pa"])
                            P.op("act", lambda e, c=c: e.activation(out=yd[:, c, :], in_=pc[:, 128:384], func=AF.Copy), reads=["pc"], writes=["yd"])
                            P.op("act", lambda e, c=c: e.activation(out=stl[:, c, :], in_=pa[:, 128:384], func=AF.Copy), reads=["pa"], writes=["stl"])
                        P.op("dve", lambda e: e.memset(hs[:], 0.0), writes=["hs"])
                        for rnd in range(4):
                            P.op("act", lambda e: e.activation(out=hsb[:], in_=hs[:], func=AF.Copy), reads=["hs"], writes=["hsb"])
                            for c in range(NCH):
                                cs_ = slice(c * 128, (c + 1) * 128)
                                for g in range(2):
                                    P.op("pe", lambda e, g=g, cs_=cs_: e.matmul(ptm[:, g * 128:(g + 1) * 128], lhsT=CT[:, g, cs_], rhs=hsb[:, g * 128:(g + 1) * 128],
                                                                                start=True, stop=True), reads=["CT", "hsb"], writes=["ptm"])
                                for h in range(4):
                                    hc = slice(h * 64, (h + 1) * 64)
                                    P.op("dve", lambda e, c=c, h=h, hc=hc: e.scalar_tensor_tensor(out=ya[:, c, hc], in0=ptm[:, hc], scalar=ecs[:, c, h:h + 1], in1=yd[:, c, hc],
                                                                                                  op0=ALU.mult, op1=ALU.add), reads=["ptm", "ecs", "yd"], writes=["ya"])
                                    P.op("dve", lambda e, c=c, h=h, hc=hc: e.scalar_tensor_tensor(out=hs[:, hc], in0=hs[:, hc], scalar=ecsl[:, c, h:h + 1], in1=stl[:, c, hc],
                                                                                                  op0=ALU.mult, op1=ALU.add), reads=["hs", "ecsl", "stl"], writes=["hs"])
                                P.op("act", lambda e: e.activation(out=hsb[:], in_=hs[:], func=AF.Copy), reads=["hs"], writes=["hsb"])
                            if rnd < 3:
                                res, rkey = exchange(sb1, "hs", hs[:], 256)
                                P.op("dve", lambda e, res=res: e.tensor_copy(out=hs[:], in_=res[:]), reads=[rkey], writes=["hs"])
                        ya4 = ya[:].rearrange("p c (h v) -> p (c h) v", v=64)
                        yd4 = yd[:].rearrange("p c (h v) -> p (c h) v", v=64)
                        P.op("dve", lambda e: e.tensor_tensor(out=yd[:].rearrange("p c (h v) -> p c h v", v=64), in0=xt[:].rearrange("p c (h v) -> p c h v", v=64), in1=prm(f"ssd_d_{l}").unsqueeze(1).unsqueeze(3).to_broadcast([128, NCH, 4, 64]),
                                                              op=ALU.mult), reads=["xt", "pp", "yd"], writes=["yd"])
                        P.op("dve", lambda e: e.tensor_tensor(out=ya[:], in0=ya[:], in1=yd[:], op=ALU.add), reads=["ya", "yd"], writes=["ya"])
                        P.op("dve", lambda e: e.tensor_tensor(out=ya[:], in0=ya[:], in1=zt[:], op=ALU.mult), reads=["ya", "zt"], writes=["ya"])
                        rsd = sb1("rsd", [128, NCH * 2], F32)
                        ya2 = ya[:].rearrange("p c (g v) -> p (c g) v", v=128)
                        yd2 = yd[:].rearrange("p c (g v) -> p (c g) v", v=128)
                        P.op("dve", lambda e: e.tensor_tensor(out=yd[:], in0=ya[:], in1=ya[:], op=ALU.mult), reads=["ya"], writes=["yd"])
                        P.op("dve", lambda e: e.tensor_reduce(out=rsd[:], in_=yd2, axis=AX.X, op=ALU.add), reads=["yd"], writes=["rsd"])
                        P.op("act", lambda e: e.activation(out=rsd[:], in_=rsd[:], func=AF.Sqrt, scale=1.0 / 128.0, bias=1e-5), reads=["rsd"], writes=["rsd"])
                        P.op("dve", lambda e: e.reciprocal(out=rsd[:], in_=rsd[:]), reads=["rsd"], writes=["rsd"])
                        P.op("dve", lambda e: e.tensor_tensor(out=ya2, in0=ya2, in1=rsd[:].unsqueeze(2).to_broadcast([128, NCH * 2, 128]), op=ALU.mult),
                             reads=["ya", "rsd"], writes=["ya"])
                        nw = prm(f"ssd_nw_{l}")
                        for c in range(NCH):
                            for ct in range(2):
                                P.op("pe", lambda e, c=c, ct=ct: e.transpose(out=pb_[:, 0:128], in_=ya[:, c, ct * 128:(ct + 1) * 128], identity=identf),
                                     reads=["ya", "pp"], writes=["pb"])
                                P.op("act", lambda e, c=c, ct=ct: e.activation(out=y[:, ct, c * 128:(c + 1) * 128], in_=pb_[:, 0:128], func=AF.Copy, scale=nw[:, ct:ct + 1]),
                                     reads=["pb", "pp"], writes=["yssd"])
                        out_proj(sb1, 3, y, ["yssd"])
                        P.barrier()

                if "ssd" in mixers:
                    ssd()
                P.barrier()

                def rw():
                    NC = T // 64
                    CB = 1296

                    def V(fn, r, w):
                        P.op("dve", fn, reads=r, writes=w)

                    def A(fn, r, w):
                        P.op("act", fn, reads=r, writes=w)

                    def MM(fn, r, w):
                        P.op("pe", fn, reads=r, writes=w)
                    with ExitStack() as e0:
                        def sb0(name, shape, dt):
                            return e0.enter_context(nc.sbuf_tensor(f"rw{l}_{name}", shape, dt))
                        vfb = sb0("vfb", [128, 2, T], F32)
                        t1 = sb0("t1", [8, T], F32)
                        e00 = e0.enter_context(ExitStack())
                        raw0 = e00.enter_context(nc.sbuf_tensor(f"rw{l}_raw0", [128, HW + T], F32))
                        wv = e00.enter_context(nc.sbuf_tensor(f"rw{l}_wv", [128, KC, 256], BF16))
                        P.op("poolq", lambda e: e.dma_start(out=wv[:], in_=win_d[l, :, :, CB + 512:CB + 768]), writes=["rw_wv"], lane="win_v")

                        def proj_lerp(raw, wbuf, wkey, col, Mr, mu_ap, dst, dkey):
                            proj_fm(wbuf, wkey, col, Mr,
                                    lambda p_ap, tt, rk: A(lambda e: e.activation(out=raw[0:Mr, HW + tt * 512:HW + (tt + 1) * 512], in_=p_ap, func=AF.Copy), rk, ["rw_raw"]),
                                    lambda p_ap, rk: A(lambda e: e.activation(out=raw[0:Mr, 0:HW], in_=p_ap, func=AF.Copy), rk, ["rw_raw"]))
                            V(lambda e: e.tensor_tensor(out=dst, in0=raw[0:Mr, HW - 1:HW - 1 + T], in1=raw[0:Mr, HW:HW + T], op=ALU.subtract), ["rw_raw"], [dkey])
                            V(lambda e: e.scalar_tensor_tensor(out=dst, in0=dst, scalar=mu_ap, in1=raw[0:Mr, HW:HW + T], op0=ALU.mult, op1=ALU.add),
                              ["rw_raw", dkey, "pp"], [dkey])
                        for c2 in range(2):
                            proj_lerp(raw0, wv, "rw_wv", c2 * 128, 128, prm(f"rw_muv_{l}")[:, c2:c2 + 1], vfb[:, c2, :], f"vfb{c2}")
                        if l == 0:
                            for c2 in range(2):
                                P.op("sp", lambda e, c2=c2: e.dma_start(out=vfirst_d[:, c2, :], in_=vfb[:, c2, :]), reads=[f"vfb{c2}"], writes=[f"vfd{c2}"], lane=f"vf{c2}")
                        else:
                            v1p = prm(f"rw_v1_{l}")
                            for tt in range(NT):
                                pb = tt % 2
                                for c2 in range(2):
                                    MM(lambda e, c2=c2, tt=tt, pb=pb: e.matmul(pj[0:8, pb, :], lhsT=v1p[:, c2 * 8:(c2 + 1) * 8], rhs=vfb[:, c2, tt * 512:(tt + 1) * 512],
                                                                                start=(c2 == 0), stop=(c2 == 1)), ["pp", "vfb0", "vfb1"], [f"pj{pb}"])
                                A(lambda e, tt=tt, pb=pb: e.activation(out=t1[:, tt * 512:(tt + 1) * 512], in_=pj[0:8, pb, :], func=AF.Copy), [f"pj{pb}"], ["rw_t1"])
                        P.barrier()
                        e00.close()
                        for ct in range(2):
                            rw_ct(ct, sb0, vfb, t1, NC, CB, V, A, MM)
                        P.barrier()

                def rw_ct(ct, sb0, vfb, t1, NC, CB, V, A, MM):
                    with ExitStack() as eR:
                        def sbR(name, shape, dt):
                            return eR.enter_context(nc.sbuf_tensor(f"rw{l}_{ct}_{name}", shape, dt))
                        at = sbR("at", [128, T], BF16)
                        bt = sbR("bt", [128, T], BF16)
                        kt = sbR("kt", [128, T], BF16)
                        rt = sbR("rt", [128, T], BF16)
                        bh = sbR("bh", [128, T], BF16)
                        kh = sbR("kh", [128, T], BF16)
                        vb = sbR("vb", [128, T], BF16)
                        gfm = sbR("gfm", [128, T], BF16)
                        bon = sbR("bon", [128, T], F32)
                        PC = sbR("PC", [128, NC], F32)
                        bst = sbR("bst", [128, NC], F32)
                        sm = sbR("sm", [128, 4], F32)
                        identb2 = sbR("identb2", [128, 64], BF16)
                        V(lambda e: e.tensor_copy(out=identb2[:], in_=prm("ident2")), ["pp"], ["identb2"])
                        with ExitStack() as eP:
                            def sbP(name, shape, dt):
                                return eP.enter_context(nc.sbuf_tensor(f"rw{l}_{ct}_{name}", shape, dt))
                            wr = sbP("wr", [128, KC, 128], BF16)
                            wk = sbP("wk", [128, KC, 128], BF16)
                            ws = sbP("ws", [128, KC, 64], BF16)
                            P.op("poolq", lambda e: e.dma_start(out=wr[:], in_=win_d[l, :, :, CB + ct * 128:CB + (ct + 1) * 128]), writes=["rw_wr"], lane="win_r")
                            P.op("poolq", lambda e: e.dma_start(out=wk[:], in_=win_d[l, :, :, CB + 256 + ct * 128:CB + 256 + (ct + 1) * 128]), writes=["rw_wk"], lane="win_k")
                            P.op("poolq", lambda e: e.dma_start(out=ws[:], in_=win_d[l, :, :, CB + 768:CB + 832]), writes=["rw_ws"], lane="win_s")
                            raw = sbP("raw", [128, HW + T], F32)
                            stb = sbP("stb", [32, T], F32)
                            rf = sbP("rf", [128, T], F32)
                            kf = sbP("kf", [128, T], F32)
                            lw = sbP("lw", [128, T], F32)
                            af = sbP("af", [128, T], F32)
                            t2 = sbP("t2", [128, T], F32)
                            t3 = sbP("t3", [128, T], F32)
                            vv = vfb[:, ct, :]
                            vkey = f"vfb{ct}"
                            lw_tmp = t2

                            def proj_lerp(wbuf, wkey, col, Mr, mu_ap, dst, dkey):
                                proj_fm(wbuf, wkey, col, Mr,
                                        lambda p_ap, tt, rk: A(lambda e: e.activation(out=raw[0:Mr, HW + tt * 512:HW + (tt + 1) * 512], in_=p_ap, func=AF.Copy), rk, ["raw"]),
                                        lambda p_ap, rk: A(lambda e: e.activation(out=raw[0:Mr, 0:HW], in_=p_ap, func=AF.Copy), rk, ["raw"]))
                                V(lambda e: e.tensor_tensor(out=dst, in0=raw[0:Mr, HW - 1:HW - 1 + T], in1=raw[0:Mr, HW:HW + T], op=ALU.subtract), ["raw"], [dkey])
                                V(lambda e: e.scalar_tensor_tensor(out=dst, in0=dst, scalar=mu_ap, in1=raw[0:Mr, HW:HW + T], op0=ALU.mult, op1=ALU.add),
                                  ["raw", dkey, "pp"], [dkey])

                            def lowrank(src_rows, src_key, wname, dst, dkey, func, bias_ap):
                                wmat = prm(wname)
                                for tt in range(NT):
                                    pb = tt % 2
                                    MM(lambda e, tt=tt, pb=pb: e.matmul(pj[:, pb, :], lhsT=wmat[0:src_rows, ct * 128:(ct + 1) * 128],
                                                                        rhs=stb[0:src_rows, tt * 512:(tt + 1) * 512], start=True, stop=True),
                                       ["pp", src_key], [f"pj{pb}"])
                                    if bias_ap is None:
                                        A(lambda e, tt=tt, pb=pb: e.activation(out=dst[:, tt * 512:(tt + 1) * 512], in_=pj[:, pb, :], func=func), [f"pj{pb}"], [dkey])
                                    else:
                                        A(lambda e, tt=tt, pb=pb: e.activation(out=dst[:, tt * 512:(tt + 1) * 512], in_=pj[:, pb, :], func=func, bias=bias_ap),
                                          [f"pj{pb}", "pp"], [dkey])
                            proj_lerp(wr, "rw_wr", 0, 128, prm(f"rw_mur_{l}")[:, ct:ct + 1], rf[:], "rf")
                            proj_lerp(wk, "rw_wk", 0, 128, prm(f"rw_muk_{l}")[:, ct:ct + 1], kf[:], "kf")
                            proj_lerp(ws, "rw_ws", 0, 16, prm(f"rw_musw_{l}")[0:16, 0:1], stb[0:16, :], "stb")
                            A(lambda e: e.activation(out=stb[0:16, :], in_=stb[0:16, :], func=AF.Tanh), ["stb"], ["stb"])
                            lowrank(16, "stb", f"rw_w2_{l}", lw, "lw", AF.Sigmoid, prm(f"rw_w0_{l}")[:, ct:ct + 1])
                            V(lambda e: e.tensor_scalar(out=lw[:], in0=lw[:], scalar1=-float(np.exp(-0.5)), scalar2=None, op0=ALU.mult), ["lw"], ["lw"])
                            proj_lerp(ws, "rw_ws", 16, 16, prm(f"rw_musa_{l}")[0:16, 0:1], stb[0:16, :], "stb")
                            lowrank(16, "stb", f"rw_a2_{l}", af, "af", AF.Sigmoid, prm(f"rw_a0_{l}")[:, ct:ct + 1])
                            proj_lerp(ws, "rw_ws", 32, 32, prm(f"rw_musg_{l}")[0:32, 0:1], stb[0:32, :], "stb")
                            A(lambda e: e.activation(out=stb[0:32, :], in_=stb[0:32, :], func=AF.Sigmoid), ["stb"], ["stb"])
                            lowrank(32, "stb", f"rw_g2_{l}", gfm, "gfm", AF.Copy, None)
                            if l > 0:
                                P.op("sp", lambda e: e.dma_start(out=raw[:, 0:T], in_=vfirst_d[:, ct, :]), reads=[f"vfd{ct}", "raw"], writes=["raw"], lane="vfl")
                                v2p = prm(f"rw_v2_{l}")
                                for tt in range(NT):
                                    pb = tt % 2
                                    MM(lambda e, tt=tt, pb=pb: e.matmul(pj[:, pb, :], lhsT=v2p[0:8, ct * 128:(ct + 1) * 128], rhs=t1[0:8, tt * 512:(tt + 1) * 512],
                                                                        start=True, stop=True), ["pp", "rw_t1"], [f"pj{pb}"])
                                    A(lambda e, tt=tt, pb=pb: e.activation(out=t2[:, tt * 512:(tt + 1) * 512], in_=pj[:, pb, :], func=AF.Sigmoid,
                                                                           bias=prm(f"rw_v0_{l}")[:, ct:ct + 1]), [f"pj{pb}", "pp"], ["t2"])
                                V(lambda e: e.tensor_tensor(out=t3[:], in0=raw[:, 0:T], in1=vv, op=ALU.subtract), ["raw", vkey], ["t3"])
                                V(lambda e: e.tensor_tensor(out=t3[:], in0=t3[:], in1=t2[:], op=ALU.mult), ["t3", "t2"], ["t3"])
                                V(lambda e: e.tensor_tensor(out=t2[:], in0=vv, in1=t3[:], op=ALU.add), ["t3", vkey], ["t2"])
                                vuse, vukey = t2, "t2"
                                V(lambda e: e.tensor_copy(out=vb[:], in_=t2[:]), ["t2"], ["vb"])
                            else:
                                V(lambda e: e.tensor_copy(out=vb[:], in_=vv), [vkey], ["vb"])
                            kkc = prm(f"rw_k_k_{l}")[:, ct:ct + 1]
                            kac = prm(f"rw_k_a_{l}")[:, ct:ct + 1]
                            V(lambda e: e.tensor_scalar(out=sm[:, 0:1], in0=kac, scalar1=-1.0, scalar2=1.0, op0=ALU.mult, op1=ALU.add), ["pp"], ["sm"])
                            V(lambda e: e.tensor_scalar(out=raw[:, 0:T], in0=kf[:], scalar1=kkc, scalar2=None, op0=ALU.mult), ["kf", "pp", "raw"], ["raw"])
                            V(lambda e: e.tensor_tensor(out=t3[:], in0=raw[:, 0:T], in1=raw[:, 0:T], op=ALU.mult), ["raw", "vb"], ["t3"])
                            for tt in range(NT):
                                pb = tt % 2
                                MM(lambda e, tt=tt, pb=pb: e.matmul(pj[:, pb, :], lhsT=prm("bones"), rhs=t3[:, tt * 512:(tt + 1) * 512], start=True, stop=True),
                                   ["pp", "t3"], [f"pj{pb}"])
                                A(lambda e, tt=tt, pb=pb: e.activation(out=lw_tmp[:, tt * 512:(tt + 1) * 512], in_=pj[:, pb, :], func=AF.Sqrt), [f"pj{pb}"], ["t2"])
                            V(lambda e: e.tensor_scalar(out=lw_tmp[:], in0=lw_tmp[:], scalar1=1e-12, scalar2=None, op0=ALU.max), ["t2"], ["t2"])
                            V(lambda e: e.reciprocal(out=lw_tmp[:], in_=lw_tmp[:]), ["t2"], ["t2"])
                            V(lambda e: e.tensor_tensor(out=raw[:, 0:T], in0=raw[:, 0:T], in1=lw_tmp[:], op=ALU.mult), ["raw", "t2"], ["raw"])
                            V(lambda e: e.tensor_scalar(out=t3[:], in0=af[:], scalar1=kac, scalar2=sm[:, 0:1], op0=ALU.mult, op1=ALU.add), ["af", "pp", "sm"], ["t3"])
                            V(lambda e: e.tensor_tensor(out=kf[:], in0=kf[:], in1=t3[:], op=ALU.mult), ["kf", "t3"], ["kf"])
                            V(lambda e: e.tensor_tensor(out=t3[:], in0=rf[:], in1=kf[:], op=ALU.mult), ["rf", "kf"], ["t3"])
                            V(lambda e: e.tensor_scalar(out=t3[:], in0=t3[:], scalar1=prm(f"rw_r_k_{l}")[:, ct:ct + 1], scalar2=None, op0=ALU.mult), ["t3", "pp"], ["t3"])
                            for tt in range(NT):
                                pb = tt % 2
                                MM(lambda e, tt=tt, pb=pb: e.matmul(pj[:, pb, :], lhsT=prm("bones"), rhs=t3[:, tt * 512:(tt + 1) * 512], start=True, stop=True),
                                   ["pp", "t3"], [f"pj{pb}"])
                                V(lambda e, tt=tt, pb=pb: e.tensor_tensor(out=bon[:, tt * 512:(tt + 1) * 512], in0=pj[:, pb, :], in1=vb[:, tt * 512:(tt + 1) * 512], op=ALU.mult),
                                  [f"pj{pb}", "vb"], ["bon"])
                            V(lambda e: e.tensor_tensor(out=af[:], in0=af[:], in1=raw[:, 0:T], op=ALU.mult), ["af", "raw"], ["af"])
                            V(lambda e: e.memset(lw_tmp[:], 1.0), ["t2"], ["t2"])
                            V(lambda e: e.tensor_tensor_scan(out=t3[:], data0=lw_tmp[:], data1=lw[:], initial=0.0, op0=ALU.mult, op1=ALU.add), ["t2", "lw"], ["t3"])
                            t33 = t3[:].rearrange("p (c i) -> p c i", i=64)
                            V(lambda e: e.memset(bst[:, 0:1], 0.0), [], ["bst"])
                            V(lambda e: e.tensor_copy(out=bst[:, 1:NC], in_=t33[:, 0:NC - 1, 63]), ["t3"], ["bst"])
                            V(lambda e: e.tensor_tensor(out=t33, in0=t33, in1=bst[:].unsqueeze(2).to_broadcast([128, NC, 64]), op=ALU.subtract), ["t3", "bst"], ["t3"])
                            A(lambda e: e.activation(out=PC[:], in_=t33[:, :, 63], func=AF.Exp), ["t3"], ["PC"])
                            V(lambda e: e.tensor_tensor(out=lw[:], in0=t3[:], in1=lw[:], op=ALU.subtract), ["t3", "lw"], ["lw"])
                            A(lambda e: e.activation(out=lw[:], in_=lw[:], func=AF.Exp), ["lw"], ["lw"])
                            V(lambda e: e.scalar_tensor_tensor(out=at[:], in0=raw[:, 0:T], scalar=-1.0, in1=lw[:], op0=ALU.mult, op1=ALU.mult), ["raw", "lw"], ["at"])
                            A(lambda e: e.activation(out=lw[:], in_=t3[:], func=AF.Exp), ["t3", "lw", "at"], ["lw"])
                            V(lambda e: e.tensor_tensor(out=rt[:], in0=rf[:], in1=lw[:], op=ALU.mult), ["rf", "lw"], ["rt"])
                            A(lambda e: e.activation(out=t3[:], in_=t3[:], func=AF.Exp, scale=-1.0), ["t3", "rt"], ["t3"])
                            PCb = PC[:].unsqueeze(2).to_broadcast([128, NC, 64])
                            V(lambda e: e.tensor_tensor(out=af[:], in0=af[:], in1=t3[:], op=ALU.mult), ["af", "t3"], ["af"])
                            V(lambda e: e.tensor_copy(out=bt[:], in_=af[:]), ["af"], ["bt"])
                            V(lambda e: e.tensor_tensor(out=bh[:].rearrange("p (c i) -> p c i", i=64), in0=af[:].rearrange("p (c i) -> p c i", i=64), in1=PCb, op=ALU.mult),
                              ["af", "PC"], ["bh"])
                            V(lambda e: e.tensor_tensor(out=kf[:], in0=kf[:], in1=t3[:], op=ALU.mult), ["kf", "t3", "bon"], ["kf"])
                            V(lambda e: e.tensor_copy(out=kt[:], in_=kf[:]), ["kf"], ["kt"])
                            V(lambda e: e.tensor_tensor(out=kh[:].rearrange("p (c i) -> p c i", i=64), in0=kf[:].rearrange("p (c i) -> p c i", i=64), in1=PCb, op=ALU.mult),
                              ["kf", "PC"], ["kh"])
                            P.barrier()
                        import os
                        if int(os.environ.get("RW_STAGE", "9")) >= 2:
                            rw_chunks(ct, sbR, at, bt, kt, rt, bh, kh, vb, gfm, bon, PC, identb2, NC, V, A, MM)
                        P.barrier()

                def rw_chunks(ct, sbR, at, bt, kt, rt, bh, kh, vb, gfm, bon, PC, identb2, NC, V, A, MM):
                    with ExitStack() as eC:
                        def sbC(name, shape, dt):
                            return eC.enter_context(nc.sbuf_tensor(f"rwc{l}_{ct}_{name}", shape, dt))

                        def psC(name, shape, dt=F32):
                            return eC.enter_context(nc.psum_tensor(f"rwc{l}_{ct}_{name}", shape, dt))
                        GTb = sbC("GTb", [128, NC, 64], BF16)
                        Jst = sbC("Jst", [128, NC, 64], F32)
                        CoefTb = sbC("CoefTb", [128, NC, 64], BF16)
                        Yc = sbC("Yc", [128, NC, 64], F32)
                        A01 = sbC("A01", [128, 2, 128], F32)
                        V(lambda e: e.memset(A01[:], 0.0), [], ["A01"])
                        A23 = sbC("A23", [128, 3, 64], BF16)
                        PsPt = sbC("PsPt", [128, 2, 128], F32)
                        Tt = sbC("Tt", [128, 128], F32)
                        Ttb = sbC("Ttb", [128, 64], BF16)
                        tmx = sbC("tmx", [128, 4, 64], BF16)
                        TpX = sbC("TpX", [128, 2, 64], BF16)
                        Tppb = sbC("Tppb", [128, 64], BF16)
                        H = sbC("H", [128, 64], F32)
                        Hb = sbC("Hb", [128, 64], BF16)
                        bankA = (pj[:, 0, :], pj[:, 1, :])
                        bankC = (psC("bankC0", [128, 512]), psC("bankC1", [128, 512]))
                        bankJ = (psC("bankJ0", [128, 512]), psC("bankJ1", [128, 512]))
                        ptx = (psC("ptx0", [128, 1024], BF16), psC("ptx1", [128, 1024], BF16))

                        def V2(fn, r, w):
                            for hh_, ps_ in enumerate(HS):
                                P.op("dve", lambda e, hh_=hh_, ps_=ps_: fn(e, hh_, ps_), reads=[k.format(hh=hh_) for k in r], writes=[k.format(hh=hh_) for k in w])

                        def A2(fn, r, w):
                            for hh_, ps_ in enumerate(HS):
                                P.op("act", lambda e, hh_=hh_, ps_=ps_: fn(e, hh_, ps_), reads=[k.format(hh=hh_) for k in r], writes=[k.format(hh=hh_) for k in w])
                        HS = (slice(0, 64), slice(64, 128))

                        def MT(fn, r, w):
                            P.op("pe", fn, reads=[k.format(hh=hh) for k in r], writes=[k.format(hh=hh) for k in w], mode="t64")
                        maskA = prm("maskA")
                        ident2 = prm("ident2")
                        for c in range(NC):
                            cs = slice(c * 64, (c + 1) * 64)
                            for hh, ps_ in enumerate(HS):
                                for k_, (lt_, rh_) in enumerate(((bt, at), (at, bt), (kt, at), (bt, rt), (kt, rt))):
                                    MT(lambda e, ps_=ps_, hh=hh, k_=k_, lt_=lt_, rh_=rh_: e.matmul(bankA[hh][ps_, k_ * 64:(k_ + 1) * 64], lhsT=lt_[ps_, cs], rhs=rh_[ps_, cs],
                                                                                          start=True, stop=True), ["at", "bt", "kt", "rt"], ["pj{hh}"])
                                for k_, X in enumerate(() if os.environ.get("RW_NOTR") else (at, bh, kh, vb)):
                                    MT(lambda e, ps_=ps_, hh=hh, k_=k_, X=X: e.transpose(out=ptx[hh][ps_, k_ * 64:(k_ + 1) * 64], in_=X[ps_, cs], identity=identb2[ps_, :]),
                                       ["at", "bh", "kh", "vb", "identb2"], ["ptx{hh}"])
                            V2(lambda e, hh, ps_: e.tensor_tensor(out=A01[ps_, :, hh * 64:(hh + 1) * 64], in0=bankA[hh][ps_, 0:128].rearrange("p (a b) -> p a b", b=64),
                                                                  in1=maskA[ps_, 0:128].rearrange("p (a b) -> p a b", b=64), op=ALU.mult),
                               ["pj{hh}", "pp"], ["A01"])
                            V2(lambda e, hh, ps_: e.tensor_tensor(out=A23[ps_].rearrange("p a b -> p (a b)"), in0=bankA[hh][ps_, 128:320], in1=maskA[ps_, 128:320], op=ALU.mult),
                               ["pj{hh}", "pp"], ["A23"])
                            A2(lambda e, hh, ps_: e.activation(out=tmx[ps_].rearrange("p a b -> p (a b)"), in_=ptx[hh][ps_, 0:256], func=AF.Copy), ["ptx{hh}"], ["tmx"])
                            V(lambda e: e.tensor_tensor(out=Tt[:], in0=A01[:, 0, :], in1=prm("ident"), op=ALU.add), ["A01", "pp"], ["Tt"])
                            cur_s, cur_t, ckey = A01[:, 1, :], A01[:, 0, :], "A01"
                            for lev in range(0 if os.environ.get("RW_NOINV") else 5):
                                MM(lambda e, cur_s=cur_s, cur_t=cur_t: e.matmul(bankJ[0][:, 128:256], lhsT=cur_t, rhs=cur_s, start=True, stop=True), [ckey], ["bJ0"])
                                MM(lambda e, cur_s=cur_s, cur_t=cur_t: e.matmul(bankJ[0][:, 256:384], lhsT=cur_s, rhs=cur_t, start=True, stop=True), [ckey], ["bJ0"])
                                A(lambda e: e.activation(out=PsPt[:].rearrange("p a b -> p (a b)"), in_=bankJ[0][:, 128:384], func=AF.Copy), ["bJ0"], ["PsPt"])
                                cur_s, cur_t, ckey = PsPt[:, 0, :], PsPt[:, 1, :], "PsPt"
                                MM(lambda e, cur_s=cur_s: e.matmul(bankJ[0][:, 384:512], lhsT=cur_s, rhs=Tt[:], start=True, stop=True), ["PsPt", "Tt"], ["bJ0"])
                                V(lambda e: e.tensor_tensor(out=Tt[:], in0=bankJ[0][:, 384:512], in1=Tt[:], op=ALU.add), ["bJ0", "Tt"], ["Tt"])
                            if int(os.environ.get("RW_STAGE", "9")) == 2:
                                continue
                            V2(lambda e, hh, ps_: e.tensor_copy(out=Ttb[ps_], in_=Tt[ps_, hh * 64:(hh + 1) * 64]), ["Tt"], ["Ttb"])
                            for hh, ps_ in enumerate(HS):
                                MT(lambda e, ps_=ps_, hh=hh: e.matmul(bankC[hh][ps_, 0:64], lhsT=Ttb[ps_], rhs=tmx[ps_, 0, :], start=True, stop=True), ["Ttb", "tmx"], ["bC{hh}"])
                                MT(lambda e, ps_=ps_, hh=hh: e.matmul(bankC[hh][ps_, 64:128], lhsT=A23[ps_, 0, :], rhs=tmx[ps_, 3, :], start=True, stop=True), ["A23", "tmx"], ["bC{hh}"])
                            A2(lambda e, hh, ps_: e.activation(out=TpX[ps_].rearrange("p a b -> p (a b)"), in_=bankC[hh][ps_, 0:128], func=AF.Copy), ["bC{hh}"], ["TpX"])
                            for hh, ps_ in enumerate(HS):
                                MT(lambda e, ps_=ps_, hh=hh: e.matmul(bankC[hh][ps_, 128:192], lhsT=Ttb[ps_], rhs=TpX[ps_, 1, :], start=True, stop=True), ["Ttb", "TpX"], ["bC{hh}"])
                            A2(lambda e, hh, ps_: e.activation(out=Tppb[ps_], in_=bankC[hh][ps_, 128:192], func=AF.Copy), ["bC{hh}"], ["Tppb"])
                            for hh, ps_ in enumerate(HS):
                                MT(lambda e, ps_=ps_, hh=hh: e.matmul(bankC[hh][ps_, 192:256], lhsT=TpX[ps_, 0, :], rhs=tmx[ps_, 1, :], start=True, stop=True), ["TpX", "tmx"], ["bC{hh}"])
                                MT(lambda e, ps_=ps_, hh=hh: e.matmul(bankC[hh][ps_, 256:320], lhsT=TpX[ps_, 0, :], rhs=A23[ps_, 1, :], start=True, stop=True), ["TpX", "A23"], ["bC{hh}"])
                                MT(lambda e, ps_=ps_, hh=hh: e.matmul(bankJ[hh][ps_, 0:64], lhsT=tmx[ps_, 1, :], rhs=Tppb[ps_], start=True, stop=False), ["tmx", "Tppb"], ["bJ{hh}"])
                                MT(lambda e, ps_=ps_, hh=hh: e.matmul(bankJ[hh][ps_, 0:64], lhsT=tmx[ps_, 2, :], rhs=tmx[ps_, 3, :], start=False, stop=True), ["tmx"], ["bJ{hh}"])
                                MT(lambda e, ps_=ps_, hh=hh: e.matmul(bankJ[hh][ps_, 64:128], lhsT=A23[ps_, 1, :], rhs=Tppb[ps_], start=True, stop=False), ["A23", "Tppb"], ["bJ{hh}"])
                                MT(lambda e, ps_=ps_, hh=hh: e.matmul(bankJ[hh][ps_, 64:128], lhsT=A23[ps_, 2, :], rhs=tmx[ps_, 3, :], start=False, stop=True), ["A23", "tmx"], ["bJ{hh}"])
                            V2(lambda e, hh, ps_, c=c: e.scalar_tensor_tensor(out=GTb[ps_, c, :], in0=ident2[ps_], scalar=PC[ps_, c:c + 1], in1=bankC[hh][ps_, 192:256],
                                                                                op0=ALU.mult, op1=ALU.add), ["bC{hh}", "pp", "PC"], [f"GT{c}"])
                            V2(lambda e, hh, ps_, c=c, cs=cs: e.tensor_tensor(out=CoefTb[ps_, c, :], in0=bankC[hh][ps_, 256:320], in1=rt[ps_, cs], op=ALU.add), ["bC{hh}", "rt"], [f"Cf{c}"])
                            A2(lambda e, hh, ps_, c=c: e.activation(out=Jst[ps_, c, :], in_=bankJ[hh][ps_, 0:64], func=AF.Copy), ["bJ{hh}"], [f"J{c}"])
                            A2(lambda e, hh, ps_, c=c: e.activation(out=Yc[ps_, c, :], in_=bankJ[hh][ps_, 64:128], func=AF.Copy), ["bJ{hh}"], [f"Yc{c}"])
                        if int(os.environ.get("RW_STAGE", "9")) <= 3:
                            return
                        V(lambda e: e.memset(H[:], 0.0), [], ["H"])
                        for rnd in range(4):
                            last = rnd == 3
                            A(lambda e: e.activation(out=Hb[:], in_=H[:], func=AF.Copy), ["H"], ["Hb"])
                            for c in range(NC):
                                for hh, ps_ in enumerate(HS):
                                    if last:
                                        MT(lambda e, ps_=ps_, hh=hh, c=c: e.matmul(bankC[hh][ps_, 320:384], lhsT=CoefTb[ps_, c, :], rhs=Hb[ps_], start=True, stop=True),
                                           [f"Cf{c}", "Hb"], ["bC{hh}"])
                                    MT(lambda e, ps_=ps_, hh=hh, c=c: e.matmul(bankC[hh][ps_, 384:448], lhsT=GTb[ps_, c, :], rhs=Hb[ps_], start=True, stop=True),
                                       [f"GT{c}", "Hb"], ["bC{hh}"])
                                if last:
                                    V2(lambda e, hh, ps_, c=c: e.tensor_tensor(out=Yc[ps_, c, :], in0=bankC[hh][ps_, 320:384], in1=Yc[ps_, c, :], op=ALU.add), ["bC{hh}", f"Yc{c}"], [f"Yc{c}"])
                                V2(lambda e, hh, ps_, c=c: e.tensor_tensor(out=Hb[ps_], in0=bankC[hh][ps_, 384:448], in1=Jst[ps_, c, :], op=ALU.add), ["bC{hh}", f"J{c}"], ["Hb"])
                                if c == NC - 1:
                                    V2(lambda e, hh, ps_, c=c: e.tensor_tensor(out=H[ps_], in0=bankC[hh][ps_, 384:448], in1=Jst[ps_, c, :], op=ALU.add), ["bC{hh}", f"J{c}"], ["H"])
                            if not last:
                                res, rkey = exchange(sbC, "H", H[:], 64)
                                V(lambda e, res=res: e.tensor_copy(out=H[:], in_=res[:]), [rkey], ["H"])
                        ykeys = [f"Yc{c}" for c in range(NC)]
                        jkeys = [f"J{c}" for c in range(NC)]
                        st1 = sbC("st1", [128, NC], F32)
                        st2 = sbC("st2", [128, NC], F32)
                        ynb = sbC("ynb", [128, NC, 64], BF16)
                        yT = sbC("yT", [128, T], F32)
                        y = sbC("y", [128, 1, T], BF16)
                        V(lambda e: e.tensor_reduce(out=st1[:], in_=Yc[:], axis=AX.X, op=ALU.add), ykeys, ["st1"])
                        V(lambda e: e.tensor_scalar(out=st1[:], in0=st1[:], scalar1=1.0 / 64.0, scalar2=None, op0=ALU.mult), ["st1"], ["st1"])
                        V(lambda e: e.tensor_tensor(out=Yc[:], in0=Yc[:], in1=st1[:].unsqueeze(2).to_broadcast([128, NC, 64]), op=ALU.subtract), ykeys + ["st1"], ykeys)
                        V(lambda e: e.tensor_tensor(out=Jst[:], in0=Yc[:], in1=Yc[:], op=ALU.mult), ykeys + jkeys, jkeys)
                        V(lambda e: e.tensor_reduce(out=st2[:], in_=Jst[:], axis=AX.X, op=ALU.add), jkeys, ["st2"])
                        A(lambda e: e.activation(out=st2[:], in_=st2[:], func=AF.Sqrt, scale=1.0 / 64.0, bias=64e-5), ["st2"], ["st2"])
                        V(lambda e: e.reciprocal(out=st2[:], in_=st2[:]), ["st2"], ["st2"])
                        V(lambda e: e.tensor_tensor(out=ynb[:], in0=Yc[:], in1=st2[:].unsqueeze(2).to_broadcast([128, NC, 64]), op=ALU.mult), ykeys + ["st2"], ["ynb"])
                        gnw = prm(f"rw_gn_w_{l}")[:, ct:ct + 1]
                        gnb = prm(f"rw_gn_b_{l}")[:, ct:ct + 1]
                        for c0 in range(0, NC, 4):
                            for c in range(c0, c0 + 4):
                                for hh, ps_ in enumerate(HS):
                                    MT(lambda e, ps_=ps_, hh=hh, c=c, c0=c0: e.transpose(out=ptx[hh][ps_, (c - c0) * 64:(c - c0 + 1) * 64], in_=ynb[ps_, c, :], identity=identb2[ps_, :]),
                                       ["ynb", "identb2"], ["ptx{hh}"])
                            A2(lambda e, hh, ps_, c0=c0: e.activation(out=yT[ps_, c0 * 64:(c0 + 4) * 64], in_=ptx[hh][ps_, 0:256], func=AF.Identity, scale=gnw[ps_], bias=gnb[ps_]),
                               ["ptx{hh}", "pp"], ["yT"])
                        V(lambda e: e.tensor_tensor(out=yT[:], in0=yT[:], in1=bon[:], op=ALU.add), ["yT", "bon"], ["yT"])
                        V(lambda e: e.tensor_tensor(out=y[:, 0, :], in0=yT[:], in1=gfm[:], op=ALU.mult), ["yT", "gfm"], [f"yrw{ct}"])
                        out_proj(sbC, 20 + ct, y, [f"yrw{ct}"], cc0=4 + ct, ncc=1)
                        P.barrier()

                if "rw" in mixers:
                    rw()
                P.barrier()

        for l in range(L):
            with ExitStack() as ex:
                x = ex.enter_context(nc.sbuf_tensor(f"x_{l}", [128, KC, T], F32))
                load_x(xT_d if l == 0 else xd, l == 0)
                if l > 0:
                    ffn(l - 1, 1, f"n2_{l - 1}")
                ffn(l, 0, f"n1_{l}")
                sq = ex.enter_context(nc.sbuf_tensor(f"nsq_{l}", [128, 2, 512], BF16))
                rs = ex.enter_context(nc.sbuf_tensor(f"nrs_{l}", [128, 2, 512], F32))
                ssp = ex.enter_context(nc.psum_tensor(f"nssp_{l}", [128, 512], F32))
                rmsnorm(ex, (sq, rs, ssp), f"nm_{l}", list(range(NT)),
                        lambda kc, tt: xn2[:, kc, HW + tt * 512:HW + (tt + 1) * 512],
                        lambda kc, tt: f"xn2_{kc}_{tt}")
                store_x()
            mixer_phase(l)
            P.barrier()
        xfin_es = es.enter_context(ExitStack())
        x = xfin_es.enter_context(nc.sbuf_tensor("x_fin", [128, KC, T], F32))
        load_x(xd, False)
        ffn(L - 1, 1, f"n2_{L - 1}")

        with ExitStack() as es2:
            ob = es2.enter_context(nc.sbuf_tensor("o_ob", [128, 2, KC, 512], F32))
            sq = es2.enter_context(nc.sbuf_tensor("o_sq", [128, 2, 512], BF16))
            rs = es2.enter_context(nc.sbuf_tensor("o_rs", [128, 2, 512], F32))
            ssp = es2.enter_context(nc.psum_tensor("o_ssp", [128, 512], F32))
            for tt in range(NT):
                b = tt % 2
                rmsnorm(es2, (sq, rs, ssp), "nf", [tt], lambda kc, t2: ob[:, b, kc, :], lambda kc, t2: f"ob{b}_{kc}")
                P.op("sp", lambda e, b=b, tt=tt: e.dma_start(out=out_d[:, :, tt * 512:(tt + 1) * 512], in_=ob[:, b, :, :]),
                     reads=[f"ob{b}_{kc}" for kc in range(KC)], writes=[f"out{tt}"], lane=f"out{b}")
            P.final_wait("sp")
        print("instr counts", P.cnt, "waits", P.nwaits)
    return nc


def prep_weights(inp, L):
    wgu = np.empty((L, 2, NJ, 128, 2, KC, 128), np.float32)
    wd = np.empty((L, 2, KC, 128, NJ, 128), np.float32)
    named = {"ffn1_w_gate": inp["ffn1_w_gate"], "ffn1_w_up": inp["ffn1_w_up"], "ffn1_w_down": inp["ffn1_w_down"],
             "ffn2_w_gate": inp["ffn2_w_gate"], "ffn2_w_up": inp["ffn2_w_up"], "ffn2_w_down": inp["ffn2_w_down"]}
    for f, pre in enumerate(("ffn1", "ffn2")):
        for gi, nm in enumerate(("w_gate", "w_up")):
            w = np.asarray(named[f"{pre}_{nm}"], np.float32)[:L]
            w = w.reshape(L, KC, 128, NJ, 128)
            wgu[:, f, :, :, gi] = w.transpose(0, 3, 2, 1, 4)
        w = np.asarray(named[f"{pre}_w_down"], np.float32)[:L]
        w = w.reshape(L, NJ, 128, KC, 128)
        wd[:, f] = w.transpose(0, 3, 2, 1, 4)
    win = np.ascontiguousarray(np.asarray(inp["w_in"], np.float32)[:L].reshape(L, KC, 128, NIN).transpose(0, 2, 1, 3))
    wout = np.ascontiguousarray(np.asarray(inp["w_out"], np.float32)[:L].reshape(L, KC, 128, D).transpose(0, 2, 1, 3))
    return {"wgu": wgu.reshape(L * 2 * NJ, 128, 2 * KC * 128), "wd": wd.reshape(L * 2 * KC, 128, NJ * 128),
            "win": win, "wout": wout}


def run(inp, T, L, mixers=()):
    x = np.asarray(inp["x"], np.float32)
    B, S, _ = x.shape
    nseg = S // T
    ncores = B * nseg
    assert ncores == 8
    wts = prep_weights(inp, L)
    in_maps = []
    offs = None
    for c in range(ncores):
        b, s = divmod(c, nseg)
        xs = x[b, s * T:(s + 1) * T, :]
        xT = np.ascontiguousarray(xs.T.reshape(KC, 128, T).transpose(1, 0, 2))
        ppa, ppla, offs = pack_small(inp, L, s)
        m = {"xT": xT, "pp": ppa, "ppl": ppla}
        m.update(wts)
        in_maps.append(m)
    nc = build(T, L, offs, in_maps[0]["pp"].shape[1], in_maps[0]["ppl"].shape[2], mixers)
    print("pp cols", in_maps[0]["pp"].shape, in_maps[0]["ppl"].shape)
    res = run_bass_kernel_spmd(nc, in_maps, core_ids=list(range(ncores)), trace=bool(os.environ.get("KTRACE")))
    if os.environ.get("KTRACE"):
        print("EXEC_TIME_NS", res.exec_time_ns)
    out = np.empty((B, S, D), np.float32)
    for c in range(ncores):
        b, s = divmod(c, nseg)
        oT = np.asarray(res.results[c]["outT"], np.float32)
        out[b, s * T:(s + 1) * T, :] = oT.transpose(2, 1, 0).reshape(T, D)
    return out


def kernel(**inputs):
    return run(inputs, 2048, L_FULL, mixers=("gla", "lru", "rw", "ssd"))
```
